# Optimizing a Trainium2 kernel written in Bass

```python
import math
import jax, jax.numpy as jnp
from jax import lax
import numpy as np

D_MODEL = 1024
BATCH = 2
SEQ = 8192
DEPTH = 1
DEC_BATCH = 16
DEC_SEQ = 32
PAST_LEN = 2048

CHUNK = 64
N_META = 16
ROPE_THETA = 10000.0
EPS = 1e-6
D_FF = 2816
A_HEADS = 8
A_DIM = 64
B_HEADS = 8
B_DIM = 128
IDX_HEADS = 8
IDX_DIM = 64
TOPK_MAX = 256
QBLOCK = 128
A_WIDTH = A_HEADS * 2 * A_DIM
B_WIDTH = B_HEADS * B_DIM
SPLIT_SIZES = (A_WIDTH, A_WIDTH, A_WIDTH, B_WIDTH, B_WIDTH, B_WIDTH,
               IDX_HEADS * IDX_DIM, IDX_DIM, IDX_HEADS, 2 * D_MODEL)
SPLIT_POINTS = tuple(int(s) for s in np.cumsum(SPLIT_SIZES)[:-1])
N_IN_COLS = int(sum(SPLIT_SIZES))

kernel_name = "gated_diffattn_dsa_macaron_stream"


def rms_norm(x, g):
    xf = x.astype(jnp.float32)
    y = xf * lax.rsqrt(jnp.mean(xf * xf, axis=-1, keepdims=True) + EPS)
    return (y * g.astype(jnp.float32)).astype(x.dtype)


def swiglu(x, w_gate, w_up, w_down):
    return (jax.nn.silu(x @ w_gate) * (x @ w_up)) @ w_down


def rope(x, pos):
    half = x.shape[-1] // 2
    inv_freq = ROPE_THETA ** (-jnp.arange(half, dtype=jnp.float32) / half)
    ang = pos.astype(jnp.float32)[:, None] * inv_freq[None, :]
    cos = jnp.cos(ang)[None, :, None, :]
    sin = jnp.sin(ang)[None, :, None, :]
    xf = x.astype(jnp.float32)
    x1, x2 = xf[..., :half], xf[..., half:]
    return jnp.concatenate([x1 * cos - x2 * sin, x2 * cos + x1 * sin], axis=-1).astype(x.dtype)


def mixer_inputs(h, w_in, pos):
    b, n = h.shape[:2]
    aq, ak, av, bq, bk, bv, iq, ik, iw, gates = jnp.split(h @ w_in, SPLIT_POINTS, axis=-1)
    aq = rope(aq.reshape(b, n, 2 * A_HEADS, A_DIM), pos).reshape(b, n, A_HEADS, 2 * A_DIM)
    ak = rope(ak.reshape(b, n, 2 * A_HEADS, A_DIM), pos).reshape(b, n, A_HEADS, 2 * A_DIM)
    av = av.reshape(b, n, A_HEADS, 2 * A_DIM)
    bq = rope(bq.reshape(b, n, B_HEADS, B_DIM), pos)
    bk = rope(bk.reshape(b, n, B_HEADS, B_DIM), pos)
    bv = bv.reshape(b, n, B_HEADS, B_DIM)
    iq = rope(iq.reshape(b, n, IDX_HEADS, IDX_DIM), pos)
    ik = rope(ik.reshape(b, n, 1, IDX_DIM), pos).reshape(b, n, IDX_DIM)
    return aq, ak, av, bq, bk, bv, iq, ik, iw, gates


def map_query_blocks(fn, q_args, q_chunk):
    n = q_chunk.shape[0]
    qb = min(QBLOCK, n)
    n_blk = -(-n // qb)
    pad = n_blk * qb - n

    def to_blocks(a):
        a = jnp.pad(a, [(0, 0), (0, pad)] + [(0, 0)] * (a.ndim - 2))
        return jnp.moveaxis(a.reshape((a.shape[0], n_blk, qb) + a.shape[2:]), 1, 0)

    qc = jnp.pad(q_chunk, (0, pad), constant_values=-1).reshape(n_blk, qb)
    out = lax.map(lambda blk: fn(*blk[0], blk[1]), (tuple(to_blocks(a) for a in q_args), qc))
    out = jnp.moveaxis(out, 0, 1)
    return out.reshape((out.shape[0], n_blk * qb) + out.shape[3:])[:, :n]


def diff_attn_block(q, qc, k, v, kc, lam):
    b, nq = q.shape[:2]
    nk = k.shape[1]
    q = q.reshape(b, nq, A_HEADS, 2, A_DIM)
    k = k.reshape(b, nk, A_HEADS, 2, A_DIM)
    s = jnp.einsum('bqhmd,bkhmd->bmhqk', q, k).astype(jnp.float32) * (A_DIM ** -0.5)
    visible = kc[None, :] <= qc[:, None]
    p = jax.nn.softmax(jnp.where(visible, s, -jnp.inf), axis=-1)
    w = p[:, 0] - lam * p[:, 1]
    return jnp.einsum('bhqk,bkhe->bqhe', w.astype(v.dtype), v)


def dsa_block(q, iq, iw, qc, k, v, ik, kc, n_top):
    rel = jax.nn.relu(jnp.einsum('bqhd,bkd->bqhk', iq, ik).astype(jnp.float32) * (IDX_DIM ** -0.5))
    score = jnp.einsum('bqhk,bqh->bqk', rel, iw.astype(jnp.float32)) * (IDX_HEADS ** -0.5)
    visible = kc[None, :] <= qc[:, None]
    score = jnp.where(visible[None], score, -jnp.inf)
    top_val, top_idx = lax.top_k(score, n_top)
    valid = top_val > -jnp.inf
    k_sel = jax.vmap(lambda kb, ib: kb[ib])(k, top_idx)
    v_sel = jax.vmap(lambda vb, ib: vb[ib])(v, top_idx)
    s = jnp.einsum('bqhd,bqthd->bhqt', q, k_sel).astype(jnp.float32) * (B_DIM ** -0.5)
    s = jnp.where(valid[:, None], s, -jnp.inf)
    p = jax.nn.softmax(s, axis=-1)
    return jnp.einsum('bhqt,bqthd->bqhd', p.astype(v.dtype), v_sel)


def trunk_layer(x, pos, q_chunk, k_chunk, past, n_top, lam, lam_init,
                g_ffn1, w1_gate, w1_up, w1_down, g_mix, w_in, a_subln, w_a, w_b, w_o,
                g_ffn2, w2_gate, w2_up, w2_down):
    b, n = x.shape[:2]
    x = x + 0.5 * swiglu(rms_norm(x, g_ffn1), w1_gate, w1_up, w1_down)
    h = rms_norm(x, g_mix)
    aq, ak, av, bq, bk, bv, iq, ik, iw, gates = mixer_inputs(h, w_in, pos)
    new_rows = (ak, av, bk, bv, ik)
    if past is None:
        keys = new_rows
    else:
        keys = tuple(jnp.concatenate([c, r.astype(c.dtype)], axis=1) for c, r in zip(past, new_rows))
    ak_all, av_all, bk_all, bv_all, ik_all = keys
    oa = map_query_blocks(lambda q, qc: diff_attn_block(q, qc, ak_all, av_all, k_chunk, lam),
                          (aq,), q_chunk)
    oa = rms_norm(oa, a_subln) * (1.0 - lam_init)
    ob = map_query_blocks(lambda q, qi, qw, qc: dsa_block(q, qi, qw, qc, bk_all, bv_all, ik_all, k_chunk, n_top),
                          (bq, iq, iw), q_chunk)
    g_a, g_b = jnp.split(gates, 2, axis=-1)
    merged = (jax.nn.sigmoid(g_a) * (oa.reshape(b, n, A_WIDTH) @ w_a)
              + jax.nn.sigmoid(g_b) * (ob.reshape(b, n, B_WIDTH) @ w_b))
    x = x + merged @ w_o
    x = x + 0.5 * swiglu(rms_norm(x, g_ffn2), w2_gate, w2_up, w2_down)
    return x, new_rows


def setup_inputs(seed: int = 0) -> dict:
    key = jax.random.key(seed)
    ks = jax.random.split(key, 32)

    def nrm(k, shape, scale):
        return jax.random.normal(k, shape, jnp.float32) * scale

    n_cache = N_META + PAST_LEN
    return {
        'x_prompt': nrm(ks[0], (BATCH, SEQ, D_MODEL), 1.0),
        'x_sample': nrm(ks[1], (DEC_BATCH, DEC_SEQ, D_MODEL), 1.0),
        'cache_a_k': nrm(ks[2], (DEPTH, DEC_BATCH, n_cache, A_HEADS, 2 * A_DIM), 1.0),
        'cache_a_v': nrm(ks[3], (DEPTH, DEC_BATCH, n_cache, A_HEADS, 2 * A_DIM), 1.0),
        'cache_b_k': nrm(ks[4], (DEPTH, DEC_BATCH, n_cache, B_HEADS, B_DIM), 1.0),
        'cache_b_v': nrm(ks[5], (DEPTH, DEC_BATCH, n_cache, B_HEADS, B_DIM), 1.0),
        'cache_b_kidx': nrm(ks[6], (DEPTH, DEC_BATCH, n_cache, IDX_DIM), 1.0),
        'meta': nrm(ks[7], (N_META, D_MODEL), 1.0),
        'g_ffn1': 1.0 + nrm(ks[8], (DEPTH, D_MODEL), 0.01),
        'w1_gate': nrm(ks[9], (DEPTH, D_MODEL, D_FF), D_MODEL ** -0.5),
        'w1_up': nrm(ks[10], (DEPTH, D_MODEL, D_FF), D_MODEL ** -0.5),
        'w1_down': nrm(ks[11], (DEPTH, D_FF, D_MODEL), D_FF ** -0.5),
        'g_mix': 1.0 + nrm(ks[12], (DEPTH, D_MODEL), 0.01),
        'w_in': nrm(ks[13], (DEPTH, D_MODEL, N_IN_COLS), D_MODEL ** -0.5),
        'lam_q1': nrm(ks[14], (DEPTH, A_DIM), 0.1),
        'lam_k1': nrm(ks[15], (DEPTH, A_DIM), 0.1),
        'lam_q2': nrm(ks[16], (DEPTH, A_DIM), 0.1),
        'lam_k2': nrm(ks[17], (DEPTH, A_DIM), 0.1),
        'a_subln': 1.0 + nrm(ks[18], (DEPTH, 2 * A_DIM), 0.01),
        'w_a': nrm(ks[19], (DEPTH, A_WIDTH, D_MODEL), A_WIDTH ** -0.5),
        'w_b': nrm(ks[20], (DEPTH, B_WIDTH, D_MODEL), B_WIDTH ** -0.5),
        'w_o': nrm(ks[21], (DEPTH, D_MODEL, D_MODEL), D_MODEL ** -0.5),
        'g_ffn2': 1.0 + nrm(ks[22], (DEPTH, D_MODEL), 0.01),
        'w2_gate': nrm(ks[23], (DEPTH, D_MODEL, D_FF), D_MODEL ** -0.5),
        'w2_up': nrm(ks[24], (DEPTH, D_MODEL, D_FF), D_MODEL ** -0.5),
        'w2_down': nrm(ks[25], (DEPTH, D_FF, D_MODEL), D_FF ** -0.5),
        'g_final': 1.0 + nrm(ks[26], (D_MODEL,), 0.01),
    }


def reference(x_prompt, x_sample, cache_a_k, cache_a_v, cache_b_k, cache_b_v, cache_b_kidx,
              meta, g_ffn1, w1_gate, w1_up, w1_down, g_mix, w_in, lam_q1, lam_k1, lam_q2, lam_k2,
              a_subln, w_a, w_b, w_o, g_ffn2, w2_gate, w2_up, w2_down, g_final):
    n_top_p = min(TOPK_MAX, SEQ // 4)
    n_top_s = min(TOPK_MAX, (PAST_LEN + DEC_SEQ) // 4)
    meta_chunk = jnp.full((N_META,), -1, jnp.int32)
    pos_p = jnp.arange(N_META + SEQ, dtype=jnp.int32)
    chunk_p = jnp.concatenate([meta_chunk, jnp.arange(SEQ, dtype=jnp.int32) // CHUNK])
    new_frames = PAST_LEN + jnp.arange(DEC_SEQ, dtype=jnp.int32)
    pos_s = N_META + new_frames
    chunk_q_s = new_frames // CHUNK
    chunk_k_s = jnp.concatenate([meta_chunk, jnp.arange(PAST_LEN, dtype=jnp.int32) // CHUNK, chunk_q_s])

    xp = jnp.concatenate([jnp.broadcast_to(meta[None].astype(x_prompt.dtype),
                                           (x_prompt.shape[0], N_META, D_MODEL)), x_prompt], axis=1)
    xs = x_sample
    new_p = [[], [], [], [], []]
    new_s = [[], [], [], [], []]
    for l in range(DEPTH):
        lam_init = 0.8 - 0.6 * math.exp(-0.3 * l)
        lam = (jnp.exp(jnp.sum(lam_q1[l].astype(jnp.float32) * lam_k1[l].astype(jnp.float32)))
               - jnp.exp(jnp.sum(lam_q2[l].astype(jnp.float32) * lam_k2[l].astype(jnp.float32)))
               + lam_init)
        weights = (g_ffn1[l], w1_gate[l], w1_up[l], w1_down[l], g_mix[l], w_in[l], a_subln[l],
                   w_a[l], w_b[l], w_o[l], g_ffn2[l], w2_gate[l], w2_up[l], w2_down[l])
        xp, rows_p = trunk_layer(xp, pos_p, chunk_p, chunk_p, None, n_top_p, lam, lam_init, *weights)
        past = (cache_a_k[l], cache_a_v[l], cache_b_k[l], cache_b_v[l], cache_b_kidx[l])
        xs, rows_s = trunk_layer(xs, pos_s, chunk_q_s, chunk_k_s, past, n_top_s, lam, lam_init, *weights)
        for i in range(5):
            new_p[i].append(rows_p[i])
            new_s[i].append(rows_s[i])
    y_prompt = rms_norm(xp, g_final)[:, N_META:]
    y_sample = rms_norm(xs, g_final)
    a_k_p = jnp.stack(new_p[0])
    a_v_p = jnp.stack(new_p[1])
    b_k_p = jnp.stack(new_p[2])
    b_v_p = jnp.stack(new_p[3])
    b_kidx_p = jnp.stack(new_p[4])
    a_k_s = jnp.stack(new_s[0])
    a_v_s = jnp.stack(new_s[1])
    b_k_s = jnp.stack(new_s[2])
    b_v_s = jnp.stack(new_s[3])
    b_kidx_s = jnp.stack(new_s[4])
    return (y_prompt, y_sample, a_k_p, a_v_p, b_k_p, b_v_p, b_kidx_p, a_k_s, a_v_s, b_k_s, b_v_s, b_kidx_s)
```

```python
import os
import numpy as np
import ml_dtypes
import concourse.bass as bass
import concourse.mybir as mybir
from concourse.bass_utils import run_bass_kernel_spmd
from contextlib import ExitStack

F32 = mybir.dt.float32
BF16 = mybir.dt.bfloat16
AF = mybir.ActivationFunctionType
ALU = mybir.AluOpType
AX = mybir.AxisListType

D = 1024
DFF = 2816
NFC = DFF // 128
NT = 17
TT = NT * 128
EPS = 1e-6
NCOL = 8776
KW = 4160
RW = 4256
C_AKT, C_BKT, C_IKT, C_AV, C_BV = 0, 1024, 2048, 2176, 3216
WA_, WB_ = 2176, 2080
QW = 3584
GW = 2056
NEG = -30000.0
NBIS = 16


class Tok:
    __slots__ = ("w", "r")

    def __init__(self):
        self.w = {}
        self.r = {}


class FW:
    NDS = 32

    def __init__(self, nc, es):
        self.nc = nc
        self.names = ["pe", "act", "dve", "pool", "sp"]
        self.sem = {n: es.enter_context(nc.semaphore("s_" + n)) for n in self.names}
        self.dsem = [es.enter_context(nc.semaphore("d%d" % i)) for i in range(self.NDS)]
        self.cnt = {n: 0 for n in self.names}
        self.prog = {n: [] for n in self.names}
        self.waited = {n: {} for n in self.names}
        self.ndma = 0

    def _need(self, eng, ev):
        if ev is None:
            return
        if ev[0] == "e":
            if ev[1] == eng and eng in ("pe", "sp"):
                return
            key, val = ("e", ev[1]), ev[2]
        else:
            n = ev[1]
            key, val = ("d", n % self.NDS), 16 * (n // self.NDS + 1)
        w = self.waited[eng]
        if w.get(key, 0) >= val:
            return
        w[key] = val
        self.prog[eng].append(("w", key, val))

    def op(self, eng, fn, reads=(), writes=(), dma=False, appends=()):
        for t in reads:
            for ev in t.w.values():
                self._need(eng, ev)
        for t in writes:
            for ev in t.w.values():
                self._need(eng, ev)
            for ev in t.r.values():
                self._need(eng, ev)
        for t in appends:
            for ev in t.r.values():
                self._need(eng, ev)
        if dma:
            n = self.ndma
            self.ndma += 1
            if n >= self.NDS:
                self._need(eng, ("d", n - self.NDS))
            ev = ("d", n)
            self.prog[eng].append(("dma", fn, n))
            rkey = ("d", n % self.NDS)
        else:
            self.cnt[eng] += 1
            ev = ("e", eng, self.cnt[eng])
            self.prog[eng].append(("op", fn))
            rkey = eng
        for t in reads:
            t.r[rkey] = ev
        for t in writes:
            t.w = {rkey: ev}
            t.r = {}
        for t in appends:
            t.w[rkey] = ev
        return ev

    def barrier(self, nowait_on=()):
        for e in self.names:
            for f in self.names:
                if f in nowait_on:
                    continue
                if f != e and self.cnt[f] > 0:
                    self._need(e, ("e", f, self.cnt[f]))
            for n in range(max(0, self.ndma - self.NDS), self.ndma):
                self._need(e, ("d", n))

    def replay(self, name, e):
        for it in self.prog[name]:
            if it[0] == "w":
                key, val = it[1], it[2]
                s = self.sem[key[1]] if key[0] == "e" else self.dsem[key[1]]
                e.wait_ge(s, val)
            elif it[0] == "op":
                it[1](e).then_inc(self.sem[name], 1)
            else:
                n = it[2]
                it[1](e).then_inc(self.dsem[n % self.NDS], 16)

    def run(self):
        with self.nc.Block() as block:
            @block.tensor
            def _(e):
                self.replay("pe", e)

            @block.scalar
            def _(e):
                self.replay("act", e)

            @block.vector
            def _(e):
                self.replay("dve", e)

            @block.gpsimd
            def _(e):
                self.replay("pool", e)

            @block.sync
            def _(e):
                self.replay("sp", e)


class Buf:
    __slots__ = ("ap", "tok")

    def __init__(self, ap):
        self.ap = ap
        self.tok = Tok()


class Arena:
    def __init__(self, ap):
        self.ap = ap
        self.off = 0
        self.n = ap.shape[1]

    def f32(self, n):
        a = self.ap[:, self.off:self.off + n]
        self.off += n
        assert self.off <= self.n, ("arena overflow", self.off, self.n)
        return a

    def bf16(self, n):
        assert n % 2 == 0
        return self.f32(n // 2).bitcast(BF16)


def build_program(stop_after=99, debug=False):
    nc = bass.Bass("TRN2", target_bir_lowering=False)

    def din(name, shape, dt=F32):
        return nc.dram_tensor(name, list(shape), dt, kind="ExternalInput").ap()

    def dout(name, shape, dt=F32):
        return nc.dram_tensor(name, list(shape), dt, kind="ExternalOutput").ap()

    def dint(name, shape, dt):
        return nc.dram_tensor(name, list(shape), dt).ap()

    xin = din("xin", [NT, 128, D])
    rope = din("rope", [NT, 128, 192])
    gvec = din("gvec", [4, D])
    subln = din("subln", [128])
    lamv = din("lamv", [4, 64])
    w1g = din("w1g", [D, DFF]); w1u = din("w1u", [D, DFF]); w1d = din("w1d", [DFF, D])
    w2g = din("w2g", [D, DFF]); w2u = din("w2u", [D, DFF]); w2d = din("w2d", [DFF, D])
    win = din("win", [D, NCOL])
    wa = din("wa", [D, D]); wb = din("wb", [D, D]); wo = din("wo", [D, D])
    cak = din("cak", [2, 2064, D]); cav = din("cav", [2, 2064, D])
    cbk = din("cbk", [2, 2064, D]); cbv = din("cbv", [2, 2064, D])
    cki = din("cki", [2, 2064, 64])
    cmb = din("cmb", [4, 128, 128])
    identd = din("ident", [128, 512])

    y = dout("y", [NT, 128, D])
    kout = dout("kout", [NT, 128, KW])
    dbg = dout("dbg", [NT, 128, D]) if debug else None
    dbgq = dout("dbgq", [NT, 128, QW]) if debug else None
    dbgg = dout("dbgg", [NT, 128, GW]) if debug else None

    x1s = dint("x1s", [NT, 128, D], F32)
    x2s = dint("x2s", [NT, 128, D], F32)
    xn3s = dint("xn3s", [NT, 128, D], BF16)
    qs = dint("qs", [NT, 128, QW], BF16)
    gs = dint("gs", [NT, 128, GW], F32)
    shardA = dint("shardA", [16 * 128, WA_], BF16)
    gathA = dint("gathA", [16 * 4 * 128, WA_], BF16)
    shardB = dint("shardB", [16 * 128, WB_], BF16)
    gathB = dint("gathB", [16 * 4 * 128, WB_], BF16)
    mkv = dint("mkv", [128, RW], BF16)
    skv = dint("skv", [2, 18, 128, RW], BF16)

    with ExitStack() as es:
        fw = FW(nc, es)
        arena_t = es.enter_context(nc.sbuf_tensor("arena", [128, 51200], F32))
        small_t = es.enter_context(nc.sbuf_tensor("small", [128, 1536], F32))
        ps = [es.enter_context(nc.psum_tensor("ps%d" % i, [128, 512], F32)) for i in range(8)]
        pst = [Tok() for _ in range(8)]
        A = Arena(arena_t[:])
        S = Arena(small_t[:])

        def mm(out, lhsT, rhs, start, stop, r, w):
            fw.op("pe", lambda e: e.matmul(out, lhsT=lhsT, rhs=rhs, start=start, stop=stop), r, w)

        def tr(out, in_, ident_ap, r, w):
            fw.op("pe", lambda e: e.transpose(out=out, in_=in_, identity=ident_ap), r, w)

        def act(out, in_, func, r, w, **kw):
            fw.op("act", lambda e: e.activation(out=out, in_=in_, func=func, **kw), r, w)

        def dma(eng, out, in_, r, w):
            dset = set(id(v) for v in dtoks.values()) if dtoks else set()
            wx = [t for t in w if id(t) not in dset]
            ax = [t for t in w if id(t) in dset]
            fw.op(eng, lambda e: e.dma_start(out=out, in_=in_), r, wx, dma=True, appends=ax)

        def tt(eng, out, in0, in1, op, r, w):
            fw.op(eng, lambda e: e.tensor_tensor(out=out, in0=in0, in1=in1, op=op), r, w)

        def ts(eng, out, in0, s1, s2, op0, op1, r, w, accum_out=None):
            if op1 is None:
                fw.op(eng, lambda e: e.tensor_scalar(out=out, in0=in0, scalar1=s1, scalar2=None, op0=op0), r, w)
            elif accum_out is None:
                fw.op(eng, lambda e: e.tensor_scalar(out=out, in0=in0, scalar1=s1, scalar2=s2, op0=op0, op1=op1), r, w)
            else:
                fw.op(eng, lambda e: e.tensor_scalar(out=out, in0=in0, scalar1=s1, scalar2=s2, op0=op0, op1=op1,
                                                     accum_out=accum_out), r, w)

        def stt(eng, out, in0, scalar, in1, op0, op1, r, w):
            fw.op(eng, lambda e: e.scalar_tensor_tensor(out=out, in0=in0, scalar=scalar, in1=in1, op0=op0, op1=op1), r, w)

        def cp(eng, out, in_, r, w):
            if eng == "act":
                fw.op("act", lambda e: e.copy(out=out, in_=in_), r, w)
            else:
                fw.op(eng, lambda e: e.tensor_copy(out=out, in_=in_), r, w)

        dtoks = {}
        gtoks = [Tok() for _ in range(16)]
        ident = Buf(S.bf16(512))
        dma("pool", ident.ap, identd, [], [ident.tok])
        idn = ident.ap[:, 0:128]
        gB = Buf(A.f32(D))
        stat = [Buf(S.f32(1)) for _ in range(8)]
        junk = Buf(A.f32(D))
        dtoks.update({k: Tok() for k in ["x1s", "x2s", "xn3s", "qs", "gs", "shard", "gath", "mkv", "skv", "y", "kout", "dbg"]})
        base_mark = A.off

        epsb = Buf(S.f32(1))
        fw.op("dve", lambda e: e.memset(epsb.ap, EPS), [], [epsb.tok])

        def rstd(rs, ss, n):
            p = rs.ap.shape[0]
            act(rs.ap, ss.ap, AF.Sqrt, [ss.tok, epsb.tok], [rs.tok], scale=1.0 / n, bias=epsb.ap[0:p, :])
            fw.op("dve", lambda e: e.reciprocal(out=rs.ap, in_=rs.ap), [rs.tok], [rs.tok])

        def rms_to_xT(xb, dst3, dst_tok, gbuf, bank, defer=None):
            ss, rs = stat[0], stat[1]
            act(junk.ap, xb.ap, AF.Square, [xb.tok], [junk.tok, ss.tok], accum_out=ss.ap)
            rstd(rs, ss, D)
            stt("dve", xn.ap, xb.ap, rs.ap, gbuf.ap, ALU.mult, ALU.mult, [xb.tok, rs.tok, gbuf.tok], [xn.tok])
            def pe_part():
                pb = ps[bank][:].bitcast(BF16)
                for kc in range(8):
                    tr(pb[:, kc * 128:(kc + 1) * 128], xn.ap[:, kc * 128:(kc + 1) * 128], idn,
                       [xn.tok, ident.tok], [pst[bank]])
                cp("act", dst3, pb.rearrange("p (k c) -> p k c", k=8), [pst[bank]], [dst_tok])
            if defer is None:
                pe_part()
            else:
                defer.append(pe_part)

        xn = Buf(A.bf16(D))
        attn_mark = A.off
        xnT = A.bf16(8 * TT)
        xnT3 = xnT.rearrange("p (k t) -> p k t", k=8)
        xnT_tok = [Tok() for _ in range(NT)]
        ffn_mark = A.off

        TG = [(0, 512), (512, 512), (1024, 512), (1536, 512), (2048, 128)]

        def ffn(wg, wu, wd, xsrc, xsrc_tok, g_next_idx, final):
            A.off = ffn_mark
            actT = A.bf16(NFC * TT)
            actT3 = actT.rearrange("p (c t) -> p c t", c=NFC)
            act_tok = [[Tok() for _ in range(len(TG))] for _ in range(NFC)]
            region = A.off
            wgb = [Buf(A.bf16(8 * 256)) for _ in range(2)]
            wub = [Buf(A.bf16(8 * 256)) for _ in range(2)]
            sil = [Buf(A.f32(512)) for _ in range(2)]
            NWA = 21
            wdA = Buf(A.bf16(NWA * D))
            wdA3 = wdA.ap.rearrange("p (c n) -> p c n", c=NWA)
            wd_src = wd.rearrange("(c p) n -> p c n", p=128)
            it = 0
            for fg in range(DFF // 256):
                wgt, wut = wgb[fg % 2], wub[fg % 2]
                dma("pool", wgt.ap.rearrange("p (k f) -> p k f", k=8),
                    wg[:, fg * 256:(fg + 1) * 256].rearrange("(k p) f -> p k f", p=128), [], [wgt.tok])
                dma("pool", wut.ap.rearrange("p (k f) -> p k f", k=8),
                    wu[:, fg * 256:(fg + 1) * 256].rearrange("(k p) f -> p k f", p=128), [], [wut.tok])
                wg3 = wgt.ap.rearrange("p (k f) -> p k f", k=8)
                wu3 = wut.ap.rearrange("p (k f) -> p k f", k=8)
                if fg == 1:
                    for c0_ in range(0, NWA, 7):
                        fw.op("pool", lambda e, o=wdA3[:, c0_:c0_ + 7, :], i_=wd_src[:, c0_:c0_ + 7, :]: e.dma_start(out=o, in_=i_),
                              [], [], dma=True, appends=[wdA.tok])
                for sub in range(2):
                    fc = fg * 2 + sub
                    for gi, (t0, tn) in enumerate(TG):
                        bg, bu = (it % 2) * 2, (it % 2) * 2 + 1
                        it += 1
                        rtoks = [xnT_tok[tt_] for tt_ in range(t0 // 128, (t0 + tn) // 128)]
                        for kc in range(8):
                            mm(ps[bg][:, 0:tn], wg3[:, kc, sub * 128:(sub + 1) * 128], xnT3[:, kc, t0:t0 + tn],
                               kc == 0, kc == 7, [wgt.tok] + rtoks, [pst[bg]])
                        for kc in range(8):
                            mm(ps[bu][:, 0:tn], wu3[:, kc, sub * 128:(sub + 1) * 128], xnT3[:, kc, t0:t0 + tn],
                               kc == 0, kc == 7, [wut.tok] + rtoks, [pst[bu]])
                        sb = sil[it % 2]
                        act(sb.ap[:, 0:tn], ps[bg][:, 0:tn], AF.Silu, [pst[bg]], [sb.tok])
                        tt("dve", actT3[:, fc, t0:t0 + tn], sb.ap[:, 0:tn], ps[bu][:, 0:tn], ALU.mult,
                           [sb.tok, pst[bu]], [act_tok[fc][gi]])
            fw.barrier()
            A.off = region
            wdB = Buf(A.bf16((NFC - NWA) * D))
            wdB3 = wdB.ap.rearrange("p (c n) -> p c n", c=NFC - NWA)
            dma("pool", wdB3, wd_src[:, NWA:NFC, :], [], [wdB.tok])

            def wdc(c, half):
                if c < NWA:
                    return wdA3[:, c, half * 512:(half + 1) * 512], wdA.tok
                return wdB3[:, c - NWA, half * 512:(half + 1) * 512], wdB.tok
            xb2 = [Buf(A.f32(D)) for _ in range(2)]
            xr = [Buf(A.f32(D)) for _ in range(2)]
            dma("sp", gB.ap, gvec[g_next_idx].partition_broadcast(128), [], [gB.tok])
            fpend = []
            for t in range(NT):
                xb = xb2[t % 2]
                dma("sp", xb.ap, xsrc[t], [xsrc_tok], [xb.tok])
                b0 = 4 + (t % 2) * 2
                for half in range(2):
                    for c in range(NFC):
                        wap, wtk = wdc(c, half)
                        mm(ps[b0 + half][:], actT3[:, c, t * 128:(t + 1) * 128], wap,
                           c == 0, c == NFC - 1, [wtk, act_tok[c][min(t // 4, 4)]], [pst[b0 + half]])
                while fpend:
                    fpend.pop(0)()
                x1 = xr[t % 2]
                for half in range(2):
                    stt("dve", x1.ap[:, half * 512:(half + 1) * 512], ps[b0 + half][:], 0.5,
                        xb.ap[:, half * 512:(half + 1) * 512], ALU.mult, ALU.add,
                        [pst[b0 + half], xb.tok], [x1.tok])
                if not final:
                    dma("sp", x1s[t], x1.ap, [x1.tok], [dtoks["x1s"]])
                    if debug and stop_after == 1:
                        dma("sp", dbg[t], x1.ap, [x1.tok], [dtoks["dbg"]])
                    rms_to_xT(x1, xnT3[:, :, t * 128:(t + 1) * 128], xnT_tok[t], gB, 0 + (t % 2), defer=fpend)
                else:
                    ss, rs = stat[0], stat[1]
                    act(junk.ap, x1.ap, AF.Square, [x1.tok], [junk.tok, ss.tok], accum_out=ss.ap)
                    rstd(rs, ss, D)
                    stt("dve", xb.ap, x1.ap, rs.ap, gB.ap, ALU.mult, ALU.mult, [x1.tok, rs.tok, gB.tok], [xb.tok])
                    dma("sp", y[t], xb.ap, [xb.tok], [dtoks["y"]])
            while fpend:
                fpend.pop(0)()
            fw.barrier()

        dma("sp", gB.ap, gvec[0].partition_broadcast(128), [], [gB.tok])
        A.off = ffn_mark
        xin_b = [Buf(A.f32(D)) for _ in range(2)]
        for t in range(NT):
            xb = xin_b[t % 2]
            dma("sp", xb.ap, xin[t], [], [xb.tok])
            rms_to_xT(xb, xnT3[:, :, t * 128:(t + 1) * 128], xnT_tok[t], gB, t % 2)
        fw.barrier()
        ffn(w1g, w1u, w1d, xin, Tok(), 1, False)
        if stop_after >= 2:
            def phase_win():
                A.off = ffn_mark
                ropet = Buf(A.f32(NT * 192))
                dma("sp", ropet.ap.rearrange("p (t c) -> p t c", t=NT), rope.rearrange("t p c -> p t c"), [], [ropet.tok])
                wbuf = [Buf(A.bf16(8 * 512)) for _ in range(2)]
                wstg = [Buf(A.f32(8 * 512)) for _ in range(2)]
                zf = [Buf(A.f32(512)) for _ in range(4)]
                rf = [Buf(A.f32(512)) for _ in range(4)]
                rb = [Buf(A.bf16(512)) for _ in range(4)]
                tmpd = [Buf(A.f32(256)) for _ in range(4)]
                tmpd2 = [Buf(A.f32(256)) for _ in range(4)]
                tmpp = [Buf(A.f32(256)) for _ in range(4)]
                tmpp2 = [Buf(A.f32(256)) for _ in range(4)]
                stg = [Buf(A.bf16(512)) for _ in range(4)]
                stgq = [Buf(A.bf16(1024)) for _ in range(4)]
                stgv = [Buf(A.bf16(520)) for _ in range(4)]
                gst = [Buf(A.f32(8)) for _ in range(4)]
                for b_ in stgq:
                    fw.op("dve", lambda e, a=b_.ap: e.memset(a, 0.0), [], [b_.tok])
                for b_ in stgv:
                    fw.op("dve", lambda e, a=b_.ap: e.memset(a, 0.0), [], [b_.tok])
                    v3 = b_.ap.rearrange("p (h c) -> p h c", h=4)
                    fw.op("dve", lambda e, a=v3[:, :, 128:129]: e.memset(a, 1.0), [], [b_.tok])
                groups = []
                for hh in range(2):
                    groups.append((1024 + hh * 512, 512, "ak", hh))
                for hh in range(2):
                    groups.append((2048 + hh * 512, 512, "av", hh))
                for hh in range(2):
                    groups.append((4096 + hh * 512, 512, "bk", hh))
                for hh in range(2):
                    groups.append((5120 + hh * 512, 512, "bv", hh))
                groups.append((6656, 72, "ikw", 0))
                NKG = len(groups)
                for hh in range(2):
                    groups.append((hh * 512, 512, "aq", hh))
                for hh in range(2):
                    groups.append((3072 + hh * 512, 512, "bq", hh))
                groups.append((6144, 512, "iq", 0))
                for j in range(4):
                    groups.append((6728 + j * 512, 512, "g", j))

                def gen_coll():
                    for m_ in range(16):
                        for sh_, ga_ in ((shardA, gathA), (shardB, gathB)):
                            fw.op("pool", lambda e, i_=sh_[m_ * 128:(m_ + 1) * 128, :], o_=ga_[m_ * 512:(m_ + 1) * 512, :]:
                                  e.collective_compute("AllGather", ALU.bypass, replica_groups=[[0, 1, 2, 3], [4, 5, 6, 7]],
                                                       ins=[i_], outs=[o_]),
                                  [dtoks["shard"]], [gtoks[m_]])
                            yield
                g_coll = gen_coll()
                coll_n = [0]
                cnt = [0]
                pend = []

                def rope_piece(src, dst, nv, half, cos, sin, k):
                    s4 = src.ap.rearrange("p (v two h) -> p v two h", two=2, h=half)
                    d4 = dst.ap[:, 0:nv * 2 * half].rearrange("p (v two h) -> p v two h", two=2, h=half)
                    cb = cos.unsqueeze(1).to_broadcast([128, nv, half])
                    sb_ = sin.unsqueeze(1).to_broadcast([128, nv, half])
                    x1, x2 = s4[:, :, 0, :], s4[:, :, 1, :]
                    ta, ta2, tb, tb2 = tmpd[k], tmpd2[k], tmpp[k], tmpp2[k]
                    dst.tok.w = {}
                    v3_ = lambda b_: b_.ap[:, 0:nv * half].rearrange("p (v h) -> p v h", v=nv)
                    tt("dve", v3_(ta), x1, cb, ALU.mult, [src.tok, ropet.tok], [ta.tok])
                    tt("dve", v3_(ta2), x2, sb_, ALU.mult, [src.tok, ropet.tok], [ta2.tok])
                    fw.op("dve", lambda e: e.tensor_tensor(out=d4[:, :, 0, :], in0=v3_(ta), in1=v3_(ta2), op=ALU.subtract),
                          [ta.tok, ta2.tok], [], appends=[dst.tok])
                    tt("dve", v3_(tb), x2, cb, ALU.mult, [src.tok, ropet.tok], [tb.tok])
                    tt("dve", v3_(tb2), x1, sb_, ALU.mult, [src.tok, ropet.tok], [tb2.tok])
                    fw.op("dve", lambda e: e.tensor_tensor(out=d4[:, :, 1, :], in0=v3_(tb), in1=v3_(tb2), op=ALU.add),
                          [tb.tok, tb2.tok], [], appends=[dst.tok])

                def kt_store(t, sbuf, ccol, ncols_per_head, nh):
                    s3 = sbuf.ap[:, 0:nh * 128].rearrange("p (h k) -> p h k", h=nh)
                    if t < 16:
                        dma("sp", shardA[t * 128:(t + 1) * 128, ccol:ccol + nh * 128], sbuf.ap[:, 0:nh * 128],
                            [sbuf.tok], [dtoks["shard"]])
                    else:
                        dma("sp", mkv[:, ccol:ccol + nh * 128].rearrange("p (h k) -> p h k", h=nh)[:, :, 0:16],
                            s3[:, :, 0:16], [sbuf.tok], [dtoks["mkv"]])
                        for s in range(2):
                            dma("sp", skv[s, 17, :, ccol:ccol + nh * 128].rearrange("p (h k) -> p h k", h=nh)[:, :, 0:32],
                                s3[:, :, 32 + 32 * s:64 + 32 * s], [sbuf.tok], [dtoks["skv"]])

                def v_store(t, sbuf, ccol):
                    if t < 16:
                        dma("sp", shardB[t * 128:(t + 1) * 128, ccol - WA_:ccol - WA_ + 520], sbuf.ap, [sbuf.tok], [dtoks["shard"]])
                    else:
                        dma("sp", mkv[0:16, ccol:ccol + 520], sbuf.ap[0:16, :], [sbuf.tok], [dtoks["mkv"]])
                        for s in range(2):
                            dma("sp", skv[s, 17, 0:32, ccol:ccol + 520], sbuf.ap[32 + 32 * s:64 + 32 * s, :],
                                [sbuf.tok], [dtoks["skv"]])

                for gi, (c0, wd_, kind, idx) in enumerate(groups):
                    wt = wbuf[gi % 2]
                    w3 = wt.ap.rearrange("p (k f) -> p k f", k=8)
                    if gi < NKG:
                        dma("pool", w3[:, :, 0:wd_], win[:, c0:c0 + wd_].rearrange("(k p) f -> p k f", p=128), [], [wt.tok])
                    else:
                        ws_ = wstg[gi % 2]
                        ws3 = ws_.ap.rearrange("p (k f) -> p k f", k=8)
                        dma("sp", ws3[:, :, 0:wd_], win[:, c0:c0 + wd_].rearrange("(k p) f -> p k f", p=128), [], [ws_.tok])
                        for kc2 in range(0, 8, 2):
                            cp("dve", w3[:, kc2:kc2 + 2, 0:wd_], ws3[:, kc2:kc2 + 2, 0:wd_], [ws_.tok], [wt.tok])
                    for t in range(NT):
                        k = cnt[0] % 4
                        pb, tb_ = cnt[0] % 2, 2 + cnt[0] % 2
                        cnt[0] += 1
                        for kc in range(8):
                            mm(ps[pb][:, 0:wd_], xnT3[:, kc, t * 128:(t + 1) * 128], w3[:, kc, 0:wd_], kc == 0, kc == 7,
                               [wt.tok, xnT_tok[t]], [pst[pb]])
                        while len(pend) > 1:
                            pend.pop(0)()
                        cA = ropet.ap[:, t * 192:t * 192 + 32]
                        sA = ropet.ap[:, t * 192 + 32:t * 192 + 64]
                        cB = ropet.ap[:, t * 192 + 64:t * 192 + 128]
                        sB = ropet.ap[:, t * 192 + 128:t * 192 + 192]
                        ptb = ps[tb_][:].bitcast(BF16)
                        if kind in ("aq", "ak", "bq", "bk", "iq"):
                            cp("act", zf[k].ap, ps[pb][:], [pst[pb]], [zf[k].tok])
                            isb = kind in ("bq", "bk")
                            nv, half = (4, 64) if isb else (8, 32)
                            cos, sin = (cB, sB) if isb else (cA, sA)
                            if kind in ("ak", "bk"):
                                rope_piece(zf[k], rf[k], nv, half, cos, sin, k)
                                kc0 = (0 if kind == "ak" else 2048) + idx * 512
                                dma("sp", kout[t][:, kc0:kc0 + 512], rf[k].ap, [rf[k].tok], [dtoks["kout"]])
                                cp("act", rb[k].ap, rf[k].ap, [rf[k].tok], [rb[k].tok])
                            else:
                                rope_piece(zf[k], rb[k], nv, half, cos, sin, k)
                            def fin(k=k, t=t, kind=kind, idx=idx, tb_=tb_, ptb=ptb):
                              for h in range(4):
                                tr(ptb[:, h * 128:(h + 1) * 128], rb[k].ap[:, h * 128:(h + 1) * 128], idn,
                                   [rb[k].tok, ident.tok], [pst[tb_]])
                              if kind == "aq":
                                q4 = stgq[k].ap.rearrange("p (h m c) -> p h m c", h=4, m=2)
                                p3 = ptb[:, 0:512].rearrange("p (h c) -> p h c", h=4)
                                cp("act", q4[0:64, :, 0, :], p3[0:64, :, :], [pst[tb_]], [stgq[k].tok])
                                cp("act", q4[64:128, :, 1, :], p3[64:128, :, :], [pst[tb_]], [stgq[k].tok])
                                dma("sp", qs[t][:, idx * 1024:(idx + 1) * 1024], stgq[k].ap, [stgq[k].tok], [dtoks["qs"]])
                              else:
                                cp("act", stg[k].ap, ptb[:, 0:512], [pst[tb_]], [stg[k].tok])
                                if kind == "bq":
                                    dma("sp", qs[t][:, 2048 + idx * 512:2048 + (idx + 1) * 512], stg[k].ap,
                                        [stg[k].tok], [dtoks["qs"]])
                                elif kind == "iq":
                                    dma("sp", qs[t][:, 3072:3584], stg[k].ap, [stg[k].tok], [dtoks["qs"]])
                                else:
                                    kt_store(t, stg[k], (C_AKT if kind == "ak" else C_BKT) + idx * 512, 128, 4)
                            pend.append(fin)
                        elif kind in ("av", "bv"):
                            cp("act", rf[k].ap, ps[pb][:], [pst[pb]], [rf[k].tok])
                            kc0 = (1024 if kind == "av" else 3072) + idx * 512
                            dma("sp", kout[t][:, kc0:kc0 + 512], rf[k].ap, [rf[k].tok], [dtoks["kout"]])
                            v3 = stgv[k].ap.rearrange("p (h c) -> p h c", h=4)
                            cp("dve", v3[:, :, 0:128], rf[k].ap.rearrange("p (h c) -> p h c", h=4), [rf[k].tok], [stgv[k].tok])
                            v_store(t, stgv[k], (C_AV if kind == "av" else C_BV) + idx * 520)
                        elif kind == "ikw":
                            cp("act", zf[k].ap[:, 0:72], ps[pb][:, 0:72], [pst[pb]], [zf[k].tok])
                            zsub = Buf(zf[k].ap[:, 0:64]); zsub.tok = zf[k].tok
                            rsub = Buf(rf[k].ap[:, 0:64]); rsub.tok = rf[k].tok
                            rope_piece(zsub, rsub, 1, 32, cA, sA, k)
                            dma("sp", kout[t][:, 4096:4160], rf[k].ap[:, 0:64], [rf[k].tok], [dtoks["kout"]])
                            cp("act", rb[k].ap[:, 0:64], rf[k].ap[:, 0:64], [rf[k].tok], [rb[k].tok])
                            cp("act", rb[k].ap[:, 64:128], rf[k].ap[:, 0:64], [rf[k].tok], [rb[k].tok])
                            def fin2(k=k, t=t, tb_=tb_, ptb=ptb):
                                tr(ptb[:, 0:128], rb[k].ap[:, 0:128], idn, [rb[k].tok, ident.tok], [pst[tb_]])
                                cp("act", stg[k].ap[:, 0:128], ptb[:, 0:128], [pst[tb_]], [stg[k].tok])
                                kt_store(t, stg[k], C_IKT, 128, 1)
                            pend.append(fin2)
                            cp("dve", gst[k].ap, zf[k].ap[:, 64:72], [zf[k].tok], [gst[k].tok])
                            dma("sp", gs[t][:, 2048:2056], gst[k].ap, [gst[k].tok], [dtoks["gs"]])
                        else:
                            act(rf[k].ap, ps[pb][:], AF.Sigmoid, [pst[pb]], [rf[k].tok])
                            dma("sp", gs[t][:, idx * 512:(idx + 1) * 512], rf[k].ap, [rf[k].tok], [dtoks["gs"]])
                        if gi >= NKG and t >= 3:
                            npieces = (gi - NKG) * NT + t - 3
                            while coll_n[0] < 32 and coll_n[0] * 3 <= npieces:
                                next(g_coll)
                                coll_n[0] += 1
                        yield
                while pend:
                    pend.pop(0)()
                while coll_n[0] < 32:
                    next(g_coll)
                    coll_n[0] += 1
                yield

            def gen_sample_prep():
                cst = [Buf(A.bf16(D)) for _ in range(3)]
                stK = [Buf(A.bf16(D)) for _ in range(2)]
                stV = [Buf(A.bf16(1040)) for _ in range(3)]
                for b_ in stV:
                    fw.op("dve", lambda e, a=b_.ap: e.memset(a, 0.0), [], [b_.tok])
                    fw.op("dve", lambda e, a=b_.ap.rearrange("p (h c) -> p h c", h=8)[:, :, 128:129]: e.memset(a, 1.0),
                          [], [b_.tok])
                n = 0
                nv_ = 0
                for s in range(2):
                    for blk in range(17):
                        r0 = blk * 128
                        kn = 128 if blk < 16 else 16
                        for src, ccol in ((cak, C_AKT), (cbk, C_BKT)):
                            k = n % 2; c_ = cst[n % 3]; n += 1
                            dma("pool", c_.ap[0:kn, :], src[s, r0:r0 + kn, :], [], [c_.tok])
                            bk_ = 4 + k
                            pb_ = ps[bk_][:].bitcast(BF16)
                            for h in range(8):
                                tr(pb_[:, h * 128:h * 128 + kn], c_.ap[0:kn, h * 128:(h + 1) * 128], idn[0:kn, 0:kn],
                                   [c_.tok, ident.tok], [pst[bk_]])
                            p3 = pb_.rearrange("p (h c) -> p h c", h=8)
                            s3 = stK[k].ap.rearrange("p (h c) -> p h c", h=8)
                            cp("act", s3[:, :, 0:kn], p3[:, :, 0:kn], [pst[bk_]], [stK[k].tok])
                            dma("sp", skv[s, blk, :, ccol:ccol + 1024].rearrange("p (h c) -> p h c", h=8)[:, :, 0:kn],
                                s3[:, :, 0:kn], [stK[k].tok], [dtoks["skv"]])
                            yield
                        k = n % 2; c_ = cst[n % 3]; n += 1
                        dma("pool", c_.ap[0:kn, 0:64], cki[s, r0:r0 + kn, :], [], [c_.tok])
                        fw.op("pool", lambda e, o=c_.ap[0:kn, 64:128], i_=cki[s, r0:r0 + kn, :]: e.dma_start(out=o, in_=i_),
                              [], [], dma=True, appends=[c_.tok])
                        bk_ = 4 + k
                        pb_ = ps[bk_][:].bitcast(BF16)
                        tr(pb_[:, 0:kn], c_.ap[0:kn, 0:128], idn[0:kn, 0:kn], [c_.tok, ident.tok], [pst[bk_]])
                        cp("act", stK[k].ap[:, 0:kn], pb_[:, 0:kn], [pst[bk_]], [stK[k].tok])
                        dma("sp", skv[s, blk, :, C_IKT:C_IKT + kn], stK[k].ap[:, 0:kn], [stK[k].tok], [dtoks["skv"]])
                        yield
                        for src, ccol in ((cav, C_AV), (cbv, C_BV)):
                            sv_ = stV[nv_ % 3]; nv_ += 1
                            v3 = sv_.ap.rearrange("p (h c) -> p h c", h=8)
                            dma("pool", v3[0:kn, :, 0:128], src[s, r0:r0 + kn, :].rearrange("p (h c) -> p h c", h=8),
                                [], [sv_.tok])
                            dma("sp", skv[s, blk, 0:kn, ccol:ccol + 1040], sv_.ap[0:kn, :], [sv_.tok], [dtoks["skv"]])
                            yield

            def _interleave(ga, gb, na, nb_):
                da = db = True
                ia = ib = 0
                while da or db:
                    if da and (not db or ia * max(nb_, 1) <= ib * max(na, 1)):
                        try:
                            next(ga); ia += 1
                        except StopIteration:
                            da = False
                    elif db:
                        try:
                            next(gb); ib += 1
                        except StopIteration:
                            db = False

            g_p4 = phase_win()
            next(g_p4)
            _interleave(g_p4, gen_sample_prep(), 140, 170)
            fw.barrier(nowait_on=("pool",))
            if debug and stop_after == 2:
                for t in range(NT):
                    dma("pool", dbgq[t], qs[t], [dtoks["qs"]], [dtoks["dbg"]])
                    dma("sp", dbgg[t], gs[t], [dtoks["gs"]], [dtoks["dbg"]])

        def phase_attn():

            A.off = attn_mark
            xst = Buf(A.bf16(D))
            wab = [Buf(A.bf16(8 * D)) for _ in range(3)]
            wa3, wb3, wo3 = [w_.ap.rearrange("p (k n) -> p k n", k=8) for w_ in wab]
            prep_mark = A.off
            NKMAX = 16 + 8192
            Sb = Buf(A.f32(NKMAX))
            MB = Buf(A.bf16(NKMAX))
            mbf = MB.ap.bitcast(F32)
            for wb_, src in zip(wab, (wa, wb, wo)):
                src3_ = src.rearrange("(k p) n -> p k n", p=128)
                for hf_ in range(2):
                    dma("sp", mbf[:, 0:4 * D].rearrange("p (k n) -> p k n", k=4), src3_[:, hf_ * 4:(hf_ + 1) * 4, :],
                        [], [MB.tok])
                    for q_ in range(2):
                        cp("dve", wb_.ap[:, hf_ * 4096 + q_ * 2048:hf_ * 4096 + (q_ + 1) * 2048],
                           mbf[:, q_ * 2048:(q_ + 1) * 2048], [MB.tok], [wb_.tok])
            kTb = [Buf(A.bf16(4 * 512)) for _ in range(4)]
            vb = [Buf(A.bf16(4 * 520)) for _ in range(4)]
            ikb = [Buf(A.bf16(4 * 128)) for _ in range(2)]
            pT = [Buf(A.bf16(1024)) for _ in range(2)]
            Rb = [Buf(A.f32(512)) for _ in range(2)]
            qrow = Buf(A.bf16(QW))
            oaT = Buf(A.bf16(1024)); obT = Buf(A.bf16(1024))
            tmpA, tmpB = Rb[0], Rb[1]
            mg = Buf(A.bf16(1024)); mgT = Buf(A.bf16(1024))
            x2b = Buf(A.f32(D))
            gt = Buf(A.f32(2048))
            x1b = Buf(A.f32(D))
            cmbb = Buf(A.bf16(512)); cmbf = Buf(A.f32(512))
            sublnB = Buf(A.f32(128))
            of_ = Buf(A.f32(128)); ob16 = Buf(A.bf16(128))
            lam4 = Buf(A.f32(256)); lamp = Buf(A.f32(128))
            iwt = Buf(S.f32(8))
            htab = Buf(S.f32(32)); pw2 = Buf(S.f32(32))
            for i_ in range(NBIS):
                fw.op("dve", lambda e, a=pw2.ap[:, i_:i_ + 1], v=0.5 ** (i_ + 1): e.memset(a, v), [], [pw2.tok])
            sm = {k: Buf(S.f32(1)) for k in ["lo", "h", "mid", "cnt", "g", "w0", "mx", "r1", "r2", "e1", "e2", "nlam", "ss", "rs"]}
            dma("sp", cmbf.ap.rearrange("p (r k) -> p r k", r=4), cmb.rearrange("r q k -> q r k"), [], [cmbf.tok])
            cp("dve", cmbb.ap, cmbf.ap, [cmbf.tok], [cmbb.tok])
            dma("sp", sublnB.ap, subln.partition_broadcast(128), [], [sublnB.tok])
            ts("dve", sublnB.ap, sublnB.ap, 0.8, None, ALU.mult, None, [sublnB.tok], [sublnB.tok])
            dma("sp", lam4.ap, lamv.rearrange("a b -> (a b)").partition_broadcast(128), [], [lam4.tok])
            for j, ek in enumerate(["e1", "e2"]):
                tt("dve", lamp.ap[:, 0:64], lam4.ap[:, j * 128:j * 128 + 64], lam4.ap[:, j * 128 + 64:j * 128 + 128],
                   ALU.mult, [lam4.tok], [lamp.tok])
                fw.op("dve", lambda e, o=sm[ek].ap: e.reduce_sum(out=o, in_=lamp.ap[:, 0:64], axis=AX.X),
                      [lamp.tok], [sm[ek].tok])
                act(sm[ek].ap, sm[ek].ap, AF.Exp, [sm[ek].tok], [sm[ek].tok])
            tt("dve", sm["nlam"].ap, sm["e2"].ap, sm["e1"].ap, ALU.subtract, [sm["e1"].tok, sm["e2"].tok], [sm["nlam"].tok])
            ts("dve", sm["nlam"].ap, sm["nlam"].ap, -0.2, None, ALU.add, None, [sm["nlam"].tok], [sm["nlam"].tok])
            dma("sp", gB.ap, gvec[2].partition_broadcast(128), [], [gB.tok])

            gathA4 = gathA.rearrange("(m r p) c -> p r m c", r=4, m=16)
            gathB4 = gathB.rearrange("(m r p) c -> p r m c", r=4, m=16)

            def gsrc(mm_):
                def f(c0, wdt):
                    if c0 < WA_:
                        return gathA4[:, :, mm_, c0:c0 + wdt]
                    return gathB4[:, :, mm_, c0 - WA_:c0 - WA_ + wdt]
                return f

            def asrc(ap3):
                return lambda c0, wdt: ap3[:, :, c0:c0 + wdt]
            I4 = ident.ap
            print("attn arena used", A.off, "of", A.n)


            class Ctx:
                pass

            qrows = [qrow, Buf(A.bf16(QW))]
            iwts = [iwt, Buf(S.f32(8))]
            oaTs = [oaT, Buf(A.bf16(1024))]
            obTs = [obT, Buf(A.bf16(1024))]
            ldc = {"kT": 0, "v": 0, "ik": 0}

            def load(kind, bufs, src3, c0, wdt, nb, dtok):
                b_ = bufs[ldc[kind] % len(bufs)]
                ldc[kind] += 1
                dma("sp", b_.ap[:, 0:nb * wdt].rearrange("p (b c) -> p b c", b=nb), src3(c0, wdt), [dtok], [b_.tok])
                return b_

            def make_ctx(k, t, nq, qoff, groups, causal, pair):
                c = Ctx()
                c.t, c.nq, c.qoff, c.groups, c.causal = t, nq, qoff, groups, causal
                c.qrow, c.iwt = qrows[k % 2], iwts[k % 2]
                c.oaT, c.obT = oaTs[pair], obTs[pair]
                c.nblk = sum(len(g_[2]) for g_ in groups)
                c.NK = sum(sum(g_[2]) for g_ in groups)
                return c

            def gen_loadq(c):
                dma("sp", c.qrow.ap, qs[c.t], [dtoks["qs"]], [c.qrow.tok])
                dma("sp", c.iwt.ap[0:c.nq, :], gs[c.t][c.qoff:c.qoff + c.nq, 2048:2056], [dtoks["gs"]], [c.iwt.tok])
                yield

            def gen_diff(c):
                nq, qoff = c.nq, c.qoff
                aq4 = c.qrow.ap[:, 0:2048].rearrange("p (h m c) -> p h m c", h=8, m=2)
                for g in range(2):
                    def acc(hl, mp):
                        i_ = hl * 2 + mp
                        return i_ // 3, (i_ % 3) * 160
                    bi_all = 0
                    it = 0
                    pending = None
                    for gi_, (src3, dtok, kns) in enumerate(c.groups):
                        nb = len(kns)
                        kt_ = load("kT", kTb, src3, C_AKT + 512 * g, 512, nb, dtok)
                        v_ = load("v", vb, src3, C_AV + 520 * g, 520, nb, dtok)
                        kt3 = kt_.ap[:, 0:nb * 512].rearrange("p (b c) -> p b c", b=nb)
                        v3 = v_.ap[:, 0:nb * 520].rearrange("p (b c) -> p b c", b=nb)
                        last_g = c.causal and gi_ == len(c.groups) - 1
                        for bi, kn in enumerate(kns):
                            s0 = 3 + 2 * (it % 2)
                            p_ = pT[it % 2]
                            it += 1
                            for hl in range(4):
                                bank = s0 + hl // 2
                                o3 = ps[bank][0:kn, (hl % 2) * 2 * nq:(hl % 2 + 1) * 2 * nq].rearrange("p (a b) -> p a b", a=2)
                                mm(o3, kt3[:, bi, hl * 128:hl * 128 + kn], aq4[:, 4 * g + hl, :, qoff:qoff + nq],
                                   hl % 2 == 0, not last_g, [kt_.tok, c.qrow.tok], [pst[bank]])
                            if last_g:
                                for bl in range(2):
                                    mm(ps[s0 + bl][:, :], cmbb.ap[:, bi * 128:(bi + 1) * 128], I4, False, True,
                                       [cmbb.tok, ident.tok], [pst[s0 + bl]])
                            for bl in range(2):
                                act(p_.ap[0:kn, bl * 4 * nq:(bl + 1) * 4 * nq], ps[s0 + bl][0:kn, 0:4 * nq], AF.Exp,
                                    [pst[s0 + bl]], [p_.tok], scale=0.125)
                            if pending is not None:
                                pending()

                            def pv(p_=p_, v_=v_, v3=v3, bi=bi, kn=kn, first=(bi_all == 0), last=(bi_all == c.nblk - 1)):
                                for hl in range(4):
                                    for mp in range(2):
                                        ab, ao = acc(hl, mp)
                                        mm(ps[ab][0:nq, ao:ao + 130], p_.ap[0:kn, (hl * 2 + mp) * nq:(hl * 2 + mp + 1) * nq],
                                           v3[0:kn, bi, hl * 130:hl * 130 + 130], first and (hl * 2 + mp) % 3 == 0, last,
                                           [p_.tok, v_.tok], [pst[ab]])
                            pending = pv
                            bi_all += 1
                            yield
                    pending()
                    for hl in range(4):
                        (b1, o1), (b2, o2) = acc(hl, 0), acc(hl, 1)
                        r1, r2, ss, rs = sm["r1"], sm["r2"], sm["ss"], sm["rs"]
                        fw.op("dve", lambda e, o=r1.ap[0:nq, :], i_=ps[b1][0:nq, o1 + 128:o1 + 129]: e.reciprocal(out=o, in_=i_),
                              [pst[b1]], [r1.tok])
                        fw.op("dve", lambda e, o=r2.ap[0:nq, :], i_=ps[b2][0:nq, o2 + 128:o2 + 129]: e.reciprocal(out=o, in_=i_),
                              [pst[b2]], [r2.tok])
                        tt("dve", r2.ap[0:nq, :], r2.ap[0:nq, :], sm["nlam"].ap[0:nq, :], ALU.mult, [r2.tok, sm["nlam"].tok], [r2.tok])
                        ts("dve", of_.ap[0:nq, :], ps[b1][0:nq, o1:o1 + 128], r1.ap[0:nq, :], None, ALU.mult, None,
                           [pst[b1], r1.tok], [of_.tok])
                        stt("dve", of_.ap[0:nq, :], ps[b2][0:nq, o2:o2 + 128], r2.ap[0:nq, :], of_.ap[0:nq, :], ALU.mult, ALU.add,
                            [pst[b2], r2.tok, of_.tok], [of_.tok])
                        act(junk.ap[0:nq, 0:128], of_.ap[0:nq, :], AF.Square, [of_.tok], [junk.tok, ss.tok], accum_out=ss.ap[0:nq, :])
                        rs_v = Buf(rs.ap[0:nq, :]); rs_v.tok = rs.tok
                        ss_v = Buf(ss.ap[0:nq, :]); ss_v.tok = ss.tok
                        act(rs_v.ap, ss_v.ap, AF.Ln, [ss_v.tok, epsb.tok], [rs_v.tok], scale=1.0 / 128, bias=epsb.ap[0:nq, :])
                        act(rs_v.ap, rs_v.ap, AF.Exp, [rs_v.tok], [rs_v.tok], scale=-0.5)
                        stt("dve", ob16.ap[0:nq, :], of_.ap[0:nq, :], rs.ap[0:nq, :], sublnB.ap[0:nq, :], ALU.mult, ALU.mult,
                            [of_.tok, rs.tok, sublnB.tok], [ob16.tok])
                        pb_ = ps[7][:].bitcast(BF16)
                        tr(pb_[:, 0:nq], ob16.ap[0:nq, :], idn[0:nq, 0:nq], [ob16.tok, ident.tok], [pst[7]])
                        cp("act", c.oaT.ap[:, (4 * g + hl) * 128 + qoff:(4 * g + hl) * 128 + qoff + nq], pb_[:, 0:nq],
                           [pst[7]], [c.oaT.tok])
                        yield

            IDXE = os.environ.get("K_IDXE", "dve")

            def gen_idx(c):
                nq, qoff = c.nq, c.qoff
                iq3 = c.qrow.ap[:, 3072:3584].rearrange("p (j c) -> p j c", j=4)
                c0 = 0
                it = 0
                for gi_, (src3, dtok, kns) in enumerate(c.groups):
                    nb = len(kns)
                    ik_ = load("ik", ikb, src3, C_IKT, 128, nb, dtok)
                    ik3 = ik_.ap[:, 0:nb * 128].rearrange("p (b c) -> p b c", b=nb)
                    full = all(kn == 128 for kn in kns)
                    pieces = [(0, nb, nb * 128)] if full else [(bi, 1, kn) for bi, kn in enumerate(kns)]
                    for (b0, nbp, ncols) in pieces:
                        for h in range(8):
                            j, e_ = h // 2, h % 2
                            bank = 5 + (it % 2)
                            r_ = Rb[it % 2]
                            it += 1
                            rhs = ik3[64 * e_:64 * e_ + 64, b0:b0 + nbp, :] if full else ik3[64 * e_:64 * e_ + 64, b0, 0:ncols]
                            out = ps[bank][0:nq, 0:ncols]
                            if full:
                                out = out.rearrange("p (b c) -> p b c", b=nbp)
                            mm(out, iq3[64 * e_:64 * e_ + 64, j, qoff:qoff + nq], rhs, True, True,
                               [ik_.tok, c.qrow.tok], [pst[bank]])
                            act(r_.ap[0:nq, 0:ncols], ps[bank][0:nq, 0:ncols], AF.Relu, [pst[bank]], [r_.tok])
                            if h == 0:
                                ts(IDXE, Sb.ap[0:nq, c0:c0 + ncols], r_.ap[0:nq, 0:ncols], c.iwt.ap[0:nq, 0:1], None,
                                   ALU.mult, None, [r_.tok, c.iwt.tok], [Sb.tok])
                            elif IDXE == "pool":
                                ts("pool", r_.ap[0:nq, 0:ncols], r_.ap[0:nq, 0:ncols], c.iwt.ap[0:nq, h:h + 1], None,
                                   ALU.mult, None, [r_.tok, c.iwt.tok], [r_.tok])
                                tt("pool", Sb.ap[0:nq, c0:c0 + ncols], Sb.ap[0:nq, c0:c0 + ncols], r_.ap[0:nq, 0:ncols],
                                   ALU.add, [r_.tok, Sb.tok], [Sb.tok])
                            else:
                                stt("dve", Sb.ap[0:nq, c0:c0 + ncols], r_.ap[0:nq, 0:ncols], c.iwt.ap[0:nq, h:h + 1],
                                    Sb.ap[0:nq, c0:c0 + ncols], ALU.mult, ALU.add, [r_.tok, c.iwt.tok, Sb.tok], [Sb.tok])
                            yield
                        c0 += ncols
                assert c0 == c.NK

            def gen_topk(c):
                nq, NK = c.nq, c.NK
                lo, hh_, mid, cnt, gg, w0, mx = [sm[k_] for k_ in ["lo", "h", "mid", "cnt", "g", "w0", "mx"]]
                sv = Sb.ap[0:nq, 0:NK]
                fw.op("dve", lambda e: e.tensor_reduce(out=lo.ap[0:nq, :], in_=sv, axis=AX.X, op=ALU.min), [Sb.tok], [lo.tok])
                if c.causal:
                    tt("dve", Sb.ap[0:nq, NK - 512:NK], Sb.ap[0:nq, NK - 512:NK], cmbf.ap[0:nq, :], ALU.add,
                       [Sb.tok, cmbf.tok], [Sb.tok])
                fw.op("dve", lambda e: e.reduce_max(out=mx.ap[0:nq, :], in_=sv, axis=AX.X), [Sb.tok], [mx.tok])
                tt("dve", w0.ap[0:nq, :], mx.ap[0:nq, :], lo.ap[0:nq, :], ALU.subtract, [mx.tok, lo.tok], [w0.tok])
                ts("dve", htab.ap[0:nq, 0:NBIS], pw2.ap[0:nq, 0:NBIS], w0.ap[0:nq, :], None, ALU.mult, None,
                   [pw2.tok, w0.tok], [htab.tok])
                tt("dve", mid.ap[0:nq, :], lo.ap[0:nq, :], htab.ap[0:nq, 0:1], ALU.add, [lo.tok, htab.tok], [mid.tok])
                yield
                for itb in range(NBIS):
                    ts("dve", MB.ap[0:nq, 0:NK], sv, mid.ap[0:nq, :], 0.0, ALU.is_ge, ALU.add, [Sb.tok, mid.tok],
                       [MB.tok, cnt.tok], accum_out=cnt.ap[0:nq, :])
                    ts("dve", gg.ap[0:nq, :], cnt.ap[0:nq, :], 256.0, htab.ap[0:nq, itb:itb + 1], ALU.is_ge, ALU.mult,
                       [cnt.tok, htab.tok], [gg.tok])
                    hn = htab.ap[0:nq, itb + 1:itb + 2] if itb + 1 < NBIS else htab.ap[0:nq, itb:itb + 1]
                    dst_ = mid if itb + 1 < NBIS else lo
                    stt("dve", dst_.ap[0:nq, :], mid.ap[0:nq, :], hn, gg.ap[0:nq, :], ALU.subtract, ALU.add,
                        [mid.tok, htab.tok, gg.tok], [dst_.tok])
                    yield
                ts("dve", MB.ap[0:nq, 0:NK], sv, lo.ap[0:nq, :], NEG, ALU.is_lt, ALU.mult, [Sb.tok, lo.tok], [MB.tok])
                yield

            def gen_dsa(c):
                nq, qoff = c.nq, c.qoff
                bq3 = c.qrow.ap[:, 2048:3072].rearrange("p (h c) -> p h c", h=8)
                I4q = I4[0:nq, :].rearrange("p (a b) -> p a b", a=4)[:, :, 0:nq]
                for g in range(2):
                    def accb(hl):
                        return hl // 3, (hl % 3) * 160
                    bi_all = 0
                    it = 0
                    c0 = 0
                    pending = None
                    for gi_, (src3, dtok, kns) in enumerate(c.groups):
                        nb = len(kns)
                        kt_ = load("kT", kTb, src3, C_BKT + 512 * g, 512, nb, dtok)
                        v_ = load("v", vb, src3, C_BV + 520 * g, 520, nb, dtok)
                        kt3 = kt_.ap[:, 0:nb * 512].rearrange("p (b c) -> p b c", b=nb)
                        v3 = v_.ap[:, 0:nb * 520].rearrange("p (b c) -> p b c", b=nb)
                        for bi, kn in enumerate(kns):
                            sbk = 3 + (it % 2)
                            p_ = pT[it % 2]
                            it += 1
                            for hl in range(4):
                                mm(ps[sbk][0:kn, hl * nq:(hl + 1) * nq], kt3[:, bi, hl * 128:hl * 128 + kn],
                                   bq3[:, 4 * g + hl, qoff:qoff + nq], hl == 0, False, [kt_.tok, c.qrow.tok], [pst[sbk]])
                            mm(ps[sbk][0:kn, 0:4 * nq].rearrange("p (a b) -> p a b", a=4), MB.ap[0:nq, c0:c0 + kn], I4q,
                               False, True, [MB.tok, ident.tok], [pst[sbk]])
                            act(p_.ap[0:kn, 0:4 * nq], ps[sbk][0:kn, 0:4 * nq], AF.Exp, [pst[sbk]], [p_.tok],
                                scale=float(128 ** -0.5))
                            if pending is not None:
                                pending()

                            def pv(p_=p_, v_=v_, v3=v3, bi=bi, kn=kn, first=(bi_all == 0), last=(bi_all == c.nblk - 1)):
                                for hl in range(4):
                                    ab, ao = accb(hl)
                                    mm(ps[ab][0:nq, ao:ao + 130], p_.ap[0:kn, hl * nq:(hl + 1) * nq],
                                       v3[0:kn, bi, hl * 130:hl * 130 + 130], first and hl % 3 == 0, last,
                                       [p_.tok, v_.tok], [pst[ab]])
                            pending = pv
                            bi_all += 1
                            c0 += kn
                            yield
                    pending()
                    for hl in range(4):
                        ab, ao = accb(hl)
                        r1 = sm["r1"]
                        fw.op("dve", lambda e, o=r1.ap[0:nq, :], i_=ps[ab][0:nq, ao + 128:ao + 129]: e.reciprocal(out=o, in_=i_),
                              [pst[ab]], [r1.tok])
                        ts("dve", ob16.ap[0:nq, :], ps[ab][0:nq, ao:ao + 128], r1.ap[0:nq, :], None, ALU.mult, None,
                           [pst[ab], r1.tok], [ob16.tok])
                        pb_ = ps[7][:].bitcast(BF16)
                        tr(pb_[:, 0:nq], ob16.ap[0:nq, :], idn[0:nq, 0:nq], [ob16.tok, ident.tok], [pst[7]])
                        cp("act", c.obT.ap[:, (4 * g + hl) * 128 + qoff:(4 * g + hl) * 128 + qoff + nq], pb_[:, 0:nq],
                           [pst[7]], [c.obT.tok])
                    yield

            def post(t, oaT_, obT_):
                dma("sp", gt.ap, gs[t][:, 0:2048], [dtoks["gs"]], [gt.tok])
                dma("sp", x1b.ap, x1s[t], [dtoks["x1s"]], [x1b.tok])
                oa3 = oaT_.ap.rearrange("p (h c) -> p h c", h=8)
                ob3 = obT_.ap.rearrange("p (h c) -> p h c", h=8)
                for half in range(2):
                    for h in range(8):
                        mm(ps[0][:, :], oa3[:, h, :], wa3[:, h, half * 512:(half + 1) * 512], h == 0, h == 7,
                           [oaT_.tok, wab[0].tok], [pst[0]])
                    for h in range(8):
                        mm(ps[1][:, :], ob3[:, h, :], wb3[:, h, half * 512:(half + 1) * 512], h == 0, h == 7,
                           [obT_.tok, wab[1].tok], [pst[1]])
                    tt("dve", tmpA.ap, ps[0][:, :], gt.ap[:, half * 512:(half + 1) * 512], ALU.mult, [pst[0], gt.tok], [tmpA.tok])
                    tt("dve", tmpB.ap, ps[1][:, :], gt.ap[:, 1024 + half * 512:1024 + (half + 1) * 512], ALU.mult,
                       [pst[1], gt.tok], [tmpB.tok])
                    tt("dve", mg.ap[:, half * 512:(half + 1) * 512], tmpA.ap, tmpB.ap, ALU.add, [tmpA.tok, tmpB.tok], [mg.tok])
                pb_ = ps[2][:].bitcast(BF16)
                for kc in range(8):
                    tr(pb_[:, kc * 128:(kc + 1) * 128], mg.ap[:, kc * 128:(kc + 1) * 128], idn, [mg.tok, ident.tok], [pst[2]])
                cp("act", mgT.ap, pb_, [pst[2]], [mgT.tok])
                mg3 = mgT.ap.rearrange("p (k c) -> p k c", k=8)
                for half in range(2):
                    for kc in range(8):
                        mm(ps[0][:, :], mg3[:, kc, :], wo3[:, kc, half * 512:(half + 1) * 512], kc == 0, kc == 7,
                           [mgT.tok, wab[2].tok], [pst[0]])
                    tt("dve", x2b.ap[:, half * 512:(half + 1) * 512], ps[0][:, :], x1b.ap[:, half * 512:(half + 1) * 512],
                       ALU.add, [pst[0], x1b.tok], [x2b.tok])
                dma("sp", x2s[t], x2b.ap, [x2b.tok], [dtoks["x2s"]])
                if debug and stop_after == 3:
                    dma("sp", dbg[t], x2b.ap, [x2b.tok], [dtoks["dbg"]])
                    dma("pool", dbgq[t][:, 0:1024], oaT_.ap, [oaT_.tok], [dtoks["dbg"]])
                    dma("pool", dbgq[t][:, 1024:2048], obT_.ap, [obT_.tok], [dtoks["dbg"]])
                rms_to_xT(x2b, xst.ap.rearrange("p (k c) -> p k c", k=8), xst.tok, gB, 1)
                dma("sp", xn3s[t], xst.ap, [xst.tok], [dtoks["xn3s"]])

            def run(g):
                for _ in g:
                    pass

            def interleave(ga, gb, na, nb_):
                da = db = True
                ia = ib = 0
                while da or db:
                    if da and (not db or ia * max(nb_, 1) <= ib * max(na, 1)):
                        try:
                            next(ga); ia += 1
                        except StopIteration:
                            da = False
                    elif db:
                        try:
                            next(gb); ib += 1
                        except StopIteration:
                            db = False

            mkv3 = mkv.rearrange("(b p) c -> p b c", b=1)
            ctxs = []
            k = 0
            for s in range(2):
                sk3 = skv[s].rearrange("b p c -> p b c")
                groups = [(asrc(sk3[:, 4 * q_:4 * q_ + 4, :]), dtoks["skv"], [128] * 4) for q_ in range(4)]
                groups.append((asrc(sk3[:, 16:18, :]), dtoks["skv"], [16, 32]))
                ctxs.append(make_ctx(k, 16, 32, 32 + 32 * s, groups, False, 0)); k += 1
            order = [0, 1, 2, 3] + list(range(15, 3, -1))
            for oi, t in enumerate(order):
                groups = [(asrc(mkv3), dtoks["mkv"], [16])]
                for mm_ in range(t + 1):
                    groups.append((gsrc(mm_), gtoks[mm_], [128] * 4))
                ctxs.append(make_ctx(k, t, 128, 0, groups, True, (oi + 1) % 2)); k += 1
            fw.op("dve", lambda e: e.memset(oaTs[0].ap, 0.0), [], [oaTs[0].tok])
            fw.op("dve", lambda e: e.memset(obTs[0].ap, 0.0), [], [obTs[0].tok])
            run(gen_loadq(ctxs[0]))
            run(gen_diff(ctxs[0]))
            run(gen_idx(ctxs[0]))
            for k, c in enumerate(ctxs):
                nxt = ctxs[k + 1] if k + 1 < len(ctxs) else None
                if nxt is not None:
                    run(gen_loadq(nxt))
                    interleave(gen_topk(c), gen_diff(nxt), NBIS + 2, 2 * nxt.nblk + 8)
                else:
                    run(gen_topk(c))
                if nxt is not None:
                    interleave(gen_dsa(c), gen_idx(nxt), 2 * c.nblk + 2, 8 * (len(nxt.groups) + 1))
                else:
                    run(gen_dsa(c))
                if k >= 1:
                    post(c.t, c.oaT, c.obT)
            fw.barrier()

        if stop_after >= 3:
            phase_attn()
        if stop_after >= 4:
            for t in range(NT):
                dma("sp", xnT3[:, :, t * 128:(t + 1) * 128], xn3s[t].rearrange("p (k c) -> p k c", k=8),
                    [dtoks["xn3s"]], [xnT_tok[t]])
            ffn(w2g, w2u, w2d, x2s, dtoks["x2s"], 3, True)

        fw.barrier()
        fw.run()
    return nc


def _host_prep(inputs):
    xp = np.asarray(inputs["x_prompt"], np.float32)
    xs = np.asarray(inputs["x_sample"], np.float32)
    meta = np.asarray(inputs["meta"], np.float32)
    half_a = 32
    half_b = 64
    inv_a = (np.float32(10000.0) ** (-np.arange(half_a, dtype=np.float32) / np.float32(half_a))).astype(np.float32)
    inv_b = (np.float32(10000.0) ** (-np.arange(half_b, dtype=np.float32) / np.float32(half_b))).astype(np.float32)
    ident = np.tile(np.eye(128, dtype=np.float32), (1, 4))
    gvec = np.stack([np.asarray(inputs[k], np.float32).reshape(-1) for k in ["g_ffn1", "g_mix", "g_ffn2", "g_final"]])
    lamv = np.stack([np.asarray(inputs[k], np.float32).reshape(-1) for k in ["lam_q1", "lam_k1", "lam_q2", "lam_k2"]])
    common = {
        "gvec": gvec, "subln": np.asarray(inputs["a_subln"], np.float32).reshape(-1), "lamv": lamv,
        "w1g": np.asarray(inputs["w1_gate"], np.float32)[0], "w1u": np.asarray(inputs["w1_up"], np.float32)[0],
        "w1d": np.asarray(inputs["w1_down"], np.float32)[0],
        "w2g": np.asarray(inputs["w2_gate"], np.float32)[0], "w2u": np.asarray(inputs["w2_up"], np.float32)[0],
        "w2d": np.asarray(inputs["w2_down"], np.float32)[0],
        "win": np.asarray(inputs["w_in"], np.float32)[0],
        "wa": np.asarray(inputs["w_a"], np.float32)[0], "wb": np.asarray(inputs["w_b"], np.float32)[0],
        "wo": np.asarray(inputs["w_o"], np.float32)[0],
        "ident": ident,
    }
    maps = []
    for c in range(8):
        b, i = c // 4, c % 4
        xin = np.zeros((NT, 128, D), np.float32)
        pos = np.zeros((NT, 128), np.float32)
        for m in range(16):
            j = 4 * m + i
            xin[m] = xp[b, 128 * j:128 * (j + 1)]
            pos[m] = 16 + 128 * j + np.arange(128)
        xin[16, 0:16] = meta
        pos[16, 0:16] = np.arange(16)
        for s in range(2):
            xin[16, 32 + 32 * s:64 + 32 * s] = xs[2 * c + s]
            pos[16, 32 + 32 * s:64 + 32 * s] = 16 + 2048 + np.arange(32)
        ang_a = pos[:, :, None].astype(np.float32) * inv_a[None, None, :]
        ang_b = pos[:, :, None].astype(np.float32) * inv_b[None, None, :]
        rope = np.concatenate([np.cos(ang_a), np.sin(ang_a), np.cos(ang_b), np.sin(ang_b)], -1).astype(np.float32)
        cm = np.zeros((4, 128, 128), np.float32)
        for r in range(4):
            if r > i:
                cm[r] = NEG
            elif r == i:
                cm[r, 0:64, 64:128] = NEG
        d = dict(common)
        d.update({
            "xin": xin, "rope": rope, "cmb": cm,
            "cak": np.asarray(inputs["cache_a_k"], np.float32)[0, 2 * c:2 * c + 2].reshape(2, 2064, D),
            "cav": np.asarray(inputs["cache_a_v"], np.float32)[0, 2 * c:2 * c + 2].reshape(2, 2064, D),
            "cbk": np.asarray(inputs["cache_b_k"], np.float32)[0, 2 * c:2 * c + 2].reshape(2, 2064, D),
            "cbv": np.asarray(inputs["cache_b_v"], np.float32)[0, 2 * c:2 * c + 2].reshape(2, 2064, D),
            "cki": np.asarray(inputs["cache_b_kidx"], np.float32)[0, 2 * c:2 * c + 2],
        })
        maps.append(d)
    return maps


def _assemble(results):
    y_p = np.zeros((2, 8192, D), np.float32)
    y_s = np.zeros((16, 32, D), np.float32)
    kp = [np.zeros((1, 2, 8208, 1024), np.float32) for _ in range(4)] + [np.zeros((1, 2, 8208, 64), np.float32)]
    ks = [np.zeros((1, 16, 32, 1024), np.float32) for _ in range(4)] + [np.zeros((1, 16, 32, 64), np.float32)]
    offs = [(0, 1024), (1024, 2048), (2048, 3072), (3072, 4096), (4096, 4160)]
    for c in range(8):
        b, i = c // 4, c % 4
        r = results[c]
        yy, kk = r["y"], r["kout"]
        for m in range(16):
            j = 4 * m + i
            y_p[b, 128 * j:128 * (j + 1)] = yy[m]
            for q, (a0, a1) in enumerate(offs):
                kp[q][0, b, 16 + 128 * j:16 + 128 * (j + 1)] = kk[m][:, a0:a1]
        if i == 0:
            for q, (a0, a1) in enumerate(offs):
                kp[q][0, b, 0:16] = kk[16][0:16, a0:a1]
        for s in range(2):
            y_s[2 * c + s] = yy[16][32 + 32 * s:64 + 32 * s]
            for q, (a0, a1) in enumerate(offs):
                ks[q][0, 2 * c + s] = kk[16][32 + 32 * s:64 + 32 * s, a0:a1]
    out = [y_p, y_s]
    out += [kp[0].reshape(1, 2, 8208, 8, 128), kp[1].reshape(1, 2, 8208, 8, 128),
            kp[2].reshape(1, 2, 8208, 8, 128), kp[3].reshape(1, 2, 8208, 8, 128), kp[4]]
    out += [ks[0].reshape(1, 16, 32, 8, 128), ks[1].reshape(1, 16, 32, 8, 128),
            ks[2].reshape(1, 16, 32, 8, 128), ks[3].reshape(1, 16, 32, 8, 128), ks[4]]
    return tuple(out)


_NC_CACHE = {}


def kernel(**inputs):
    maps = _host_prep(inputs)
    if "nc" not in _NC_CACHE:
        _NC_CACHE["nc"] = build_program()
    res = run_bass_kernel_spmd(_NC_CACHE["nc"], maps, core_ids=list(range(8)))
    return _assemble(res.results)
```

```python
import os
import numpy as np
import ml_dtypes
import concourse.bass as bass
import concourse.mybir as mybir
from concourse.bass_utils import run_bass_kernel_spmd
from contextlib import ExitStack

F32 = mybir.dt.float32
BF16 = mybir.dt.bfloat16
AF = mybir.ActivationFunctionType
ALU = mybir.AluOpType
AX = mybir.AxisListType

D = 1024
DFF = 2816
NFC = DFF // 128
NT = 17
TT = NT * 128
EPS = 1e-6
NCOL = 8776
KW = 4160
RW = 4256
C_AKT, C_BKT, C_IKT, C_AV, C_BV = 0, 1024, 2048, 2176, 3216
WA_, WB_ = 2176, 2080
QW = 3584
GW = 2056
NEG = -30000.0
NBIS = 16


class Tok:
    __slots__ = ("w", "r")

    def __init__(self):
        self.w = {}
        self.r = {}


class FW:
    NDS = 32

    def __init__(self, nc, es):
        self.nc = nc
        self.names = ["pe", "act", "dve", "pool", "sp"]
        self.sem = {n: es.enter_context(nc.semaphore("s_" + n)) for n in self.names}
        self.dsem = [es.enter_context(nc.semaphore("d%d" % i)) for i in range(self.NDS)]
        self.cnt = {n: 0 for n in self.names}
        self.prog = {n: [] for n in self.names}
        self.waited = {n: {} for n in self.names}
        self.ndma = 0

    def _need(self, eng, ev):
        if ev is None:
            return
        if ev[0] == "e":
            if ev[1] == eng and eng in ("pe", "sp"):
                return
            key, val = ("e", ev[1]), ev[2]
        else:
            n = ev[1]
            key, val = ("d", n % self.NDS), 16 * (n // self.NDS + 1)
        w = self.waited[eng]
        if w.get(key, 0) >= val:
            return
        w[key] = val
        self.prog[eng].append(("w", key, val))

    def op(self, eng, fn, reads=(), writes=(), dma=False, appends=()):
        for t in reads:
            for ev in t.w.values():
                self._need(eng, ev)
        for t in writes:
            for ev in t.w.values():
                self._need(eng, ev)
            for ev in t.r.values():
                self._need(eng, ev)
        for t in appends:
            for ev in t.r.values():
                self._need(eng, ev)
        if dma:
            n = self.ndma
            self.ndma += 1
            if n >= self.NDS:
                self._need(eng, ("d", n - self.NDS))
            ev = ("d", n)
            self.prog[eng].append(("dma", fn, n))
            rkey = ("d", n % self.NDS)
        else:
            self.cnt[eng] += 1
            ev = ("e", eng, self.cnt[eng])
            self.prog[eng].append(("op", fn))
            rkey = eng
        for t in reads:
            t.r[rkey] = ev
        for t in writes:
            t.w = {rkey: ev}
            t.r = {}
        for t in appends:
            t.w[rkey] = ev
        return ev

    def barrier(self, nowait_on=()):
        for e in self.names:
            for f in self.names:
                if f in nowait_on:
                    continue
                if f != e and self.cnt[f] > 0:
                    self._need(e, ("e", f, self.cnt[f]))
            for n in range(max(0, self.ndma - self.NDS), self.ndma):
                self._need(e, ("d", n))

    def replay(self, name, e):
        for it in self.prog[name]:
            if it[0] == "w":
                key, val = it[1], it[2]
                s = self.sem[key[1]] if key[0] == "e" else self.dsem[key[1]]
                e.wait_ge(s, val)
            elif it[0] == "op":
                it[1](e).then_inc(self.sem[name], 1)
            else:
                n = it[2]
                it[1](e).then_inc(self.dsem[n % self.NDS], 16)

    def run(self):
        with self.nc.Block() as block:
            @block.tensor
            def _(e):
                self.replay("pe", e)

            @block.scalar
            def _(e):
                self.replay("act", e)

            @block.vector
            def _(e):
                self.replay("dve", e)

            @block.gpsimd
            def _(e):
                self.replay("pool", e)

            @block.sync
            def _(e):
                self.replay("sp", e)


class Buf:
    __slots__ = ("ap", "tok")

    def __init__(self, ap):
        self.ap = ap
        self.tok = Tok()


class Arena:
    def __init__(self, ap):
        self.ap = ap
        self.off = 0
        self.n = ap.shape[1]

    def f32(self, n):
        a = self.ap[:, self.off:self.off + n]
        self.off += n
        assert self.off <= self.n, ("arena overflow", self.off, self.n)
        return a

    def bf16(self, n):
        assert n % 2 == 0
        return self.f32(n // 2).bitcast(BF16)


def build_program(stop_after=99, debug=False):
    nc = bass.Bass("TRN2", target_bir_lowering=False)

    def din(name, shape, dt=F32):
        return nc.dram_tensor(name, list(shape), dt, kind="ExternalInput").ap()

    def dout(name, shape, dt=F32):
        return nc.dram_tensor(name, list(shape), dt, kind="ExternalOutput").ap()

    def dint(name, shape, dt):
        return nc.dram_tensor(name, list(shape), dt).ap()

    xin = din("xin", [NT, 128, D])
    rope = din("rope", [NT, 128, 192])
    gvec = din("gvec", [4, D])
    subln = din("subln", [128])
    lamv = din("lamv", [4, 64])
    w1g = din("w1g", [D, DFF]); w1u = din("w1u", [D, DFF]); w1d = din("w1d", [DFF, D])
    w2g = din("w2g", [D, DFF]); w2u = din("w2u", [D, DFF]); w2d = din("w2d", [DFF, D])
    win = din("win", [D, NCOL])
    wa = din("wa", [D, D]); wb = din("wb", [D, D]); wo = din("wo", [D, D])
    cak = din("cak", [2, 2064, D]); cav = din("cav", [2, 2064, D])
    cbk = din("cbk", [2, 2064, D]); cbv = din("cbv", [2, 2064, D])
    cki = din("cki", [2, 2064, 64])
    cmb = din("cmb", [4, 128, 128])
    identd = din("ident", [128, 512])

    y = dout("y", [NT, 128, D])
    kout = dout("kout", [NT, 128, KW])
    dbg = dout("dbg", [NT, 128, D]) if debug else None
    dbgq = dout("dbgq", [NT, 128, QW]) if debug else None
    dbgg = dout("dbgg", [NT, 128, GW]) if debug else None

    x1s = dint("x1s", [NT, 128, D], F32)
    x2s = dint("x2s", [NT, 128, D], F32)
    xn3s = dint("xn3s", [NT, 128, D], BF16)
    qs = dint("qs", [NT, 128, QW], BF16)
    gs = dint("gs", [NT, 128, GW], F32)
    shardA = dint("shardA", [16 * 128, WA_], BF16)
    gathA = dint("gathA", [16 * 4 * 128, WA_], BF16)
    shardB = dint("shardB", [16 * 128, WB_], BF16)
    gathB = dint("gathB", [16 * 4 * 128, WB_], BF16)
    mkv = dint("mkv", [128, RW], BF16)
    skv = dint("skv", [2, 18, 128, RW], BF16)

    with ExitStack() as es:
        fw = FW(nc, es)
        arena_t = es.enter_context(nc.sbuf_tensor("arena", [128, 51200], F32))
        small_t = es.enter_context(nc.sbuf_tensor("small", [128, 1536], F32))
        ps = [es.enter_context(nc.psum_tensor("ps%d" % i, [128, 512], F32)) for i in range(8)]
        pst = [Tok() for _ in range(8)]
        A = Arena(arena_t[:])
        S = Arena(small_t[:])

        def mm(out, lhsT, rhs, start, stop, r, w):
            fw.op("pe", lambda e: e.matmul(out, lhsT=lhsT, rhs=rhs, start=start, stop=stop), r, w)

        def tr(out, in_, ident_ap, r, w):
            fw.op("pe", lambda e: e.transpose(out=out, in_=in_, identity=ident_ap), r, w)

        def act(out, in_, func, r, w, **kw):
            fw.op("act", lambda e: e.activation(out=out, in_=in_, func=func, **kw), r, w)

        def dma(eng, out, in_, r, w):
            dset = set(id(v) for v in dtoks.values()) if dtoks else set()
            wx = [t for t in w if id(t) not in dset]
            ax = [t for t in w if id(t) in dset]
            fw.op(eng, lambda e: e.dma_start(out=out, in_=in_), r, wx, dma=True, appends=ax)

        def tt(eng, out, in0, in1, op, r, w):
            fw.op(eng, lambda e: e.tensor_tensor(out=out, in0=in0, in1=in1, op=op), r, w)

        def ts(eng, out, in0, s1, s2, op0, op1, r, w, accum_out=None):
            if op1 is None:
                fw.op(eng, lambda e: e.tensor_scalar(out=out, in0=in0, scalar1=s1, scalar2=None, op0=op0), r, w)
            elif accum_out is None:
                fw.op(eng, lambda e: e.tensor_scalar(out=out, in0=in0, scalar1=s1, scalar2=s2, op0=op0, op1=op1), r, w)
            else:
                fw.op(eng, lambda e: e.tensor_scalar(out=out, in0=in0, scalar1=s1, scalar2=s2, op0=op0, op1=op1,
                                                     accum_out=accum_out), r, w)

        def stt(eng, out, in0, scalar, in1, op0, op1, r, w):
            fw.op(eng, lambda e: e.scalar_tensor_tensor(out=out, in0=in0, scalar=scalar, in1=in1, op0=op0, op1=op1), r, w)

        def cp(eng, out, in_, r, w):
            if eng == "act":
                fw.op("act", lambda e: e.copy(out=out, in_=in_), r, w)
            else:
                fw.op(eng, lambda e: e.tensor_copy(out=out, in_=in_), r, w)

        dtoks = {}
        gtoks = [Tok() for _ in range(16)]
        ident = Buf(S.bf16(512))
        dma("pool", ident.ap, identd, [], [ident.tok])
        idn = ident.ap[:, 0:128]
        gB = Buf(A.f32(D))
        stat = [Buf(S.f32(1)) for _ in range(8)]
        junk = Buf(A.f32(D))
        dtoks.update({k: Tok() for k in ["x1s", "x2s", "xn3s", "qs", "gs", "shard", "gath", "mkv", "skv", "y", "kout", "dbg"]})
        base_mark = A.off

        epsb = Buf(S.f32(1))
        fw.op("dve", lambda e: e.memset(epsb.ap, EPS), [], [epsb.tok])

        def rstd(rs, ss, n):
            p = rs.ap.shape[0]
            act(rs.ap, ss.ap, AF.Sqrt, [ss.tok, epsb.tok], [rs.tok], scale=1.0 / n, bias=epsb.ap[0:p, :])
            fw.op("dve", lambda e: e.reciprocal(out=rs.ap, in_=rs.ap), [rs.tok], [rs.tok])

        def rms_to_xT(xb, dst3, dst_tok, gbuf, bank, defer=None):
            ss, rs = stat[0], stat[1]
            act(junk.ap, xb.ap, AF.Square, [xb.tok], [junk.tok, ss.tok], accum_out=ss.ap)
            rstd(rs, ss, D)
            stt("dve", xn.ap, xb.ap, rs.ap, gbuf.ap, ALU.mult, ALU.mult, [xb.tok, rs.tok, gbuf.tok], [xn.tok])
            def pe_part():
                pb = ps[bank][:].bitcast(BF16)
                for kc in range(8):
                    tr(pb[:, kc * 128:(kc + 1) * 128], xn.ap[:, kc * 128:(kc + 1) * 128], idn,
                       [xn.tok, ident.tok], [pst[bank]])
                cp("act", dst3, pb.rearrange("p (k c) -> p k c", k=8), [pst[bank]], [dst_tok])
            if defer is None:
                pe_part()
            else:
                defer.append(pe_part)

        xn = Buf(A.bf16(D))
        attn_mark = A.off
        xnT = A.bf16(8 * TT)
        xnT3 = xnT.rearrange("p (k t) -> p k t", k=8)
        xnT_tok = [Tok() for _ in range(NT)]
        ffn_mark = A.off

        TG = [(0, 512), (512, 512), (1024, 512), (1536, 512), (2048, 128)]

        def ffn(wg, wu, wd, xsrc, xsrc_tok, g_next_idx, final):
            A.off = ffn_mark
            actT = A.bf16(NFC * TT)
            actT3 = actT.rearrange("p (c t) -> p c t", c=NFC)
            act_tok = [[Tok() for _ in range(len(TG))] for _ in range(NFC)]
            region = A.off
            wgb = [Buf(A.bf16(8 * 256)) for _ in range(2)]
            wub = [Buf(A.bf16(8 * 256)) for _ in range(2)]
            sil = [Buf(A.f32(512)) for _ in range(2)]
            NWA = 21
            wdA = Buf(A.bf16(NWA * D))
            wdA3 = wdA.ap.rearrange("p (c n) -> p c n", c=NWA)
            wd_src = wd.rearrange("(c p) n -> p c n", p=128)
            it = 0
            for fg in range(DFF // 256):
                wgt, wut = wgb[fg % 2], wub[fg % 2]
                dma("pool", wgt.ap.rearrange("p (k f) -> p k f", k=8),
                    wg[:, fg * 256:(fg + 1) * 256].rearrange("(k p) f -> p k f", p=128), [], [wgt.tok])
                dma("pool", wut.ap.rearrange("p (k f) -> p k f", k=8),
                    wu[:, fg * 256:(fg + 1) * 256].rearrange("(k p) f -> p k f", p=128), [], [wut.tok])
                wg3 = wgt.ap.rearrange("p (k f) -> p k f", k=8)
                wu3 = wut.ap.rearrange("p (k f) -> p k f", k=8)
                if fg == 1:
                    for c0_ in range(0, NWA, 7):
                        fw.op("pool", lambda e, o=wdA3[:, c0_:c0_ + 7, :], i_=wd_src[:, c0_:c0_ + 7, :]: e.dma_start(out=o, in_=i_),
                              [], [], dma=True, appends=[wdA.tok])
                for sub in range(2):
                    fc = fg * 2 + sub
                    for gi, (t0, tn) in enumerate(TG):
                        bg, bu = (it % 2) * 2, (it % 2) * 2 + 1
                        it += 1
                        rtoks = [xnT_tok[tt_] for tt_ in range(t0 // 128, (t0 + tn) // 128)]
                        for kc in range(8):
                            mm(ps[bg][:, 0:tn], wg3[:, kc, sub * 128:(sub + 1) * 128], xnT3[:, kc, t0:t0 + tn],
                               kc == 0, kc == 7, [wgt.tok] + rtoks, [pst[bg]])
                        for kc in range(8):
                            mm(ps[bu][:, 0:tn], wu3[:, kc, sub * 128:(sub + 1) * 128], xnT3[:, kc, t0:t0 + tn],
                               kc == 0, kc == 7, [wut.tok] + rtoks, [pst[bu]])
                        sb = sil[it % 2]
                        act(sb.ap[:, 0:tn], ps[bg][:, 0:tn], AF.Silu, [pst[bg]], [sb.tok])
                        tt("dve", actT3[:, fc, t0:t0 + tn], sb.ap[:, 0:tn], ps[bu][:, 0:tn], ALU.mult,
                           [sb.tok, pst[bu]], [act_tok[fc][gi]])
            fw.barrier()
            A.off = region
            wdB = Buf(A.bf16((NFC - NWA) * D))
            wdB3 = wdB.ap.rearrange("p (c n) -> p c n", c=NFC - NWA)
            dma("pool", wdB3, wd_src[:, NWA:NFC, :], [], [wdB.tok])

            def wdc(c, half):
                if c < NWA:
                    return wdA3[:, c, half * 512:(half + 1) * 512], wdA.tok
                return wdB3[:, c - NWA, half * 512:(half + 1) * 512], wdB.tok
            xb2 = [Buf(A.f32(D)) for _ in range(2)]
            xr = [Buf(A.f32(D)) for _ in range(2)]
            dma("sp", gB.ap, gvec[g_next_idx].partition_broadcast(128), [], [gB.tok])
            fpend = []
            for t in range(NT):
                xb = xb2[t % 2]
                dma("sp", xb.ap, xsrc[t], [xsrc_tok], [xb.tok])
                b0 = 4 + (t % 2) * 2
                for half in range(2):
                    for c in range(NFC):
                        wap, wtk = wdc(c, half)
                        mm(ps[b0 + half][:], actT3[:, c, t * 128:(t + 1) * 128], wap,
                           c == 0, c == NFC - 1, [wtk, act_tok[c][min(t // 4, 4)]], [pst[b0 + half]])
                while fpend:
                    fpend.pop(0)()
                x1 = xr[t % 2]
                for half in range(2):
                    stt("dve", x1.ap[:, half * 512:(half + 1) * 512], ps[b0 + half][:], 0.5,
                        xb.ap[:, half * 512:(half + 1) * 512], ALU.mult, ALU.add,
                        [pst[b0 + half], xb.tok], [x1.tok])
                if not final:
                    dma("sp", x1s[t], x1.ap, [x1.tok], [dtoks["x1s"]])
                    if debug and stop_after == 1:
                        dma("sp", dbg[t], x1.ap, [x1.tok], [dtoks["dbg"]])
                    rms_to_xT(x1, xnT3[:, :, t * 128:(t + 1) * 128], xnT_tok[t], gB, 0 + (t % 2), defer=fpend)
                else:
                    ss, rs = stat[0], stat[1]
                    act(junk.ap, x1.ap, AF.Square, [x1.tok], [junk.tok, ss.tok], accum_out=ss.ap)
                    rstd(rs, ss, D)
                    stt("dve", xb.ap, x1.ap, rs.ap, gB.ap, ALU.mult, ALU.mult, [x1.tok, rs.tok, gB.tok], [xb.tok])
                    dma("sp", y[t], xb.ap, [xb.tok], [dtoks["y"]])
            while fpend:
                fpend.pop(0)()
            fw.barrier()

        dma("sp", gB.ap, gvec[0].partition_broadcast(128), [], [gB.tok])
        A.off = ffn_mark
        xin_b = [Buf(A.f32(D)) for _ in range(2)]
        for t in range(NT):
            xb = xin_b[t % 2]
            dma("sp", xb.ap, xin[t], [], [xb.tok])
            rms_to_xT(xb, xnT3[:, :, t * 128:(t + 1) * 128], xnT_tok[t], gB, t % 2)
        fw.barrier()
        ffn(w1g, w1u, w1d, xin, Tok(), 1, False)
        if stop_after >= 2:
            def phase_win():
                A.off = ffn_mark
                ropet = Buf(A.f32(NT * 192))
                dma("sp", ropet.ap.rearrange("p (t c) -> p t c", t=NT), rope.rearrange("t p c -> p t c"), [], [ropet.tok])
                wbuf = [Buf(A.bf16(8 * 512)) for _ in range(2)]
                wstg = [Buf(A.f32(8 * 512)) for _ in range(2)]
                zf = [Buf(A.f32(512)) for _ in range(4)]
                rf = [Buf(A.f32(512)) for _ in range(4)]
                rb = [Buf(A.bf16(512)) for _ in range(4)]
                tmpd = [Buf(A.f32(256)) for _ in range(4)]
                tmpd2 = [Buf(A.f32(256)) for _ in range(4)]
                tmpp = [Buf(A.f32(256)) for _ in range(4)]
                tmpp2 = [Buf(A.f32(256)) for _ in range(4)]
                stg = [Buf(A.bf16(512)) for _ in range(4)]
                stgq = [Buf(A.bf16(1024)) for _ in range(4)]
                stgv = [Buf(A.bf16(520)) for _ in range(4)]
                gst = [Buf(A.f32(8)) for _ in range(4)]
                for b_ in stgq:
                    fw.op("dve", lambda e, a=b_.ap: e.memset(a, 0.0), [], [b_.tok])
                for b_ in stgv:
                    fw.op("dve", lambda e, a=b_.ap: e.memset(a, 0.0), [], [b_.tok])
                    v3 = b_.ap.rearrange("p (h c) -> p h c", h=4)
                    fw.op("dve", lambda e, a=v3[:, :, 128:129]: e.memset(a, 1.0), [], [b_.tok])
                groups = []
                for hh in range(2):
                    groups.append((1024 + hh * 512, 512, "ak", hh))
                for hh in range(2):
                    groups.append((2048 + hh * 512, 512, "av", hh))
                for hh in range(2):
                    groups.append((4096 + hh * 512, 512, "bk", hh))
                for hh in range(2):
                    groups.append((5120 + hh * 512, 512, "bv", hh))
                groups.append((6656, 72, "ikw", 0))
                NKG = len(groups)
                for hh in range(2):
                    groups.append((hh * 512, 512, "aq", hh))
                for hh in range(2):
                    groups.append((3072 + hh * 512, 512, "bq", hh))
                groups.append((6144, 512, "iq", 0))
                for j in range(4):
                    groups.append((6728 + j * 512, 512, "g", j))

                def gen_coll():
                    for m_ in range(16):
                        for sh_, ga_ in ((shardA, gathA), (shardB, gathB)):
                            fw.op("pool", lambda e, i_=sh_[m_ * 128:(m_ + 1) * 128, :], o_=ga_[m_ * 512:(m_ + 1) * 512, :]:
                                  e.collective_compute("AllGather", ALU.bypass, replica_groups=[[0, 1, 2, 3], [4, 5, 6, 7]],
                                                       ins=[i_], outs=[o_]),
                                  [dtoks["shard"]], [gtoks[m_]])
                            yield
                g_coll = gen_coll()
                coll_n = [0]
                cnt = [0]
                pend = []

                def rope_piece(src, dst, nv, half, cos, sin, k):
                    s4 = src.ap.rearrange("p (v two h) -> p v two h", two=2, h=half)
                    d4 = dst.ap[:, 0:nv * 2 * half].rearrange("p (v two h) -> p v two h", two=2, h=half)
                    cb = cos.unsqueeze(1).to_broadcast([128, nv, half])
                    sb_ = sin.unsqueeze(1).to_broadcast([128, nv, half])
                    x1, x2 = s4[:, :, 0, :], s4[:, :, 1, :]
                    ta, ta2, tb, tb2 = tmpd[k], tmpd2[k], tmpp[k], tmpp2[k]
                    dst.tok.w = {}
                    v3_ = lambda b_: b_.ap[:, 0:nv * half].rearrange("p (v h) -> p v h", v=nv)
                    tt("dve", v3_(ta), x1, cb, ALU.mult, [src.tok, ropet.tok], [ta.tok])
                    tt("dve", v3_(ta2), x2, sb_, ALU.mult, [src.tok, ropet.tok], [ta2.tok])
                    fw.op("dve", lambda e: e.tensor_tensor(out=d4[:, :, 0, :], in0=v3_(ta), in1=v3_(ta2), op=ALU.subtract),
                          [ta.tok, ta2.tok], [], appends=[dst.tok])
                    tt("dve", v3_(tb), x2, cb, ALU.mult, [src.tok, ropet.tok], [tb.tok])
                    tt("dve", v3_(tb2), x1, sb_, ALU.mult, [src.tok, ropet.tok], [tb2.tok])
                    fw.op("dve", lambda e: e.tensor_tensor(out=d4[:, :, 1, :], in0=v3_(tb), in1=v3_(tb2), op=ALU.add),
                          [tb.tok, tb2.tok], [], appends=[dst.tok])

                def kt_store(t, sbuf, ccol, ncols_per_head, nh):
                    s3 = sbuf.ap[:, 0:nh * 128].rearrange("p (h k) -> p h k", h=nh)
                    if t < 16:
                        dma("sp", shardA[t * 128:(t + 1) * 128, ccol:ccol + nh * 128], sbuf.ap[:, 0:nh * 128],
                            [sbuf.tok], [dtoks["shard"]])
                    else:
                        dma("sp", mkv[:, ccol:ccol + nh * 128].rearrange("p (h k) -> p h k", h=nh)[:, :, 0:16],
                            s3[:, :, 0:16], [sbuf.tok], [dtoks["mkv"]])
                        for s in range(2):
                            dma("sp", skv[s, 17, :, ccol:ccol + nh * 128].rearrange("p (h k) -> p h k", h=nh)[:, :, 0:32],
                                s3[:, :, 32 + 32 * s:64 + 32 * s], [sbuf.tok], [dtoks["skv"]])

                def v_store(t, sbuf, ccol):
                    if t < 16:
                        dma("sp", shardB[t * 128:(t + 1) * 128, ccol - WA_:ccol - WA_ + 520], sbuf.ap, [sbuf.tok], [dtoks["shard"]])
                    else:
                        dma("sp", mkv[0:16, ccol:ccol + 520], sbuf.ap[0:16, :], [sbuf.tok], [dtoks["mkv"]])
                        for s in range(2):
                            dma("sp", skv[s, 17, 0:32, ccol:ccol + 520], sbuf.ap[32 + 32 * s:64 + 32 * s, :],
                                [sbuf.tok], [dtoks["skv"]])

                for gi, (c0, wd_, kind, idx) in enumerate(groups):
                    wt = wbuf[gi % 2]
                    w3 = wt.ap.rearrange("p (k f) -> p k f", k=8)
                    if gi < NKG:
                        dma("pool", w3[:, :, 0:wd_], win[:, c0:c0 + wd_].rearrange("(k p) f -> p k f", p=128), [], [wt.tok])
                    else:
                        ws_ = wstg[gi % 2]
                        ws3 = ws_.ap.rearrange("p (k f) -> p k f", k=8)
                        dma("sp", ws3[:, :, 0:wd_], win[:, c0:c0 + wd_].rearrange("(k p) f -> p k f", p=128), [], [ws_.tok])
                        for kc2 in range(0, 8, 2):
                            cp("dve", w3[:, kc2:kc2 + 2, 0:wd_], ws3[:, kc2:kc2 + 2, 0:wd_], [ws_.tok], [wt.tok])
                    for t in range(NT):
                        k = cnt[0] % 4
                        pb, tb_ = cnt[0] % 2, 2 + cnt[0] % 2
                        cnt[0] += 1
                        for kc in range(8):
                            mm(ps[pb][:, 0:wd_], xnT3[:, kc, t * 128:(t + 1) * 128], w3[:, kc, 0:wd_], kc == 0, kc == 7,
                               [wt.tok, xnT_tok[t]], [pst[pb]])
                        while len(pend) > 1:
                            pend.pop(0)()
                        cA = ropet.ap[:, t * 192:t * 192 + 32]
                        sA = ropet.ap[:, t * 192 + 32:t * 192 + 64]
                        cB = ropet.ap[:, t * 192 + 64:t * 192 + 128]
                        sB = ropet.ap[:, t * 192 + 128:t * 192 + 192]
                        ptb = ps[tb_][:].bitcast(BF16)
                        if kind in ("aq", "ak", "bq", "bk", "iq"):
                            cp("act", zf[k].ap, ps[pb][:], [pst[pb]], [zf[k].tok])
                            isb = kind in ("bq", "bk")
                            nv, half = (4, 64) if isb else (8, 32)
                            cos, sin = (cB, sB) if isb else (cA, sA)
                            if kind in ("ak", "bk"):
                                rope_piece(zf[k], rf[k], nv, half, cos, sin, k)
                                kc0 = (0 if kind == "ak" else 2048) + idx * 512
                                dma("sp", kout[t][:, kc0:kc0 + 512], rf[k].ap, [rf[k].tok], [dtoks["kout"]])
                                cp("act", rb[k].ap, rf[k].ap, [rf[k].tok], [rb[k].tok])
                            else:
                                rope_piece(zf[k], rb[k], nv, half, cos, sin, k)
                            def fin(k=k, t=t, kind=kind, idx=idx, tb_=tb_, ptb=ptb):
                              for h in range(4):
                                tr(ptb[:, h * 128:(h + 1) * 128], rb[k].ap[:, h * 128:(h + 1) * 128], idn,
                                   [rb[k].tok, ident.tok], [pst[tb_]])
                              if kind == "aq":
                                q4 = stgq[k].ap.rearrange("p (h m c) -> p h m c", h=4, m=2)
                                p3 = ptb[:, 0:512].rearrange("p (h c) -> p h c", h=4)
                                cp("act", q4[0:64, :, 0, :], p3[0:64, :, :], [pst[tb_]], [stgq[k].tok])
                                cp("act", q4[64:128, :, 1, :], p3[64:128, :, :], [pst[tb_]], [stgq[k].tok])
                                dma("sp", qs[t][:, idx * 1024:(idx + 1) * 1024], stgq[k].ap, [stgq[k].tok], [dtoks["qs"]])
                              else:
                                cp("act", stg[k].ap, ptb[:, 0:512], [pst[tb_]], [stg[k].tok])
                                if kind == "bq":
                                    dma("sp", qs[t][:, 2048 + idx * 512:2048 + (idx + 1) * 512], stg[k].ap,
                                        [stg[k].tok], [dtoks["qs"]])
                                elif kind == "iq":
                                    dma("sp", qs[t][:, 3072:3584], stg[k].ap, [stg[k].tok], [dtoks["qs"]])
                                else:
                                    kt_store(t, stg[k], (C_AKT if kind == "ak" else C_BKT) + idx * 512, 128, 4)
                            pend.append(fin)
                        elif kind in ("av", "bv"):
                            cp("act", rf[k].ap, ps[pb][:], [pst[pb]], [rf[k].tok])
                            kc0 = (1024 if kind == "av" else 3072) + idx * 512
                            dma("sp", kout[t][:, kc0:kc0 + 512], rf[k].ap, [rf[k].tok], [dtoks["kout"]])
                            v3 = stgv[k].ap.rearrange("p (h c) -> p h c", h=4)
                            cp("dve", v3[:, :, 0:128], rf[k].ap.rearrange("p (h c) -> p h c", h=4), [rf[k].tok], [stgv[k].tok])
                            v_store(t, stgv[k], (C_AV if kind == "av" else C_BV) + idx * 520)
                        elif kind == "ikw":
                            cp("act", zf[k].ap[:, 0:72], ps[pb][:, 0:72], [pst[pb]], [zf[k].tok])
                            zsub = Buf(zf[k].ap[:, 0:64]); zsub.tok = zf[k].tok
                            rsub = Buf(rf[k].ap[:, 0:64]); rsub.tok = rf[k].tok
                            rope_piece(zsub, rsub, 1, 32, cA, sA, k)
                            dma("sp", kout[t][:, 4096:4160], rf[k].ap[:, 0:64], [rf[k].tok], [dtoks["kout"]])
                            cp("act", rb[k].ap[:, 0:64], rf[k].ap[:, 0:64], [rf[k].tok], [rb[k].tok])
                            cp("act", rb[k].ap[:, 64:128], rf[k].ap[:, 0:64], [rf[k].tok], [rb[k].tok])
                            def fin2(k=k, t=t, tb_=tb_, ptb=ptb):
                                tr(ptb[:, 0:128], rb[k].ap[:, 0:128], idn, [rb[k].tok, ident.tok], [pst[tb_]])
                                cp("act", stg[k].ap[:, 0:128], ptb[:, 0:128], [pst[tb_]], [stg[k].tok])
                                kt_store(t, stg[k], C_IKT, 128, 1)
                            pend.append(fin2)
                            cp("dve", gst[k].ap, zf[k].ap[:, 64:72], [zf[k].tok], [gst[k].tok])
                            dma("sp", gs[t][:, 2048:2056], gst[k].ap, [gst[k].tok], [dtoks["gs"]])
                        else:
                            act(rf[k].ap, ps[pb][:], AF.Sigmoid, [pst[pb]], [rf[k].tok])
                            dma("sp", gs[t][:, idx * 512:(idx + 1) * 512], rf[k].ap, [rf[k].tok], [dtoks["gs"]])
                        if gi >= NKG and t >= 3:
                            npieces = (gi - NKG) * NT + t - 3
                            while coll_n[0] < 32 and coll_n[0] * 3 <= npieces:
                                next(g_coll)
                                coll_n[0] += 1
                        yield
                while pend:
                    pend.pop(0)()
                while coll_n[0] < 32:
                    next(g_coll)
                    coll_n[0] += 1
                yield

            def gen_sample_prep():
                cst = [Buf(A.bf16(D)) for _ in range(3)]
                stK = [Buf(A.bf16(D)) for _ in range(2)]
                stV = [Buf(A.bf16(1040)) for _ in range(3)]
                for b_ in stV:
                    fw.op("dve", lambda e, a=b_.ap: e.memset(a, 0.0), [], [b_.tok])
                    fw.op("dve", lambda e, a=b_.ap.rearrange("p (h c) -> p h c", h=8)[:, :, 128:129]: e.memset(a, 1.0),
                          [], [b_.tok])
                n = 0
                nv_ = 0
                for s in range(2):
                    for blk in range(17):
                        r0 = blk * 128
                        kn = 128 if blk < 16 else 16
                        for src, ccol in ((cak, C_AKT), (cbk, C_BKT)):
                            k = n % 2; c_ = cst[n % 3]; n += 1
                            dma("pool", c_.ap[0:kn, :], src[s, r0:r0 + kn, :], [], [c_.tok])
                            bk_ = 4 + k
                            pb_ = ps[bk_][:].bitcast(BF16)
                            for h in range(8):
                                tr(pb_[:, h * 128:h * 128 + kn], c_.ap[0:kn, h * 128:(h + 1) * 128], idn[0:kn, 0:kn],
                                   [c_.tok, ident.tok], [pst[bk_]])
                            p3 = pb_.rearrange("p (h c) -> p h c", h=8)
                            s3 = stK[k].ap.rearrange("p (h c) -> p h c", h=8)
                            cp("act", s3[:, :, 0:kn], p3[:, :, 0:kn], [pst[bk_]], [stK[k].tok])
                            dma("sp", skv[s, blk, :, ccol:ccol + 1024].rearrange("p (h c) -> p h c", h=8)[:, :, 0:kn],
                                s3[:, :, 0:kn], [stK[k].tok], [dtoks["skv"]])
                            yield
                        k = n % 2; c_ = cst[n % 3]; n += 1
                        dma("pool", c_.ap[0:kn, 0:64], cki[s, r0:r0 + kn, :], [], [c_.tok])
                        fw.op("pool", lambda e, o=c_.ap[0:kn, 64:128], i_=cki[s, r0:r0 + kn, :]: e.dma_start(out=o, in_=i_),
                              [], [], dma=True, appends=[c_.tok])
                        bk_ = 4 + k
                        pb_ = ps[bk_][:].bitcast(BF16)
                        tr(pb_[:, 0:kn], c_.ap[0:kn, 0:128], idn[0:kn, 0:kn], [c_.tok, ident.tok], [pst[bk_]])
                        cp("act", stK[k].ap[:, 0:kn], pb_[:, 0:kn], [pst[bk_]], [stK[k].tok])
                        dma("sp", skv[s, blk, :, C_IKT:C_IKT + kn], stK[k].ap[:, 0:kn], [stK[k].tok], [dtoks["skv"]])
                        yield
                        for src, ccol in ((cav, C_AV), (cbv, C_BV)):
                            sv_ = stV[nv_ % 3]; nv_ += 1
                            v3 = sv_.ap.rearrange("p (h c) -> p h c", h=8)
                            dma("pool", v3[0:kn, :, 0:128], src[s, r0:r0 + kn, :].rearrange("p (h c) -> p h c", h=8),
                                [], [sv_.tok])
                            dma("sp", skv[s, blk, 0:kn, ccol:ccol + 1040], sv_.ap[0:kn, :], [sv_.tok], [dtoks["skv"]])
                            yield

            def _interleave(ga, gb, na, nb_):
                da = db = True
                ia = ib = 0
                while da or db:
                    if da and (not db or ia * max(nb_, 1) <= ib * max(na, 1)):
                        try:
                            next(ga); ia += 1
                        except StopIteration:
                            da = False
                    elif db:
                        try:
                            next(gb); ib += 1
                        except StopIteration:
                            db = False

            g_p4 = phase_win()
            next(g_p4)
            _interleave(g_p4, gen_sample_prep(), 140, 170)
            fw.barrier(nowait_on=("pool",))
            if debug and stop_after == 2:
                for t in range(NT):
                    dma("pool", dbgq[t], qs[t], [dtoks["qs"]], [dtoks["dbg"]])
                    dma("sp", dbgg[t], gs[t], [dtoks["gs"]], [dtoks["dbg"]])

        def phase_attn():

            A.off = attn_mark
            xst = Buf(A.bf16(D))
            wab = [Buf(A.bf16(8 * D)) for _ in range(3)]
            wa3, wb3, wo3 = [w_.ap.rearrange("p (k n) -> p k n", k=8) for w_ in wab]
            prep_mark = A.off
            NKMAX = 16 + 8192
            Sb = Buf(A.f32(NKMAX))
            MB = Buf(A.bf16(NKMAX))

            kTb = [Buf(A.bf16(4 * 512)) for _ in range(4)]
            vb = [Buf(A.bf16(4 * 520)) for _ in range(4)]
            ikb = [Buf(A.bf16(4 * 128)) for _ in range(2)]
            pT = [Buf(A.bf16(1024)) for _ in range(2)]
            Rb = [Buf(A.f32(512)) for _ in range(2)]
            qrow = Buf(A.bf16(QW))
            oaT = Buf(A.bf16(1024)); obT = Buf(A.bf16(1024))
            tmpA, tmpB = Rb[0], Rb[1]
            mg = Buf(A.bf16(1024)); mgT = Buf(A.bf16(1024))
            wst_off = A.off
            x2b = Buf(A.f32(D))
            gt = Buf(A.f32(2048))
            x1b = Buf(A.f32(D))
            slotB = A.ap[:, wst_off:wst_off + 4 * D]
            cmbb = Buf(A.bf16(512)); cmbf = Buf(A.f32(512))
            sublnB = Buf(A.f32(128))
            of_ = Buf(A.f32(128)); ob16 = Buf(A.bf16(128))
            lam4 = Buf(A.f32(256)); lamp = Buf(A.f32(128))
            iwt = Buf(S.f32(8))
            htab = Buf(S.f32(32)); pw2 = Buf(S.f32(32))
            for i_ in range(NBIS):
                fw.op("dve", lambda e, a=pw2.ap[:, i_:i_ + 1], v=0.5 ** (i_ + 1): e.memset(a, v), [], [pw2.tok])
            sm = {k: Buf(S.f32(1)) for k in ["lo", "h", "mid", "cnt", "g", "w0", "mx", "r1", "r2", "e1", "e2", "nlam", "ss", "rs"]}
            dma("sp", cmbf.ap.rearrange("p (r k) -> p r k", r=4), cmb.rearrange("r q k -> q r k"), [], [cmbf.tok])
            cp("dve", cmbb.ap, cmbf.ap, [cmbf.tok], [cmbb.tok])
            dma("sp", sublnB.ap, subln.partition_broadcast(128), [], [sublnB.tok])
            ts("dve", sublnB.ap, sublnB.ap, 0.8, None, ALU.mult, None, [sublnB.tok], [sublnB.tok])
            dma("sp", lam4.ap, lamv.rearrange("a b -> (a b)").partition_broadcast(128), [], [lam4.tok])
            for j, ek in enumerate(["e1", "e2"]):
                tt("dve", lamp.ap[:, 0:64], lam4.ap[:, j * 128:j * 128 + 64], lam4.ap[:, j * 128 + 64:j * 128 + 128],
                   ALU.mult, [lam4.tok], [lamp.tok])
                fw.op("dve", lambda e, o=sm[ek].ap: e.reduce_sum(out=o, in_=lamp.ap[:, 0:64], axis=AX.X),
                      [lamp.tok], [sm[ek].tok])
                act(sm[ek].ap, sm[ek].ap, AF.Exp, [sm[ek].tok], [sm[ek].tok])
            tt("dve", sm["nlam"].ap, sm["e2"].ap, sm["e1"].ap, ALU.subtract, [sm["e1"].tok, sm["e2"].tok], [sm["nlam"].tok])
            ts("dve", sm["nlam"].ap, sm["nlam"].ap, -0.2, None, ALU.add, None, [sm["nlam"].tok], [sm["nlam"].tok])
            dma("sp", gB.ap, gvec[2].partition_broadcast(128), [], [gB.tok])

            gathA4 = gathA.rearrange("(m r p) c -> p r m c", r=4, m=16)
            gathB4 = gathB.rearrange("(m r p) c -> p r m c", r=4, m=16)

            def gsrc(mm_):
                def f(c0, wdt):
                    if c0 < WA_:
                        return gathA4[:, :, mm_, c0:c0 + wdt]
                    return gathB4[:, :, mm_, c0 - WA_:c0 - WA_ + wdt]
                return f

            def asrc(ap3):
                return lambda c0, wdt: ap3[:, :, c0:c0 + wdt]
            I4 = ident.ap
            print("attn arena used", A.off, "of", A.n)


            class Ctx:
                pass

            qrows = [qrow, Buf(A.bf16(QW))]
            iwts = [iwt, Buf(S.f32(8))]
            oaTs = [oaT, Buf(A.bf16(1024))]
            obTs = [obT, Buf(A.bf16(1024))]
            ldc = {"kT": 0, "v": 0, "ik": 0}

            def load(kind, bufs, src3, c0, wdt, nb, dtok):
                b_ = bufs[ldc[kind] % len(bufs)]
                ldc[kind] += 1
                dma("sp", b_.ap[:, 0:nb * wdt].rearrange("p (b c) -> p b c", b=nb), src3(c0, wdt), [dtok], [b_.tok])
                return b_

            def make_ctx(k, t, nq, qoff, groups, causal, pair):
                c = Ctx()
                c.t, c.nq, c.qoff, c.groups, c.causal = t, nq, qoff, groups, causal
                c.qrow, c.iwt = qrows[k % 2], iwts[k % 2]
                c.oaT, c.obT = oaTs[pair], obTs[pair]
                c.nblk = sum(len(g_[2]) for g_ in groups)
                c.NK = sum(sum(g_[2]) for g_ in groups)
                return c

            def gen_loadq(c):
                dma("sp", c.qrow.ap, qs[c.t], [dtoks["qs"]], [c.qrow.tok])
                dma("sp", c.iwt.ap[0:c.nq, :], gs[c.t][c.qoff:c.qoff + c.nq, 2048:2056], [dtoks["gs"]], [c.iwt.tok])
                yield

            def gen_diff(c):
                nq, qoff = c.nq, c.qoff
                aq4 = c.qrow.ap[:, 0:2048].rearrange("p (h m c) -> p h m c", h=8, m=2)
                for g in range(2):
                    def acc(hl, mp):
                        i_ = hl * 2 + mp
                        return i_ // 3, (i_ % 3) * 160
                    bi_all = 0
                    it = 0
                    pending = None
                    for gi_, (src3, dtok, kns) in enumerate(c.groups):
                        nb = len(kns)
                        kt_ = load("kT", kTb, src3, C_AKT + 512 * g, 512, nb, dtok)
                        v_ = load("v", vb, src3, C_AV + 520 * g, 520, nb, dtok)
                        kt3 = kt_.ap[:, 0:nb * 512].rearrange("p (b c) -> p b c", b=nb)
                        v3 = v_.ap[:, 0:nb * 520].rearrange("p (b c) -> p b c", b=nb)
                        last_g = c.causal and gi_ == len(c.groups) - 1
                        for bi, kn in enumerate(kns):
                            s0 = 3 + 2 * (it % 2)
                            p_ = pT[it % 2]
                            it += 1
                            for hl in range(4):
                                bank = s0 + hl // 2
                                o3 = ps[bank][0:kn, (hl % 2) * 2 * nq:(hl % 2 + 1) * 2 * nq].rearrange("p (a b) -> p a b", a=2)
                                mm(o3, kt3[:, bi, hl * 128:hl * 128 + kn], aq4[:, 4 * g + hl, :, qoff:qoff + nq],
                                   hl % 2 == 0, not last_g, [kt_.tok, c.qrow.tok], [pst[bank]])
                            if last_g:
                                for bl in range(2):
                                    mm(ps[s0 + bl][:, :], cmbb.ap[:, bi * 128:(bi + 1) * 128], I4, False, True,
                                       [cmbb.tok, ident.tok], [pst[s0 + bl]])
                            for bl in range(2):
                                act(p_.ap[0:kn, bl * 4 * nq:(bl + 1) * 4 * nq], ps[s0 + bl][0:kn, 0:4 * nq], AF.Exp,
                                    [pst[s0 + bl]], [p_.tok], scale=0.125)
                            if pending is not None:
                                pending()

                            def pv(p_=p_, v_=v_, v3=v3, bi=bi, kn=kn, first=(bi_all == 0), last=(bi_all == c.nblk - 1)):
                                for hl in range(4):
                                    for mp in range(2):
                                        ab, ao = acc(hl, mp)
                                        mm(ps[ab][0:nq, ao:ao + 130], p_.ap[0:kn, (hl * 2 + mp) * nq:(hl * 2 + mp + 1) * nq],
                                           v3[0:kn, bi, hl * 130:hl * 130 + 130], first and (hl * 2 + mp) % 3 == 0, last,
                                           [p_.tok, v_.tok], [pst[ab]])
                            pending = pv
                            bi_all += 1
                            yield
                    pending()
                    for hl in range(4):
                        (b1, o1), (b2, o2) = acc(hl, 0), acc(hl, 1)
                        r1, r2, ss, rs = sm["r1"], sm["r2"], sm["ss"], sm["rs"]
                        fw.op("dve", lambda e, o=r1.ap[0:nq, :], i_=ps[b1][0:nq, o1 + 128:o1 + 129]: e.reciprocal(out=o, in_=i_),
                              [pst[b1]], [r1.tok])
                        fw.op("dve", lambda e, o=r2.ap[0:nq, :], i_=ps[b2][0:nq, o2 + 128:o2 + 129]: e.reciprocal(out=o, in_=i_),
                              [pst[b2]], [r2.tok])
                        tt("dve", r2.ap[0:nq, :], r2.ap[0:nq, :], sm["nlam"].ap[0:nq, :], ALU.mult, [r2.tok, sm["nlam"].tok], [r2.tok])
                        ts("dve", of_.ap[0:nq, :], ps[b1][0:nq, o1:o1 + 128], r1.ap[0:nq, :], None, ALU.mult, None,
                           [pst[b1], r1.tok], [of_.tok])
                        stt("dve", of_.ap[0:nq, :], ps[b2][0:nq, o2:o2 + 128], r2.ap[0:nq, :], of_.ap[0:nq, :], ALU.mult, ALU.add,
                            [pst[b2], r2.tok, of_.tok], [of_.tok])
                        act(junk.ap[0:nq, 0:128], of_.ap[0:nq, :], AF.Square, [of_.tok], [junk.tok, ss.tok], accum_out=ss.ap[0:nq, :])
                        rs_v = Buf(rs.ap[0:nq, :]); rs_v.tok = rs.tok
                        ss_v = Buf(ss.ap[0:nq, :]); ss_v.tok = ss.tok
                        act(rs_v.ap, ss_v.ap, AF.Ln, [ss_v.tok, epsb.tok], [rs_v.tok], scale=1.0 / 128, bias=epsb.ap[0:nq, :])
                        act(rs_v.ap, rs_v.ap, AF.Exp, [rs_v.tok], [rs_v.tok], scale=-0.5)
                        stt("dve", ob16.ap[0:nq, :], of_.ap[0:nq, :], rs.ap[0:nq, :], sublnB.ap[0:nq, :], ALU.mult, ALU.mult,
                            [of_.tok, rs.tok, sublnB.tok], [ob16.tok])
                        pb_ = ps[7][:].bitcast(BF16)
                        tr(pb_[:, 0:nq], ob16.ap[0:nq, :], idn[0:nq, 0:nq], [ob16.tok, ident.tok], [pst[7]])
                        cp("act", c.oaT.ap[:, (4 * g + hl) * 128 + qoff:(4 * g + hl) * 128 + qoff + nq], pb_[:, 0:nq],
                           [pst[7]], [c.oaT.tok])
                        yield

            IDXE = os.environ.get("K_IDXE", "dve")

            def gen_idx(c):
                nq, qoff = c.nq, c.qoff
                iq3 = c.qrow.ap[:, 3072:3584].rearrange("p (j c) -> p j c", j=4)
                c0 = 0
                it = 0
                for gi_, (src3, dtok, kns) in enumerate(c.groups):
                    nb = len(kns)
                    ik_ = load("ik", ikb, src3, C_IKT, 128, nb, dtok)
                    ik3 = ik_.ap[:, 0:nb * 128].rearrange("p (b c) -> p b c", b=nb)
                    full = all(kn == 128 for kn in kns)
                    pieces = [(0, nb, nb * 128)] if full else [(bi, 1, kn) for bi, kn in enumerate(kns)]
                    for (b0, nbp, ncols) in pieces:
                        for h in range(8):
                            j, e_ = h // 2, h % 2
                            bank = 5 + (it % 2)
                            r_ = Rb[it % 2]
                            it += 1
                            rhs = ik3[64 * e_:64 * e_ + 64, b0:b0 + nbp, :] if full else ik3[64 * e_:64 * e_ + 64, b0, 0:ncols]
                            out = ps[bank][0:nq, 0:ncols]
                            if full:
                                out = out.rearrange("p (b c) -> p b c", b=nbp)
                            mm(out, iq3[64 * e_:64 * e_ + 64, j, qoff:qoff + nq], rhs, True, True,
                               [ik_.tok, c.qrow.tok], [pst[bank]])
                            act(r_.ap[0:nq, 0:ncols], ps[bank][0:nq, 0:ncols], AF.Relu, [pst[bank]], [r_.tok])
                            if h == 0:
                                ts(IDXE, Sb.ap[0:nq, c0:c0 + ncols], r_.ap[0:nq, 0:ncols], c.iwt.ap[0:nq, 0:1], None,
                                   ALU.mult, None, [r_.tok, c.iwt.tok], [Sb.tok])
                            elif IDXE == "pool":
                                ts("pool", r_.ap[0:nq, 0:ncols], r_.ap[0:nq, 0:ncols], c.iwt.ap[0:nq, h:h + 1], None,
                                   ALU.mult, None, [r_.tok, c.iwt.tok], [r_.tok])
                                tt("pool", Sb.ap[0:nq, c0:c0 + ncols], Sb.ap[0:nq, c0:c0 + ncols], r_.ap[0:nq, 0:ncols],
                                   ALU.add, [r_.tok, Sb.tok], [Sb.tok])
                            else:
                                stt("dve", Sb.ap[0:nq, c0:c0 + ncols], r_.ap[0:nq, 0:ncols], c.iwt.ap[0:nq, h:h + 1],
                                    Sb.ap[0:nq, c0:c0 + ncols], ALU.mult, ALU.add, [r_.tok, c.iwt.tok, Sb.tok], [Sb.tok])
                            yield
                        c0 += ncols
                assert c0 == c.NK

            def gen_topk(c):
                nq, NK = c.nq, c.NK
                lo, hh_, mid, cnt, gg, w0, mx = [sm[k_] for k_ in ["lo", "h", "mid", "cnt", "g", "w0", "mx"]]
                sv = Sb.ap[0:nq, 0:NK]
                fw.op("dve", lambda e: e.tensor_reduce(out=lo.ap[0:nq, :], in_=sv, axis=AX.X, op=ALU.min), [Sb.tok], [lo.tok])
                if c.causal:
                    tt("dve", Sb.ap[0:nq, NK - 512:NK], Sb.ap[0:nq, NK - 512:NK], cmbf.ap[0:nq, :], ALU.add,
                       [Sb.tok, cmbf.tok], [Sb.tok])
                fw.op("dve", lambda e: e.reduce_max(out=mx.ap[0:nq, :], in_=sv, axis=AX.X), [Sb.tok], [mx.tok])
                tt("dve", w0.ap[0:nq, :], mx.ap[0:nq, :], lo.ap[0:nq, :], ALU.subtract, [mx.tok, lo.tok], [w0.tok])
                ts("dve", htab.ap[0:nq, 0:NBIS], pw2.ap[0:nq, 0:NBIS], w0.ap[0:nq, :], None, ALU.mult, None,
                   [pw2.tok, w0.tok], [htab.tok])
                tt("dve", mid.ap[0:nq, :], lo.ap[0:nq, :], htab.ap[0:nq, 0:1], ALU.add, [lo.tok, htab.tok], [mid.tok])
                yield
                for itb in range(NBIS):
                    ts("dve", MB.ap[0:nq, 0:NK], sv, mid.ap[0:nq, :], 0.0, ALU.is_ge, ALU.add, [Sb.tok, mid.tok],
                       [MB.tok, cnt.tok], accum_out=cnt.ap[0:nq, :])
                    ts("dve", gg.ap[0:nq, :], cnt.ap[0:nq, :], 256.0, htab.ap[0:nq, itb:itb + 1], ALU.is_ge, ALU.mult,
                       [cnt.tok, htab.tok], [gg.tok])
                    hn = htab.ap[0:nq, itb + 1:itb + 2] if itb + 1 < NBIS else htab.ap[0:nq, itb:itb + 1]
                    dst_ = mid if itb + 1 < NBIS else lo
                    stt("dve", dst_.ap[0:nq, :], mid.ap[0:nq, :], hn, gg.ap[0:nq, :], ALU.subtract, ALU.add,
                        [mid.tok, htab.tok, gg.tok], [dst_.tok])
                    yield
                ts("dve", MB.ap[0:nq, 0:NK], sv, lo.ap[0:nq, :], NEG, ALU.is_lt, ALU.mult, [Sb.tok, lo.tok], [MB.tok])
                yield

            def gen_dsa(c):
                nq, qoff = c.nq, c.qoff
                bq3 = c.qrow.ap[:, 2048:3072].rearrange("p (h c) -> p h c", h=8)
                I4q = I4[0:nq, :].rearrange("p (a b) -> p a b", a=4)[:, :, 0:nq]
                for g in range(2):
                    def accb(hl):
                        return hl // 3, (hl % 3) * 160
                    bi_all = 0
                    it = 0
                    c0 = 0
                    pending = None
                    for gi_, (src3, dtok, kns) in enumerate(c.groups):
                        nb = len(kns)
                        kt_ = load("kT", kTb, src3, C_BKT + 512 * g, 512, nb, dtok)
                        v_ = load("v", vb, src3, C_BV + 520 * g, 520, nb, dtok)
                        kt3 = kt_.ap[:, 0:nb * 512].rearrange("p (b c) -> p b c", b=nb)
                        v3 = v_.ap[:, 0:nb * 520].rearrange("p (b c) -> p b c", b=nb)
                        for bi, kn in enumerate(kns):
                            sbk = 3 + (it % 2)
                            p_ = pT[it % 2]
                            it += 1
                            for hl in range(4):
                                mm(ps[sbk][0:kn, hl * nq:(hl + 1) * nq], kt3[:, bi, hl * 128:hl * 128 + kn],
                                   bq3[:, 4 * g + hl, qoff:qoff + nq], hl == 0, False, [kt_.tok, c.qrow.tok], [pst[sbk]])
                            mm(ps[sbk][0:kn, 0:4 * nq].rearrange("p (a b) -> p a b", a=4), MB.ap[0:nq, c0:c0 + kn], I4q,
                               False, True, [MB.tok, ident.tok], [pst[sbk]])
                            act(p_.ap[0:kn, 0:4 * nq], ps[sbk][0:kn, 0:4 * nq], AF.Exp, [pst[sbk]], [p_.tok],
                                scale=float(128 ** -0.5))
                            if pending is not None:
                                pending()

                            def pv(p_=p_, v_=v_, v3=v3, bi=bi, kn=kn, first=(bi_all == 0), last=(bi_all == c.nblk - 1)):
                                for hl in range(4):
                                    ab, ao = accb(hl)
                                    mm(ps[ab][0:nq, ao:ao + 130], p_.ap[0:kn, hl * nq:(hl + 1) * nq],
                                       v3[0:kn, bi, hl * 130:hl * 130 + 130], first and hl % 3 == 0, last,
                                       [p_.tok, v_.tok], [pst[ab]])
                            pending = pv
                            bi_all += 1
                            c0 += kn
                            yield
                    pending()
                    for hl in range(4):
                        ab, ao = accb(hl)
                        r1 = sm["r1"]
                        fw.op("dve", lambda e, o=r1.ap[0:nq, :], i_=ps[ab][0:nq, ao + 128:ao + 129]: e.reciprocal(out=o, in_=i_),
                              [pst[ab]], [r1.tok])
                        ts("dve", ob16.ap[0:nq, :], ps[ab][0:nq, ao:ao + 128], r1.ap[0:nq, :], None, ALU.mult, None,
                           [pst[ab], r1.tok], [ob16.tok])
                        pb_ = ps[7][:].bitcast(BF16)
                        tr(pb_[:, 0:nq], ob16.ap[0:nq, :], idn[0:nq, 0:nq], [ob16.tok, ident.tok], [pst[7]])
                        cp("act", c.obT.ap[:, (4 * g + hl) * 128 + qoff:(4 * g + hl) * 128 + qoff + nq], pb_[:, 0:nq],
                           [pst[7]], [c.obT.tok])
                    yield

            def post(t, oaT_, obT_):
                dma("sp", gt.ap, gs[t][:, 0:2048], [dtoks["gs"]], [gt.tok])
                dma("sp", x1b.ap, x1s[t], [dtoks["x1s"]], [x1b.tok])
                oa3 = oaT_.ap.rearrange("p (h c) -> p h c", h=8)
                ob3 = obT_.ap.rearrange("p (h c) -> p h c", h=8)
                for half in range(2):
                    for h in range(8):
                        mm(ps[0][:, :], oa3[:, h, :], wa3[:, h, half * 512:(half + 1) * 512], h == 0, h == 7,
                           [oaT_.tok, wab[0].tok], [pst[0]])
                    for h in range(8):
                        mm(ps[1][:, :], ob3[:, h, :], wb3[:, h, half * 512:(half + 1) * 512], h == 0, h == 7,
                           [obT_.tok, wab[1].tok], [pst[1]])
                    tt("dve", tmpA.ap, ps[0][:, :], gt.ap[:, half * 512:(half + 1) * 512], ALU.mult, [pst[0], gt.tok], [tmpA.tok])
                    tt("dve", tmpB.ap, ps[1][:, :], gt.ap[:, 1024 + half * 512:1024 + (half + 1) * 512], ALU.mult,
                       [pst[1], gt.tok], [tmpB.tok])
                    tt("dve", mg.ap[:, half * 512:(half + 1) * 512], tmpA.ap, tmpB.ap, ALU.add, [tmpA.tok, tmpB.tok], [mg.tok])
                pb_ = ps[2][:].bitcast(BF16)
                for kc in range(8):
                    tr(pb_[:, kc * 128:(kc + 1) * 128], mg.ap[:, kc * 128:(kc + 1) * 128], idn, [mg.tok, ident.tok], [pst[2]])
                cp("act", mgT.ap, pb_, [pst[2]], [mgT.tok])
                mg3 = mgT.ap.rearrange("p (k c) -> p k c", k=8)
                for half in range(2):
                    for kc in range(8):
                        mm(ps[0][:, :], mg3[:, kc, :], wo3[:, kc, half * 512:(half + 1) * 512], kc == 0, kc == 7,
                           [mgT.tok, wab[2].tok], [pst[0]])
                    tt("dve", x2b.ap[:, half * 512:(half + 1) * 512], ps[0][:, :], x1b.ap[:, half * 512:(half + 1) * 512],
                       ALU.add, [pst[0], x1b.tok], [x2b.tok])
                dma("sp", x2s[t], x2b.ap, [x2b.tok], [dtoks["x2s"]])
                if debug and stop_after == 3:
                    dma("sp", dbg[t], x2b.ap, [x2b.tok], [dtoks["dbg"]])
                    dma("pool", dbgq[t][:, 0:1024], oaT_.ap, [oaT_.tok], [dtoks["dbg"]])
                    dma("pool", dbgq[t][:, 1024:2048], obT_.ap, [obT_.tok], [dtoks["dbg"]])
                rms_to_xT(x2b, xst.ap.rearrange("p (k c) -> p k c", k=8), xst.tok, gB, 1)
                dma("sp", xn3s[t], xst.ap, [xst.tok], [dtoks["xn3s"]])

            def run(g):
                for _ in g:
                    pass

            def interleave(ga, gb, na, nb_):
                da = db = True
                ia = ib = 0
                while da or db:
                    if da and (not db or ia * max(nb_, 1) <= ib * max(na, 1)):
                        try:
                            next(ga); ia += 1
                        except StopIteration:
                            da = False
                    elif db:
                        try:
                            next(gb); ib += 1
                        except StopIteration:
                            db = False

            mkv3 = mkv.rearrange("(b p) c -> p b c", b=1)
            ctxs = []
            k = 0
            for s in range(2):
                sk3 = skv[s].rearrange("b p c -> p b c")
                groups = [(asrc(sk3[:, 4 * q_:4 * q_ + 4, :]), dtoks["skv"], [128] * 4) for q_ in range(4)]
                groups.append((asrc(sk3[:, 16:18, :]), dtoks["skv"], [16, 32]))
                ctxs.append(make_ctx(k, 16, 32, 32 + 32 * s, groups, False, 0)); k += 1
            order = [0, 1, 2, 3] + list(range(15, 3, -1))
            for oi, t in enumerate(order):
                groups = [(asrc(mkv3), dtoks["mkv"], [16])]
                for mm_ in range(t + 1):
                    groups.append((gsrc(mm_), gtoks[mm_], [128] * 4))
                ctxs.append(make_ctx(k, t, 128, 0, groups, True, (oi + 1) % 2)); k += 1
            fw.op("dve", lambda e: e.memset(oaTs[0].ap, 0.0), [], [oaTs[0].tok])
            fw.op("dve", lambda e: e.memset(obTs[0].ap, 0.0), [], [obTs[0].tok])
            def gen_wab():
                mbf = MB.ap.bitcast(F32)
                slots = [(mbf[:, 0:4 * D], [MB.tok]), (slotB, [x2b.tok, gt.tok, x1b.tok])]
                n_ = 0
                for wb_, src in zip(wab, (wa, wb, wo)):
                    src3_ = src.rearrange("(k p) n -> p k n", p=128)
                    for hf_ in range(2):
                        ap_, tk_ = slots[n_ % 2]
                        n_ += 1
                        dma("sp", ap_.rearrange("p (k n) -> p k n", k=4), src3_[:, hf_ * 4:(hf_ + 1) * 4, :], [], tk_)
                        for q_ in range(2):
                            cp("dve", wb_.ap[:, hf_ * 4096 + q_ * 2048:hf_ * 4096 + (q_ + 1) * 2048],
                               ap_[:, q_ * 2048:(q_ + 1) * 2048], tk_, [wb_.tok])
                        yield

            run(gen_loadq(ctxs[0]))
            interleave(gen_diff(ctxs[0]), gen_wab(), 2 * ctxs[0].nblk + 8, 6)
            run(gen_idx(ctxs[0]))
            for k, c in enumerate(ctxs):
                nxt = ctxs[k + 1] if k + 1 < len(ctxs) else None
                if nxt is not None:
                    run(gen_loadq(nxt))
                    interleave(gen_topk(c), gen_diff(nxt), NBIS + 2, 2 * nxt.nblk + 8)
                else:
                    run(gen_topk(c))
                if nxt is not None:
                    interleave(gen_dsa(c), gen_idx(nxt), 2 * c.nblk + 2, 8 * (len(nxt.groups) + 1))
                else:
                    run(gen_dsa(c))
                if k >= 1:
                    post(c.t, c.oaT, c.obT)
            fw.barrier()

        if stop_after >= 3:
            phase_attn()
        if stop_after >= 4:
            for t in range(NT):
                dma("sp", xnT3[:, :, t * 128:(t + 1) * 128], xn3s[t].rearrange("p (k c) -> p k c", k=8),
                    [dtoks["xn3s"]], [xnT_tok[t]])
            ffn(w2g, w2u, w2d, x2s, dtoks["x2s"], 3, True)

        fw.barrier()
        fw.run()
    return nc


def _host_prep(inputs):
    xp = np.asarray(inputs["x_prompt"], np.float32)
    xs = np.asarray(inputs["x_sample"], np.float32)
    meta = np.asarray(inputs["meta"], np.float32)
    half_a = 32
    half_b = 64
    inv_a = (np.float32(10000.0) ** (-np.arange(half_a, dtype=np.float32) / np.float32(half_a))).astype(np.float32)
    inv_b = (np.float32(10000.0) ** (-np.arange(half_b, dtype=np.float32) / np.float32(half_b))).astype(np.float32)
    ident = np.tile(np.eye(128, dtype=np.float32), (1, 4))
    gvec = np.stack([np.asarray(inputs[k], np.float32).reshape(-1) for k in ["g_ffn1", "g_mix", "g_ffn2", "g_final"]])
    lamv = np.stack([np.asarray(inputs[k], np.float32).reshape(-1) for k in ["lam_q1", "lam_k1", "lam_q2", "lam_k2"]])
    common = {
        "gvec": gvec, "subln": np.asarray(inputs["a_subln"], np.float32).reshape(-1), "lamv": lamv,
        "w1g": np.asarray(inputs["w1_gate"], np.float32)[0], "w1u": np.asarray(inputs["w1_up"], np.float32)[0],
        "w1d": np.asarray(inputs["w1_down"], np.float32)[0],
        "w2g": np.asarray(inputs["w2_gate"], np.float32)[0], "w2u": np.asarray(inputs["w2_up"], np.float32)[0],
        "w2d": np.asarray(inputs["w2_down"], np.float32)[0],
        "win": np.asarray(inputs["w_in"], np.float32)[0],
        "wa": np.asarray(inputs["w_a"], np.float32)[0], "wb": np.asarray(inputs["w_b"], np.float32)[0],
        "wo": np.asarray(inputs["w_o"], np.float32)[0],
        "ident": ident,
    }
    maps = []
    for c in range(8):
        b, i = c // 4, c % 4
        xin = np.zeros((NT, 128, D), np.float32)
        pos = np.zeros((NT, 128), np.float32)
        for m in range(16):
            j = 4 * m + i
            xin[m] = xp[b, 128 * j:128 * (j + 1)]
            pos[m] = 16 + 128 * j + np.arange(128)
        xin[16, 0:16] = meta
        pos[16, 0:16] = np.arange(16)
        for s in range(2):
            xin[16, 32 + 32 * s:64 + 32 * s] = xs[2 * c + s]
            pos[16, 32 + 32 * s:64 + 32 * s] = 16 + 2048 + np.arange(32)
        ang_a = pos[:, :, None].astype(np.float32) * inv_a[None, None, :]
        ang_b = pos[:, :, None].astype(np.float32) * inv_b[None, None, :]
        rope = np.concatenate([np.cos(ang_a), np.sin(ang_a), np.cos(ang_b), np.sin(ang_b)], -1).astype(np.float32)
        cm = np.zeros((4, 128, 128), np.float32)
        for r in range(4):
            if r > i:
                cm[r] = NEG
            elif r == i:
                cm[r, 0:64, 64:128] = NEG
        d = dict(common)
        d.update({
            "xin": xin, "rope": rope, "cmb": cm,
            "cak": np.asarray(inputs["cache_a_k"], np.float32)[0, 2 * c:2 * c + 2].reshape(2, 2064, D),
            "cav": np.asarray(inputs["cache_a_v"], np.float32)[0, 2 * c:2 * c + 2].reshape(2, 2064, D),
            "cbk": np.asarray(inputs["cache_b_k"], np.float32)[0, 2 * c:2 * c + 2].reshape(2, 2064, D),
            "cbv": np.asarray(inputs["cache_b_v"], np.float32)[0, 2 * c:2 * c + 2].reshape(2, 2064, D),
            "cki": np.asarray(inputs["cache_b_kidx"], np.float32)[0, 2 * c:2 * c + 2],
        })
        maps.append(d)
    return maps


def _assemble(results):
    y_p = np.zeros((2, 8192, D), np.float32)
    y_s = np.zeros((16, 32, D), np.float32)
    kp = [np.zeros((1, 2, 8208, 1024), np.float32) for _ in range(4)] + [np.zeros((1, 2, 8208, 64), np.float32)]
    ks = [np.zeros((1, 16, 32, 1024), np.float32) for _ in range(4)] + [np.zeros((1, 16, 32, 64), np.float32)]
    offs = [(0, 1024), (1024, 2048), (2048, 3072), (3072, 4096), (4096, 4160)]
    for c in range(8):
        b, i = c // 4, c % 4
        r = results[c]
        yy, kk = r["y"], r["kout"]
        for m in range(16):
            j = 4 * m + i
            y_p[b, 128 * j:128 * (j + 1)] = yy[m]
            for q, (a0, a1) in enumerate(offs):
                kp[q][0, b, 16 + 128 * j:16 + 128 * (j + 1)] = kk[m][:, a0:a1]
        if i == 0:
            for q, (a0, a1) in enumerate(offs):
                kp[q][0, b, 0:16] = kk[16][0:16, a0:a1]
        for s in range(2):
            y_s[2 * c + s] = yy[16][32 + 32 * s:64 + 32 * s]
            for q, (a0, a1) in enumerate(offs):
                ks[q][0, 2 * c + s] = kk[16][32 + 32 * s:64 + 32 * s, a0:a1]
    out = [y_p, y_s]
    out += [kp[0].reshape(1, 2, 8208, 8, 128), kp[1].reshape(1, 2, 8208, 8, 128),
            kp[2].reshape(1, 2, 8208, 8, 128), kp[3].reshape(1, 2, 8208, 8, 128), kp[4]]
    out += [ks[0].reshape(1, 16, 32, 8, 128), ks[1].reshape(1, 16, 32, 8, 128),
            ks[2].reshape(1, 16, 32, 8, 128), ks[3].reshape(1, 16, 32, 8, 128), ks[4]]
    return tuple(out)


_NC_CACHE = {}


def kernel(**inputs):
    maps = _host_prep(inputs)
    if "nc" not in _NC_CACHE:
        _NC_CACHE["nc"] = build_program()
    res = run_bass_kernel_spmd(_NC_CACHE["nc"], maps, core_ids=list(range(8)))
    return _assemble(res.results)
```

```python
import os
import numpy as np
import ml_dtypes
import concourse.bass as bass
import concourse.mybir as mybir
from concourse.bass_utils import run_bass_kernel_spmd
from contextlib import ExitStack

F32 = mybir.dt.float32
BF16 = mybir.dt.bfloat16
AF = mybir.ActivationFunctionType
ALU = mybir.AluOpType
AX = mybir.AxisListType

D = 1024
DFF = 2816
NFC = DFF // 128
NT = 17
TT = NT * 128
EPS = 1e-6
NCOL = 8776
KW = 4160
RW = 4256
C_AKT, C_BKT, C_IKT, C_AV, C_BV = 0, 1024, 2048, 2176, 3216
WA_, WB_ = 2176, 2080
QW = 3584
GW = 2056
NEG = -30000.0
NBIS = 16


class Tok:
    __slots__ = ("w", "r")

    def __init__(self):
        self.w = {}
        self.r = {}


class FW:
    NDS = 32

    def __init__(self, nc, es):
        self.nc = nc
        self.names = ["pe", "act", "dve", "pool", "sp"]
        self.sem = {n: es.enter_context(nc.semaphore("s_" + n)) for n in self.names}
        self.dsem = [es.enter_context(nc.semaphore("d%d" % i)) for i in range(self.NDS)]
        self.cnt = {n: 0 for n in self.names}
        self.prog = {n: [] for n in self.names}
        self.waited = {n: {} for n in self.names}
        self.ndma = 0

    def _need(self, eng, ev):
        if ev is None:
            return
        if ev[0] == "e":
            if ev[1] == eng and eng in ("pe", "sp"):
                return
            key, val = ("e", ev[1]), ev[2]
        else:
            n = ev[1]
            key, val = ("d", n % self.NDS), 16 * (n // self.NDS + 1)
        w = self.waited[eng]
        if w.get(key, 0) >= val:
            return
        w[key] = val
        self.prog[eng].append(("w", key, val))

    def op(self, eng, fn, reads=(), writes=(), dma=False, appends=()):
        for t in reads:
            for ev in t.w.values():
                self._need(eng, ev)
        for t in writes:
            for ev in t.w.values():
                self._need(eng, ev)
            for ev in t.r.values():
                self._need(eng, ev)
        for t in appends:
            for ev in t.r.values():
                self._need(eng, ev)
        if dma:
            n = self.ndma
            self.ndma += 1
            if n >= self.NDS:
                self._need(eng, ("d", n - self.NDS))
            ev = ("d", n)
            self.prog[eng].append(("dma", fn, n))
            rkey = ("d", n % self.NDS)
        else:
            self.cnt[eng] += 1
            ev = ("e", eng, self.cnt[eng])
            self.prog[eng].append(("op", fn))
            rkey = eng
        for t in reads:
            t.r[rkey] = ev
        for t in writes:
            t.w = {rkey: ev}
            t.r = {}
        for t in appends:
            t.w[rkey] = ev
        return ev

    def barrier(self, nowait_on=()):
        for e in self.names:
            for f in self.names:
                if f in nowait_on:
                    continue
                if f != e and self.cnt[f] > 0:
                    self._need(e, ("e", f, self.cnt[f]))
            for n in range(max(0, self.ndma - self.NDS), self.ndma):
                self._need(e, ("d", n))

    def replay(self, name, e):
        for it in self.prog[name]:
            if it[0] == "w":
                key, val = it[1], it[2]
                s = self.sem[key[1]] if key[0] == "e" else self.dsem[key[1]]
                e.wait_ge(s, val)
            elif it[0] == "op":
                it[1](e).then_inc(self.sem[name], 1)
            else:
                n = it[2]
                it[1](e).then_inc(self.dsem[n % self.NDS], 16)

    def run(self):
        with self.nc.Block() as block:
            @block.tensor
            def _(e):
                self.replay("pe", e)

            @block.scalar
            def _(e):
                self.replay("act", e)

            @block.vector
            def _(e):
                self.replay("dve", e)

            @block.gpsimd
            def _(e):
                self.replay("pool", e)

            @block.sync
            def _(e):
                self.replay("sp", e)


class Buf:
    __slots__ = ("ap", "tok")

    def __init__(self, ap):
        self.ap = ap
        self.tok = Tok()


class Arena:
    def __init__(self, ap):
        self.ap = ap
        self.off = 0
        self.n = ap.shape[1]

    def f32(self, n):
        a = self.ap[:, self.off:self.off + n]
        self.off += n
        assert self.off <= self.n, ("arena overflow", self.off, self.n)
        return a

    def bf16(self, n):
        assert n % 2 == 0
        return self.f32(n // 2).bitcast(BF16)


def build_program(stop_after=99, debug=False):
    nc = bass.Bass("TRN2", target_bir_lowering=False)

    def din(name, shape, dt=F32):
        return nc.dram_tensor(name, list(shape), dt, kind="ExternalInput").ap()

    def dout(name, shape, dt=F32):
        return nc.dram_tensor(name, list(shape), dt, kind="ExternalOutput").ap()

    def dint(name, shape, dt):
        return nc.dram_tensor(name, list(shape), dt).ap()

    xin = din("xin", [NT, 128, D])
    rope = din("rope", [NT, 128, 192])
    gvec = din("gvec", [4, D])
    subln = din("subln", [128])
    lamv = din("lamv", [4, 64])
    w1g = din("w1g", [D, DFF]); w1u = din("w1u", [D, DFF]); w1d = din("w1d", [DFF, D])
    w2g = din("w2g", [D, DFF]); w2u = din("w2u", [D, DFF]); w2d = din("w2d", [DFF, D])
    win = din("win", [D, NCOL])
    wa = din("wa", [D, D]); wb = din("wb", [D, D]); wo = din("wo", [D, D])
    cak = din("cak", [2, 2064, D]); cav = din("cav", [2, 2064, D])
    cbk = din("cbk", [2, 2064, D]); cbv = din("cbv", [2, 2064, D])
    cki = din("cki", [2, 2064, 64])
    cmb = din("cmb", [4, 128, 128])
    identd = din("ident", [128, 512])

    y = dout("y", [NT, 128, D])
    kout = dout("kout", [NT, 128, KW])
    dbg = dout("dbg", [NT, 128, D]) if debug else None
    dbgq = dout("dbgq", [NT, 128, QW]) if debug else None
    dbgg = dout("dbgg", [NT, 128, GW]) if debug else None

    x1s = dint("x1s", [NT, 128, D], F32)
    x2s = dint("x2s", [NT, 128, D], F32)
    xn3s = dint("xn3s", [NT, 128, D], BF16)
    qs = dint("qs", [NT, 128, QW], BF16)
    gs = dint("gs", [NT, 128, GW], F32)
    shardA = dint("shardA", [16 * 128, WA_], BF16)
    gathA = dint("gathA", [16 * 4 * 128, WA_], BF16)
    shardB = dint("shardB", [16 * 128, WB_], BF16)
    gathB = dint("gathB", [16 * 4 * 128, WB_], BF16)
    mkv = dint("mkv", [128, RW], BF16)
    skv = dint("skv", [2, 18, 128, RW], BF16)

    with ExitStack() as es:
        fw = FW(nc, es)
        arena_t = es.enter_context(nc.sbuf_tensor("arena", [128, 51200], F32))
        small_t = es.enter_context(nc.sbuf_tensor("small", [128, 1536], F32))
        ps = [es.enter_context(nc.psum_tensor("ps%d" % i, [128, 512], F32)) for i in range(8)]
        pst = [Tok() for _ in range(8)]
        A = Arena(arena_t[:])
        S = Arena(small_t[:])

        def mm(out, lhsT, rhs, start, stop, r, w):
            fw.op("pe", lambda e: e.matmul(out, lhsT=lhsT, rhs=rhs, start=start, stop=stop), r, w)

        def tr(out, in_, ident_ap, r, w):
            fw.op("pe", lambda e: e.transpose(out=out, in_=in_, identity=ident_ap), r, w)

        def act(out, in_, func, r, w, **kw):
            fw.op("act", lambda e: e.activation(out=out, in_=in_, func=func, **kw), r, w)

        def dma(eng, out, in_, r, w):
            dset = set(id(v) for v in dtoks.values()) if dtoks else set()
            wx = [t for t in w if id(t) not in dset]
            ax = [t for t in w if id(t) in dset]
            fw.op(eng, lambda e: e.dma_start(out=out, in_=in_), r, wx, dma=True, appends=ax)

        def tt(eng, out, in0, in1, op, r, w):
            fw.op(eng, lambda e: e.tensor_tensor(out=out, in0=in0, in1=in1, op=op), r, w)

        def ts(eng, out, in0, s1, s2, op0, op1, r, w, accum_out=None):
            if op1 is None:
                fw.op(eng, lambda e: e.tensor_scalar(out=out, in0=in0, scalar1=s1, scalar2=None, op0=op0), r, w)
            elif accum_out is None:
                fw.op(eng, lambda e: e.tensor_scalar(out=out, in0=in0, scalar1=s1, scalar2=s2, op0=op0, op1=op1), r, w)
            else:
                fw.op(eng, lambda e: e.tensor_scalar(out=out, in0=in0, scalar1=s1, scalar2=s2, op0=op0, op1=op1,
                                                     accum_out=accum_out), r, w)

        def stt(eng, out, in0, scalar, in1, op0, op1, r, w):
            fw.op(eng, lambda e: e.scalar_tensor_tensor(out=out, in0=in0, scalar=scalar, in1=in1, op0=op0, op1=op1), r, w)

        def cp(eng, out, in_, r, w):
            if eng == "act":
                fw.op("act", lambda e: e.copy(out=out, in_=in_), r, w)
            else:
                fw.op(eng, lambda e: e.tensor_copy(out=out, in_=in_), r, w)

        dtoks = {}
        gtoks = [Tok() for _ in range(16)]
        ident = Buf(S.bf16(512))
        dma("pool", ident.ap, identd, [], [ident.tok])
        idn = ident.ap[:, 0:128]
        gB = Buf(A.f32(D))
        stat = [Buf(S.f32(1)) for _ in range(8)]
        junk = Buf(A.f32(D))
        dtoks.update({k: Tok() for k in ["x1s", "x2s", "xn3s", "qs", "gs", "shard", "gath", "mkv", "skv", "y", "kout", "dbg"]})
        base_mark = A.off

        epsb = Buf(S.f32(1))
        fw.op("dve", lambda e: e.memset(epsb.ap, EPS), [], [epsb.tok])

        def rstd(rs, ss, n):
            p = rs.ap.shape[0]
            act(rs.ap, ss.ap, AF.Sqrt, [ss.tok, epsb.tok], [rs.tok], scale=1.0 / n, bias=epsb.ap[0:p, :])
            fw.op("dve", lambda e: e.reciprocal(out=rs.ap, in_=rs.ap), [rs.tok], [rs.tok])

        def rms_to_xT(xb, dst3, dst_tok, gbuf, bank, defer=None):
            ss, rs = stat[0], stat[1]
            act(junk.ap, xb.ap, AF.Square, [xb.tok], [junk.tok, ss.tok], accum_out=ss.ap)
            rstd(rs, ss, D)
            stt("dve", xn.ap, xb.ap, rs.ap, gbuf.ap, ALU.mult, ALU.mult, [xb.tok, rs.tok, gbuf.tok], [xn.tok])
            def pe_part():
                pb = ps[bank][:].bitcast(BF16)
                for kc in range(8):
                    tr(pb[:, kc * 128:(kc + 1) * 128], xn.ap[:, kc * 128:(kc + 1) * 128], idn,
                       [xn.tok, ident.tok], [pst[bank]])
                cp("act", dst3, pb.rearrange("p (k c) -> p k c", k=8), [pst[bank]], [dst_tok])
            if defer is None:
                pe_part()
            else:
                defer.append(pe_part)

        xn = Buf(A.bf16(D))
        attn_mark = A.off
        xnT = A.bf16(8 * TT)
        xnT3 = xnT.rearrange("p (k t) -> p k t", k=8)
        xnT_tok = [Tok() for _ in range(NT)]
        ffn_mark = A.off

        TG = [(0, 512), (512, 512), (1024, 512), (1536, 512), (2048, 128)]

        def ffn(wg, wu, wd, xsrc, xsrc_tok, g_next_idx, final):
            A.off = ffn_mark
            actT = A.bf16(NFC * TT)
            actT3 = actT.rearrange("p (c t) -> p c t", c=NFC)
            act_tok = [[Tok() for _ in range(len(TG))] for _ in range(NFC)]
            region = A.off
            wgb = [Buf(A.bf16(8 * 256)) for _ in range(2)]
            wub = [Buf(A.bf16(8 * 256)) for _ in range(2)]
            sil = [Buf(A.f32(512)) for _ in range(2)]
            NWA = 21
            wdA = Buf(A.bf16(NWA * D))
            wdA3 = wdA.ap.rearrange("p (c n) -> p c n", c=NWA)
            wd_src = wd.rearrange("(c p) n -> p c n", p=128)
            it = 0
            for fg in range(DFF // 256):
                wgt, wut = wgb[fg % 2], wub[fg % 2]
                dma("pool", wgt.ap.rearrange("p (k f) -> p k f", k=8),
                    wg[:, fg * 256:(fg + 1) * 256].rearrange("(k p) f -> p k f", p=128), [], [wgt.tok])
                dma("pool", wut.ap.rearrange("p (k f) -> p k f", k=8),
                    wu[:, fg * 256:(fg + 1) * 256].rearrange("(k p) f -> p k f", p=128), [], [wut.tok])
                wg3 = wgt.ap.rearrange("p (k f) -> p k f", k=8)
                wu3 = wut.ap.rearrange("p (k f) -> p k f", k=8)
                if fg == 1:
                    for c0_ in range(0, NWA, 7):
                        fw.op("pool", lambda e, o=wdA3[:, c0_:c0_ + 7, :], i_=wd_src[:, c0_:c0_ + 7, :]: e.dma_start(out=o, in_=i_),
                              [], [], dma=True, appends=[wdA.tok])
                for sub in range(2):
                    fc = fg * 2 + sub
                    for gi, (t0, tn) in enumerate(TG):
                        bg, bu = (it % 2) * 2, (it % 2) * 2 + 1
                        it += 1
                        rtoks = [xnT_tok[tt_] for tt_ in range(t0 // 128, (t0 + tn) // 128)]
                        for kc in range(8):
                            mm(ps[bg][:, 0:tn], wg3[:, kc, sub * 128:(sub + 1) * 128], xnT3[:, kc, t0:t0 + tn],
                               kc == 0, kc == 7, [wgt.tok] + rtoks, [pst[bg]])
                        for kc in range(8):
                            mm(ps[bu][:, 0:tn], wu3[:, kc, sub * 128:(sub + 1) * 128], xnT3[:, kc, t0:t0 + tn],
                               kc == 0, kc == 7, [wut.tok] + rtoks, [pst[bu]])
                        sb = sil[it % 2]
                        act(sb.ap[:, 0:tn], ps[bg][:, 0:tn], AF.Silu, [pst[bg]], [sb.tok])
                        tt("dve", actT3[:, fc, t0:t0 + tn], sb.ap[:, 0:tn], ps[bu][:, 0:tn], ALU.mult,
                           [sb.tok, pst[bu]], [act_tok[fc][gi]])
            fw.barrier()
            A.off = region
            wdB = Buf(A.bf16((NFC - NWA) * D))
            wdB3 = wdB.ap.rearrange("p (c n) -> p c n", c=NFC - NWA)
            dma("pool", wdB3, wd_src[:, NWA:NFC, :], [], [wdB.tok])

            def wdc(c, half):
                if c < NWA:
                    return wdA3[:, c, half * 512:(half + 1) * 512], wdA.tok
                return wdB3[:, c - NWA, half * 512:(half + 1) * 512], wdB.tok
            xb2 = [Buf(A.f32(D)) for _ in range(2)]
            xr = [Buf(A.f32(D)) for _ in range(2)]
            dma("sp", gB.ap, gvec[g_next_idx].partition_broadcast(128), [], [gB.tok])
            fpend = []
            for t in range(NT):
                xb = xb2[t % 2]
                dma("sp", xb.ap, xsrc[t], [xsrc_tok], [xb.tok])
                b0 = 4 + (t % 2) * 2
                for half in range(2):
                    for c in range(NFC):
                        wap, wtk = wdc(c, half)
                        mm(ps[b0 + half][:], actT3[:, c, t * 128:(t + 1) * 128], wap,
                           c == 0, c == NFC - 1, [wtk, act_tok[c][min(t // 4, 4)]], [pst[b0 + half]])
                while fpend:
                    fpend.pop(0)()
                x1 = xr[t % 2]
                for half in range(2):
                    stt("dve", x1.ap[:, half * 512:(half + 1) * 512], ps[b0 + half][:], 0.5,
                        xb.ap[:, half * 512:(half + 1) * 512], ALU.mult, ALU.add,
                        [pst[b0 + half], xb.tok], [x1.tok])
                if not final:
                    dma("sp", x1s[t], x1.ap, [x1.tok], [dtoks["x1s"]])
                    if debug and stop_after == 1:
                        dma("sp", dbg[t], x1.ap, [x1.tok], [dtoks["dbg"]])
                    rms_to_xT(x1, xnT3[:, :, t * 128:(t + 1) * 128], xnT_tok[t], gB, 0 + (t % 2), defer=fpend)
                else:
                    ss, rs = stat[0], stat[1]
                    act(junk.ap, x1.ap, AF.Square, [x1.tok], [junk.tok, ss.tok], accum_out=ss.ap)
                    rstd(rs, ss, D)
                    stt("dve", xb.ap, x1.ap, rs.ap, gB.ap, ALU.mult, ALU.mult, [x1.tok, rs.tok, gB.tok], [xb.tok])
                    dma("sp", y[t], xb.ap, [xb.tok], [dtoks["y"]])
            while fpend:
                fpend.pop(0)()
            fw.barrier()

        dma("sp", gB.ap, gvec[0].partition_broadcast(128), [], [gB.tok])
        A.off = ffn_mark
        xin_b = [Buf(A.f32(D)) for _ in range(2)]
        for t in range(NT):
            xb = xin_b[t % 2]
            dma("sp", xb.ap, xin[t], [], [xb.tok])
            rms_to_xT(xb, xnT3[:, :, t * 128:(t + 1) * 128], xnT_tok[t], gB, t % 2)
        fw.barrier()
        ffn(w1g, w1u, w1d, xin, Tok(), 1, False)
        if stop_after >= 2:
            def phase_win():
                A.off = ffn_mark
                ropet = Buf(A.f32(NT * 192))
                dma("sp", ropet.ap.rearrange("p (t c) -> p t c", t=NT), rope.rearrange("t p c -> p t c"), [], [ropet.tok])
                wbuf = [Buf(A.bf16(8 * 512)) for _ in range(2)]
                wstg = [Buf(A.f32(8 * 512)) for _ in range(2)]
                zf = [Buf(A.f32(512)) for _ in range(4)]
                rf = [Buf(A.f32(512)) for _ in range(4)]
                rb = [Buf(A.bf16(512)) for _ in range(4)]
                tmpd = [Buf(A.f32(256)) for _ in range(4)]
                tmpd2 = [Buf(A.f32(256)) for _ in range(4)]
                tmpp = [Buf(A.f32(256)) for _ in range(4)]
                tmpp2 = [Buf(A.f32(256)) for _ in range(4)]
                stg = [Buf(A.bf16(512)) for _ in range(4)]
                stgq = [Buf(A.bf16(1024)) for _ in range(4)]
                stgv = [Buf(A.bf16(520)) for _ in range(4)]
                gst = [Buf(A.f32(8)) for _ in range(4)]
                for b_ in stgq:
                    fw.op("dve", lambda e, a=b_.ap: e.memset(a, 0.0), [], [b_.tok])
                for b_ in stgv:
                    fw.op("dve", lambda e, a=b_.ap: e.memset(a, 0.0), [], [b_.tok])
                    v3 = b_.ap.rearrange("p (h c) -> p h c", h=4)
                    fw.op("dve", lambda e, a=v3[:, :, 128:129]: e.memset(a, 1.0), [], [b_.tok])
                groups = []
                for hh in range(2):
                    groups.append((1024 + hh * 512, 512, "ak", hh))
                for hh in range(2):
                    groups.append((2048 + hh * 512, 512, "av", hh))
                for hh in range(2):
                    groups.append((4096 + hh * 512, 512, "bk", hh))
                for hh in range(2):
                    groups.append((5120 + hh * 512, 512, "bv", hh))
                groups.append((6656, 72, "ikw", 0))
                NKG = len(groups)
                for hh in range(2):
                    groups.append((hh * 512, 512, "aq", hh))
                for hh in range(2):
                    groups.append((3072 + hh * 512, 512, "bq", hh))
                groups.append((6144, 512, "iq", 0))
                for j in range(4):
                    groups.append((6728 + j * 512, 512, "g", j))

                def gen_coll():
                    for m_ in range(16):
                        for sh_, ga_ in ((shardA, gathA), (shardB, gathB)):
                            fw.op("pool", lambda e, i_=sh_[m_ * 128:(m_ + 1) * 128, :], o_=ga_[m_ * 512:(m_ + 1) * 512, :]:
                                  e.collective_compute("AllGather", ALU.bypass, replica_groups=[[0, 1, 2, 3], [4, 5, 6, 7]],
                                                       ins=[i_], outs=[o_]),
                                  [dtoks["shard"]], [gtoks[m_]])
                            yield
                g_coll = gen_coll()
                coll_n = [0]
                cnt = [0]
                pend = []

                def rope_piece(src, dst, nv, half, cos, sin, k):
                    s4 = src.ap.rearrange("p (v two h) -> p v two h", two=2, h=half)
                    d4 = dst.ap[:, 0:nv * 2 * half].rearrange("p (v two h) -> p v two h", two=2, h=half)
                    cb = cos.unsqueeze(1).to_broadcast([128, nv, half])
                    sb_ = sin.unsqueeze(1).to_broadcast([128, nv, half])
                    x1, x2 = s4[:, :, 0, :], s4[:, :, 1, :]
                    ta, ta2, tb, tb2 = tmpd[k], tmpd2[k], tmpp[k], tmpp2[k]
                    dst.tok.w = {}
                    v3_ = lambda b_: b_.ap[:, 0:nv * half].rearrange("p (v h) -> p v h", v=nv)
                    tt("dve", v3_(ta), x1, cb, ALU.mult, [src.tok, ropet.tok], [ta.tok])
                    tt("dve", v3_(ta2), x2, sb_, ALU.mult, [src.tok, ropet.tok], [ta2.tok])
                    fw.op("dve", lambda e: e.tensor_tensor(out=d4[:, :, 0, :], in0=v3_(ta), in1=v3_(ta2), op=ALU.subtract),
                          [ta.tok, ta2.tok], [], appends=[dst.tok])
                    tt("dve", v3_(tb), x2, cb, ALU.mult, [src.tok, ropet.tok], [tb.tok])
                    tt("dve", v3_(tb2), x1, sb_, ALU.mult, [src.tok, ropet.tok], [tb2.tok])
                    fw.op("dve", lambda e: e.tensor_tensor(out=d4[:, :, 1, :], in0=v3_(tb), in1=v3_(tb2), op=ALU.add),
                          [tb.tok, tb2.tok], [], appends=[dst.tok])

                def kt_store(t, sbuf, ccol, ncols_per_head, nh):
                    s3 = sbuf.ap[:, 0:nh * 128].rearrange("p (h k) -> p h k", h=nh)
                    if t < 16:
                        dma("sp", shardA[t * 128:(t + 1) * 128, ccol:ccol + nh * 128], sbuf.ap[:, 0:nh * 128],
                            [sbuf.tok], [dtoks["shard"]])
                    else:
                        dma("sp", mkv[:, ccol:ccol + nh * 128].rearrange("p (h k) -> p h k", h=nh)[:, :, 0:16],
                            s3[:, :, 0:16], [sbuf.tok], [dtoks["mkv"]])
                        for s in range(2):
                            dma("sp", skv[s, 17, :, ccol:ccol + nh * 128].rearrange("p (h k) -> p h k", h=nh)[:, :, 0:32],
                                s3[:, :, 32 + 32 * s:64 + 32 * s], [sbuf.tok], [dtoks["skv"]])

                def v_store(t, sbuf, ccol):
                    if t < 16:
                        dma("sp", shardB[t * 128:(t + 1) * 128, ccol - WA_:ccol - WA_ + 520], sbuf.ap, [sbuf.tok], [dtoks["shard"]])
                    else:
                        dma("sp", mkv[0:16, ccol:ccol + 520], sbuf.ap[0:16, :], [sbuf.tok], [dtoks["mkv"]])
                        for s in range(2):
                            dma("sp", skv[s, 17, 0:32, ccol:ccol + 520], sbuf.ap[32 + 32 * s:64 + 32 * s, :],
                                [sbuf.tok], [dtoks["skv"]])

                for gi, (c0, wd_, kind, idx) in enumerate(groups):
                    wt = wbuf[gi % 2]
                    w3 = wt.ap.rearrange("p (k f) -> p k f", k=8)
                    if gi + 1 < len(groups) and gi + 1 >= NKG:
                        c0n, wdn = groups[gi + 1][0], groups[gi + 1][1]
                        wsn = wstg[(gi + 1) % 2]
                        dma("sp", wsn.ap.rearrange("p (k f) -> p k f", k=8)[:, :, 0:wdn],
                            win[:, c0n:c0n + wdn].rearrange("(k p) f -> p k f", p=128), [], [wsn.tok])
                    if gi < NKG:
                        dma("pool", w3[:, :, 0:wd_], win[:, c0:c0 + wd_].rearrange("(k p) f -> p k f", p=128), [], [wt.tok])
                    else:
                        ws_ = wstg[gi % 2]
                        ws3 = ws_.ap.rearrange("p (k f) -> p k f", k=8)
                        for kc2 in range(0, 8, 2):
                            cp("dve", w3[:, kc2:kc2 + 2, 0:wd_], ws3[:, kc2:kc2 + 2, 0:wd_], [ws_.tok], [wt.tok])
                    for t in range(NT):
                        k = cnt[0] % 4
                        pb, tb_ = cnt[0] % 2, 2 + cnt[0] % 2
                        cnt[0] += 1
                        for kc in range(8):
                            mm(ps[pb][:, 0:wd_], xnT3[:, kc, t * 128:(t + 1) * 128], w3[:, kc, 0:wd_], kc == 0, kc == 7,
                               [wt.tok, xnT_tok[t]], [pst[pb]])
                        while len(pend) > 1:
                            pend.pop(0)()
                        cA = ropet.ap[:, t * 192:t * 192 + 32]
                        sA = ropet.ap[:, t * 192 + 32:t * 192 + 64]
                        cB = ropet.ap[:, t * 192 + 64:t * 192 + 128]
                        sB = ropet.ap[:, t * 192 + 128:t * 192 + 192]
                        ptb = ps[tb_][:].bitcast(BF16)
                        if kind in ("aq", "ak", "bq", "bk", "iq"):
                            cp("act", zf[k].ap, ps[pb][:], [pst[pb]], [zf[k].tok])
                            isb = kind in ("bq", "bk")
                            nv, half = (4, 64) if isb else (8, 32)
                            cos, sin = (cB, sB) if isb else (cA, sA)
                            if kind in ("ak", "bk"):
                                rope_piece(zf[k], rf[k], nv, half, cos, sin, k)
                                kc0 = (0 if kind == "ak" else 2048) + idx * 512
                                dma("sp", kout[t][:, kc0:kc0 + 512], rf[k].ap, [rf[k].tok], [dtoks["kout"]])
                                cp("act", rb[k].ap, rf[k].ap, [rf[k].tok], [rb[k].tok])
                            else:
                                rope_piece(zf[k], rb[k], nv, half, cos, sin, k)
                            def fin(k=k, t=t, kind=kind, idx=idx, tb_=tb_, ptb=ptb):
                              for h in range(4):
                                tr(ptb[:, h * 128:(h + 1) * 128], rb[k].ap[:, h * 128:(h + 1) * 128], idn,
                                   [rb[k].tok, ident.tok], [pst[tb_]])
                              if kind == "aq":
                                q4 = stgq[k].ap.rearrange("p (h m c) -> p h m c", h=4, m=2)
                                p3 = ptb[:, 0:512].rearrange("p (h c) -> p h c", h=4)
                                cp("act", q4[0:64, :, 0, :], p3[0:64, :, :], [pst[tb_]], [stgq[k].tok])
                                cp("act", q4[64:128, :, 1, :], p3[64:128, :, :], [pst[tb_]], [stgq[k].tok])
                                dma("sp", qs[t][:, idx * 1024:(idx + 1) * 1024], stgq[k].ap, [stgq[k].tok], [dtoks["qs"]])
                              else:
                                cp("act", stg[k].ap, ptb[:, 0:512], [pst[tb_]], [stg[k].tok])
                                if kind == "bq":
                                    dma("sp", qs[t][:, 2048 + idx * 512:2048 + (idx + 1) * 512], stg[k].ap,
                                        [stg[k].tok], [dtoks["qs"]])
                                elif kind == "iq":
                                    dma("sp", qs[t][:, 3072:3584], stg[k].ap, [stg[k].tok], [dtoks["qs"]])
                                else:
                                    kt_store(t, stg[k], (C_AKT if kind == "ak" else C_BKT) + idx * 512, 128, 4)
                            pend.append(fin)
                        elif kind in ("av", "bv"):
                            cp("act", rf[k].ap, ps[pb][:], [pst[pb]], [rf[k].tok])
                            kc0 = (1024 if kind == "av" else 3072) + idx * 512
                            dma("sp", kout[t][:, kc0:kc0 + 512], rf[k].ap, [rf[k].tok], [dtoks["kout"]])
                            v3 = stgv[k].ap.rearrange("p (h c) -> p h c", h=4)
                            cp("dve", v3[:, :, 0:128], rf[k].ap.rearrange("p (h c) -> p h c", h=4), [rf[k].tok], [stgv[k].tok])
                            v_store(t, stgv[k], (C_AV if kind == "av" else C_BV) + idx * 520)
                        elif kind == "ikw":
                            cp("act", zf[k].ap[:, 0:72], ps[pb][:, 0:72], [pst[pb]], [zf[k].tok])
                            zsub = Buf(zf[k].ap[:, 0:64]); zsub.tok = zf[k].tok
                            rsub = Buf(rf[k].ap[:, 0:64]); rsub.tok = rf[k].tok
                            rope_piece(zsub, rsub, 1, 32, cA, sA, k)
                            dma("sp", kout[t][:, 4096:4160], rf[k].ap[:, 0:64], [rf[k].tok], [dtoks["kout"]])
                            cp("act", rb[k].ap[:, 0:64], rf[k].ap[:, 0:64], [rf[k].tok], [rb[k].tok])
                            cp("act", rb[k].ap[:, 64:128], rf[k].ap[:, 0:64], [rf[k].tok], [rb[k].tok])
                            def fin2(k=k, t=t, tb_=tb_, ptb=ptb):
                                tr(ptb[:, 0:128], rb[k].ap[:, 0:128], idn, [rb[k].tok, ident.tok], [pst[tb_]])
                                cp("act", stg[k].ap[:, 0:128], ptb[:, 0:128], [pst[tb_]], [stg[k].tok])
                                kt_store(t, stg[k], C_IKT, 128, 1)
                            pend.append(fin2)
                            cp("dve", gst[k].ap, zf[k].ap[:, 64:72], [zf[k].tok], [gst[k].tok])
                            dma("sp", gs[t][:, 2048:2056], gst[k].ap, [gst[k].tok], [dtoks["gs"]])
                        else:
                            act(rf[k].ap, ps[pb][:], AF.Sigmoid, [pst[pb]], [rf[k].tok])
                            dma("sp", gs[t][:, idx * 512:(idx + 1) * 512], rf[k].ap, [rf[k].tok], [dtoks["gs"]])
                        if gi >= NKG and t >= 3:
                            npieces = (gi - NKG) * NT + t - 3
                            while coll_n[0] < 32 and coll_n[0] * 3 <= npieces:
                                next(g_coll)
                                coll_n[0] += 1
                        yield
                while pend:
                    pend.pop(0)()
                while coll_n[0] < 32:
                    next(g_coll)
                    coll_n[0] += 1
                yield

            def gen_sample_prep():
                cst = [Buf(A.bf16(D)) for _ in range(3)]
                stK = [Buf(A.bf16(D)) for _ in range(2)]
                stV = [Buf(A.bf16(1040)) for _ in range(3)]
                for b_ in stV:
                    fw.op("dve", lambda e, a=b_.ap: e.memset(a, 0.0), [], [b_.tok])
                    fw.op("dve", lambda e, a=b_.ap.rearrange("p (h c) -> p h c", h=8)[:, :, 128:129]: e.memset(a, 1.0),
                          [], [b_.tok])
                n = 0
                nv_ = 0
                for s in range(2):
                    for blk in range(17):
                        r0 = blk * 128
                        kn = 128 if blk < 16 else 16
                        for src, ccol in ((cak, C_AKT), (cbk, C_BKT)):
                            k = n % 2; c_ = cst[n % 3]; n += 1
                            dma("pool", c_.ap[0:kn, :], src[s, r0:r0 + kn, :], [], [c_.tok])
                            bk_ = 4 + k
                            pb_ = ps[bk_][:].bitcast(BF16)
                            for h in range(8):
                                tr(pb_[:, h * 128:h * 128 + kn], c_.ap[0:kn, h * 128:(h + 1) * 128], idn[0:kn, 0:kn],
                                   [c_.tok, ident.tok], [pst[bk_]])
                            p3 = pb_.rearrange("p (h c) -> p h c", h=8)
                            s3 = stK[k].ap.rearrange("p (h c) -> p h c", h=8)
                            cp("act", s3[:, :, 0:kn], p3[:, :, 0:kn], [pst[bk_]], [stK[k].tok])
                            dma("sp", skv[s, blk, :, ccol:ccol + 1024].rearrange("p (h c) -> p h c", h=8)[:, :, 0:kn],
                                s3[:, :, 0:kn], [stK[k].tok], [dtoks["skv"]])
                            yield
                        k = n % 2; c_ = cst[n % 3]; n += 1
                        dma("pool", c_.ap[0:kn, 0:64], cki[s, r0:r0 + kn, :], [], [c_.tok])
                        fw.op("pool", lambda e, o=c_.ap[0:kn, 64:128], i_=cki[s, r0:r0 + kn, :]: e.dma_start(out=o, in_=i_),
                              [], [], dma=True, appends=[c_.tok])
                        bk_ = 4 + k
                        pb_ = ps[bk_][:].bitcast(BF16)
                        tr(pb_[:, 0:kn], c_.ap[0:kn, 0:128], idn[0:kn, 0:kn], [c_.tok, ident.tok], [pst[bk_]])
                        cp("act", stK[k].ap[:, 0:kn], pb_[:, 0:kn], [pst[bk_]], [stK[k].tok])
                        dma("sp", skv[s, blk, :, C_IKT:C_IKT + kn], stK[k].ap[:, 0:kn], [stK[k].tok], [dtoks["skv"]])
                        yield
                        for src, ccol in ((cav, C_AV), (cbv, C_BV)):
                            sv_ = stV[nv_ % 3]; nv_ += 1
                            v3 = sv_.ap.rearrange("p (h c) -> p h c", h=8)
                            dma("pool", v3[0:kn, :, 0:128], src[s, r0:r0 + kn, :].rearrange("p (h c) -> p h c", h=8),
                                [], [sv_.tok])
                            dma("sp", skv[s, blk, 0:kn, ccol:ccol + 1040], sv_.ap[0:kn, :], [sv_.tok], [dtoks["skv"]])
                            yield

            def _interleave(ga, gb, na, nb_):
                da = db = True
                ia = ib = 0
                while da or db:
                    if da and (not db or ia * max(nb_, 1) <= ib * max(na, 1)):
                        try:
                            next(ga); ia += 1
                        except StopIteration:
                            da = False
                    elif db:
                        try:
                            next(gb); ib += 1
                        except StopIteration:
                            db = False

            g_p4 = phase_win()
            next(g_p4)
            _interleave(g_p4, gen_sample_prep(), 140, 170)
            fw.barrier(nowait_on=("pool",))
            if debug and stop_after == 2:
                for t in range(NT):
                    dma("pool", dbgq[t], qs[t], [dtoks["qs"]], [dtoks["dbg"]])
                    dma("sp", dbgg[t], gs[t], [dtoks["gs"]], [dtoks["dbg"]])

        def phase_attn():

            A.off = attn_mark
            xst = Buf(A.bf16(D))
            wab = [Buf(A.bf16(8 * D)) for _ in range(3)]
            wa3, wb3, wo3 = [w_.ap.rearrange("p (k n) -> p k n", k=8) for w_ in wab]
            prep_mark = A.off
            NKMAX = 16 + 8192
            Sb = Buf(A.f32(NKMAX))
            MB = Buf(A.bf16(NKMAX))

            kTb = [Buf(A.bf16(4 * 512)) for _ in range(4)]
            vb = [Buf(A.bf16(4 * 520)) for _ in range(4)]
            ikb = [Buf(A.bf16(4 * 128)) for _ in range(2)]
            pT = [Buf(A.bf16(1024)) for _ in range(2)]
            Rb = [Buf(A.f32(512)) for _ in range(2)]
            qrow = Buf(A.bf16(QW))
            oaT = Buf(A.bf16(1024)); obT = Buf(A.bf16(1024))
            tmpA, tmpB = Rb[0], Rb[1]
            mg = Buf(A.bf16(1024)); mgT = Buf(A.bf16(1024))
            wst_off = A.off
            x2b = Buf(A.f32(D))
            gt = Buf(A.f32(2048))
            x1b = Buf(A.f32(D))
            slotB = A.ap[:, wst_off:wst_off + 4 * D]
            cmbb = Buf(A.bf16(512)); cmbf = Buf(A.f32(512))
            sublnB = Buf(A.f32(128))
            of_ = Buf(A.f32(128)); ob16 = Buf(A.bf16(128))
            lam4 = Buf(A.f32(256)); lamp = Buf(A.f32(128))
            iwt = Buf(S.f32(8))
            htab = Buf(S.f32(32)); pw2 = Buf(S.f32(32))
            for i_ in range(NBIS):
                fw.op("dve", lambda e, a=pw2.ap[:, i_:i_ + 1], v=0.5 ** (i_ + 1): e.memset(a, v), [], [pw2.tok])
            sm = {k: Buf(S.f32(1)) for k in ["lo", "h", "mid", "cnt", "g", "w0", "mx", "r1", "r2", "e1", "e2", "nlam", "ss", "rs"]}
            dma("sp", cmbf.ap.rearrange("p (r k) -> p r k", r=4), cmb.rearrange("r q k -> q r k"), [], [cmbf.tok])
            cp("dve", cmbb.ap, cmbf.ap, [cmbf.tok], [cmbb.tok])
            dma("sp", sublnB.ap, subln.partition_broadcast(128), [], [sublnB.tok])
            ts("dve", sublnB.ap, sublnB.ap, 0.8, None, ALU.mult, None, [sublnB.tok], [sublnB.tok])
            dma("sp", lam4.ap, lamv.rearrange("a b -> (a b)").partition_broadcast(128), [], [lam4.tok])
            for j, ek in enumerate(["e1", "e2"]):
                tt("dve", lamp.ap[:, 0:64], lam4.ap[:, j * 128:j * 128 + 64], lam4.ap[:, j * 128 + 64:j * 128 + 128],
                   ALU.mult, [lam4.tok], [lamp.tok])
                fw.op("dve", lambda e, o=sm[ek].ap: e.reduce_sum(out=o, in_=lamp.ap[:, 0:64], axis=AX.X),
                      [lamp.tok], [sm[ek].tok])
                act(sm[ek].ap, sm[ek].ap, AF.Exp, [sm[ek].tok], [sm[ek].tok])
            tt("dve", sm["nlam"].ap, sm["e2"].ap, sm["e1"].ap, ALU.subtract, [sm["e1"].tok, sm["e2"].tok], [sm["nlam"].tok])
            ts("dve", sm["nlam"].ap, sm["nlam"].ap, -0.2, None, ALU.add, None, [sm["nlam"].tok], [sm["nlam"].tok])
            dma("sp", gB.ap, gvec[2].partition_broadcast(128), [], [gB.tok])

            gathA4 = gathA.rearrange("(m r p) c -> p r m c", r=4, m=16)
            gathB4 = gathB.rearrange("(m r p) c -> p r m c", r=4, m=16)

            def gsrc(mm_):
                def f(c0, wdt):
                    if c0 < WA_:
                        return gathA4[:, :, mm_, c0:c0 + wdt]
                    return gathB4[:, :, mm_, c0 - WA_:c0 - WA_ + wdt]
                return f

            def asrc(ap3):
                return lambda c0, wdt: ap3[:, :, c0:c0 + wdt]
            I4 = ident.ap
            print("attn arena used", A.off, "of", A.n)


            class Ctx:
                pass

            qrows = [qrow, Buf(A.bf16(QW))]
            iwts = [iwt, Buf(S.f32(8))]
            oaTs = [oaT, Buf(A.bf16(1024))]
            obTs = [obT, Buf(A.bf16(1024))]
            ldc = {"kT": 0, "v": 0, "ik": 0}

            def load(kind, bufs, src3, c0, wdt, nb, dtok):
                b_ = bufs[ldc[kind] % len(bufs)]
                ldc[kind] += 1
                dma("sp", b_.ap[:, 0:nb * wdt].rearrange("p (b c) -> p b c", b=nb), src3(c0, wdt), [dtok], [b_.tok])
                return b_

            def make_ctx(k, t, nq, qoff, groups, causal, pair):
                c = Ctx()
                c.t, c.nq, c.qoff, c.groups, c.causal = t, nq, qoff, groups, causal
                c.qrow, c.iwt = qrows[k % 2], iwts[k % 2]
                c.oaT, c.obT = oaTs[pair], obTs[pair]
                c.nblk = sum(len(g_[2]) for g_ in groups)
                c.NK = sum(sum(g_[2]) for g_ in groups)
                return c

            def gen_loadq(c):
                dma("sp", c.qrow.ap, qs[c.t], [dtoks["qs"]], [c.qrow.tok])
                dma("sp", c.iwt.ap[0:c.nq, :], gs[c.t][c.qoff:c.qoff + c.nq, 2048:2056], [dtoks["gs"]], [c.iwt.tok])
                yield

            def gen_diff(c):
                nq, qoff = c.nq, c.qoff
                aq4 = c.qrow.ap[:, 0:2048].rearrange("p (h m c) -> p h m c", h=8, m=2)
                for g in range(2):
                    def acc(hl, mp):
                        i_ = hl * 2 + mp
                        return i_ // 3, (i_ % 3) * 160
                    bi_all = 0
                    it = 0
                    pending = None
                    for gi_, (src3, dtok, kns) in enumerate(c.groups):
                        nb = len(kns)
                        kt_ = load("kT", kTb, src3, C_AKT + 512 * g, 512, nb, dtok)
                        v_ = load("v", vb, src3, C_AV + 520 * g, 520, nb, dtok)
                        kt3 = kt_.ap[:, 0:nb * 512].rearrange("p (b c) -> p b c", b=nb)
                        v3 = v_.ap[:, 0:nb * 520].rearrange("p (b c) -> p b c", b=nb)
                        last_g = c.causal and gi_ == len(c.groups) - 1
                        for bi, kn in enumerate(kns):
                            s0 = 3 + 2 * (it % 2)
                            p_ = pT[it % 2]
                            it += 1
                            for hl in range(4):
                                bank = s0 + hl // 2
                                o3 = ps[bank][0:kn, (hl % 2) * 2 * nq:(hl % 2 + 1) * 2 * nq].rearrange("p (a b) -> p a b", a=2)
                                mm(o3, kt3[:, bi, hl * 128:hl * 128 + kn], aq4[:, 4 * g + hl, :, qoff:qoff + nq],
                                   hl % 2 == 0, not last_g, [kt_.tok, c.qrow.tok], [pst[bank]])
                            if last_g:
                                for bl in range(2):
                                    mm(ps[s0 + bl][:, :], cmbb.ap[:, bi * 128:(bi + 1) * 128], I4, False, True,
                                       [cmbb.tok, ident.tok], [pst[s0 + bl]])
                            for bl in range(2):
                                act(p_.ap[0:kn, bl * 4 * nq:(bl + 1) * 4 * nq], ps[s0 + bl][0:kn, 0:4 * nq], AF.Exp,
                                    [pst[s0 + bl]], [p_.tok], scale=0.125)
                            if pending is not None:
                                pending()

                            def pv(p_=p_, v_=v_, v3=v3, bi=bi, kn=kn, first=(bi_all == 0), last=(bi_all == c.nblk - 1)):
                                for hl in range(4):
                                    for mp in range(2):
                                        ab, ao = acc(hl, mp)
                                        mm(ps[ab][0:nq, ao:ao + 130], p_.ap[0:kn, (hl * 2 + mp) * nq:(hl * 2 + mp + 1) * nq],
                                           v3[0:kn, bi, hl * 130:hl * 130 + 130], first and (hl * 2 + mp) % 3 == 0, last,
                                           [p_.tok, v_.tok], [pst[ab]])
                            pending = pv
                            bi_all += 1
                            yield
                    pending()
                    for hl in range(4):
                        (b1, o1), (b2, o2) = acc(hl, 0), acc(hl, 1)
                        r1, r2, ss, rs = sm["r1"], sm["r2"], sm["ss"], sm["rs"]
                        fw.op("dve", lambda e, o=r1.ap[0:nq, :], i_=ps[b1][0:nq, o1 + 128:o1 + 129]: e.reciprocal(out=o, in_=i_),
                              [pst[b1]], [r1.tok])
                        fw.op("dve", lambda e, o=r2.ap[0:nq, :], i_=ps[b2][0:nq, o2 + 128:o2 + 129]: e.reciprocal(out=o, in_=i_),
                              [pst[b2]], [r2.tok])
                        tt("dve", r2.ap[0:nq, :], r2.ap[0:nq, :], sm["nlam"].ap[0:nq, :], ALU.mult, [r2.tok, sm["nlam"].tok], [r2.tok])
                        ts("dve", of_.ap[0:nq, :], ps[b1][0:nq, o1:o1 + 128], r1.ap[0:nq, :], None, ALU.mult, None,
                           [pst[b1], r1.tok], [of_.tok])
                        stt("dve", of_.ap[0:nq, :], ps[b2][0:nq, o2:o2 + 128], r2.ap[0:nq, :], of_.ap[0:nq, :], ALU.mult, ALU.add,
                            [pst[b2], r2.tok, of_.tok], [of_.tok])
                        act(junk.ap[0:nq, 0:128], of_.ap[0:nq, :], AF.Square, [of_.tok], [junk.tok, ss.tok], accum_out=ss.ap[0:nq, :])
                        rs_v = Buf(rs.ap[0:nq, :]); rs_v.tok = rs.tok
                        ss_v = Buf(ss.ap[0:nq, :]); ss_v.tok = ss.tok
                        act(rs_v.ap, ss_v.ap, AF.Ln, [ss_v.tok, epsb.tok], [rs_v.tok], scale=1.0 / 128, bias=epsb.ap[0:nq, :])
                        act(rs_v.ap, rs_v.ap, AF.Exp, [rs_v.tok], [rs_v.tok], scale=-0.5)
                        stt("dve", ob16.ap[0:nq, :], of_.ap[0:nq, :], rs.ap[0:nq, :], sublnB.ap[0:nq, :], ALU.mult, ALU.mult,
                            [of_.tok, rs.tok, sublnB.tok], [ob16.tok])
                        pb_ = ps[7][:].bitcast(BF16)
                        tr(pb_[:, 0:nq], ob16.ap[0:nq, :], idn[0:nq, 0:nq], [ob16.tok, ident.tok], [pst[7]])
                        cp("act", c.oaT.ap[:, (4 * g + hl) * 128 + qoff:(4 * g + hl) * 128 + qoff + nq], pb_[:, 0:nq],
                           [pst[7]], [c.oaT.tok])
                        yield

            IDXE = os.environ.get("K_IDXE", "dve")

            def gen_idx(c):
                nq, qoff = c.nq, c.qoff
                iq3 = c.qrow.ap[:, 3072:3584].rearrange("p (j c) -> p j c", j=4)
                c0 = 0
                it = 0
                for gi_, (src3, dtok, kns) in enumerate(c.groups):
                    nb = len(kns)
                    ik_ = load("ik", ikb, src3, C_IKT, 128, nb, dtok)
                    ik3 = ik_.ap[:, 0:nb * 128].rearrange("p (b c) -> p b c", b=nb)
                    full = all(kn == 128 for kn in kns)
                    pieces = [(0, nb, nb * 128)] if full else [(bi, 1, kn) for bi, kn in enumerate(kns)]
                    for (b0, nbp, ncols) in pieces:
                        for h in range(8):
                            j, e_ = h // 2, h % 2
                            bank = 5 + (it % 2)
                            r_ = Rb[it % 2]
                            it += 1
                            rhs = ik3[64 * e_:64 * e_ + 64, b0:b0 + nbp, :] if full else ik3[64 * e_:64 * e_ + 64, b0, 0:ncols]
                            out = ps[bank][0:nq, 0:ncols]
                            if full:
                                out = out.rearrange("p (b c) -> p b c", b=nbp)
                            mm(out, iq3[64 * e_:64 * e_ + 64, j, qoff:qoff + nq], rhs, True, True,
                               [ik_.tok, c.qrow.tok], [pst[bank]])
                            act(r_.ap[0:nq, 0:ncols], ps[bank][0:nq, 0:ncols], AF.Relu, [pst[bank]], [r_.tok])
                            if h == 0:
                                ts(IDXE, Sb.ap[0:nq, c0:c0 + ncols], r_.ap[0:nq, 0:ncols], c.iwt.ap[0:nq, 0:1], None,
                                   ALU.mult, None, [r_.tok, c.iwt.tok], [Sb.tok])
                            elif IDXE == "pool":
                                ts("pool", r_.ap[0:nq, 0:ncols], r_.ap[0:nq, 0:ncols], c.iwt.ap[0:nq, h:h + 1], None,
                                   ALU.mult, None, [r_.tok, c.iwt.tok], [r_.tok])
                                tt("pool", Sb.ap[0:nq, c0:c0 + ncols], Sb.ap[0:nq, c0:c0 + ncols], r_.ap[0:nq, 0:ncols],
                                   ALU.add, [r_.tok, Sb.tok], [Sb.tok])
                            else:
                                stt("dve", Sb.ap[0:nq, c0:c0 + ncols], r_.ap[0:nq, 0:ncols], c.iwt.ap[0:nq, h:h + 1],
                                    Sb.ap[0:nq, c0:c0 + ncols], ALU.mult, ALU.add, [r_.tok, c.iwt.tok, Sb.tok], [Sb.tok])
                            yield
                        c0 += ncols
                assert c0 == c.NK

            def gen_topk(c):
                nq, NK = c.nq, c.NK
                lo, hh_, mid, cnt, gg, w0, mx = [sm[k_] for k_ in ["lo", "h", "mid", "cnt", "g", "w0", "mx"]]
                sv = Sb.ap[0:nq, 0:NK]
                fw.op("dve", lambda e: e.tensor_reduce(out=lo.ap[0:nq, :], in_=sv, axis=AX.X, op=ALU.min), [Sb.tok], [lo.tok])
                if c.causal:
                    tt("dve", Sb.ap[0:nq, NK - 512:NK], Sb.ap[0:nq, NK - 512:NK], cmbf.ap[0:nq, :], ALU.add,
                       [Sb.tok, cmbf.tok], [Sb.tok])
                fw.op("dve", lambda e: e.reduce_max(out=mx.ap[0:nq, :], in_=sv, axis=AX.X), [Sb.tok], [mx.tok])
                tt("dve", w0.ap[0:nq, :], mx.ap[0:nq, :], lo.ap[0:nq, :], ALU.subtract, [mx.tok, lo.tok], [w0.tok])
                ts("dve", htab.ap[0:nq, 0:NBIS], pw2.ap[0:nq, 0:NBIS], w0.ap[0:nq, :], None, ALU.mult, None,
                   [pw2.tok, w0.tok], [htab.tok])
                tt("dve", mid.ap[0:nq, :], lo.ap[0:nq, :], htab.ap[0:nq, 0:1], ALU.add, [lo.tok, htab.tok], [mid.tok])
                yield
                for itb in range(NBIS):
                    ts("dve", MB.ap[0:nq, 0:NK], sv, mid.ap[0:nq, :], 0.0, ALU.is_ge, ALU.add, [Sb.tok, mid.tok],
                       [MB.tok, cnt.tok], accum_out=cnt.ap[0:nq, :])
                    ts("dve", gg.ap[0:nq, :], cnt.ap[0:nq, :], 256.0, htab.ap[0:nq, itb:itb + 1], ALU.is_ge, ALU.mult,
                       [cnt.tok, htab.tok], [gg.tok])
                    hn = htab.ap[0:nq, itb + 1:itb + 2] if itb + 1 < NBIS else htab.ap[0:nq, itb:itb + 1]
                    dst_ = mid if itb + 1 < NBIS else lo
                    stt("dve", dst_.ap[0:nq, :], mid.ap[0:nq, :], hn, gg.ap[0:nq, :], ALU.subtract, ALU.add,
                        [mid.tok, htab.tok, gg.tok], [dst_.tok])
                    yield
                ts("dve", MB.ap[0:nq, 0:NK], sv, lo.ap[0:nq, :], NEG, ALU.is_lt, ALU.mult, [Sb.tok, lo.tok], [MB.tok])
                yield

            def gen_dsa(c):
                nq, qoff = c.nq, c.qoff
                bq3 = c.qrow.ap[:, 2048:3072].rearrange("p (h c) -> p h c", h=8)
                I4q = I4[0:nq, :].rearrange("p (a b) -> p a b", a=4)[:, :, 0:nq]
                for g in range(2):
                    def accb(hl):
                        return hl // 3, (hl % 3) * 160
                    bi_all = 0
                    it = 0
                    c0 = 0
                    pending = None
                    for gi_, (src3, dtok, kns) in enumerate(c.groups):
                        nb = len(kns)
                        kt_ = load("kT", kTb, src3, C_BKT + 512 * g, 512, nb, dtok)
                        v_ = load("v", vb, src3, C_BV + 520 * g, 520, nb, dtok)
                        kt3 = kt_.ap[:, 0:nb * 512].rearrange("p (b c) -> p b c", b=nb)
                        v3 = v_.ap[:, 0:nb * 520].rearrange("p (b c) -> p b c", b=nb)
                        for bi, kn in enumerate(kns):
                            sbk = 3 + (it % 2)
                            p_ = pT[it % 2]
                            it += 1
                            for hl in range(4):
                                mm(ps[sbk][0:kn, hl * nq:(hl + 1) * nq], kt3[:, bi, hl * 128:hl * 128 + kn],
                                   bq3[:, 4 * g + hl, qoff:qoff + nq], hl == 0, False, [kt_.tok, c.qrow.tok], [pst[sbk]])
                            mm(ps[sbk][0:kn, 0:4 * nq].rearrange("p (a b) -> p a b", a=4), MB.ap[0:nq, c0:c0 + kn], I4q,
                               False, True, [MB.tok, ident.tok], [pst[sbk]])
                            act(p_.ap[0:kn, 0:4 * nq], ps[sbk][0:kn, 0:4 * nq], AF.Exp, [pst[sbk]], [p_.tok],
                                scale=float(128 ** -0.5))
                            if pending is not None:
                                pending()

                            def pv(p_=p_, v_=v_, v3=v3, bi=bi, kn=kn, first=(bi_all == 0), last=(bi_all == c.nblk - 1)):
                                for hl in range(4):
                                    ab, ao = accb(hl)
                                    mm(ps[ab][0:nq, ao:ao + 130], p_.ap[0:kn, hl * nq:(hl + 1) * nq],
                                       v3[0:kn, bi, hl * 130:hl * 130 + 130], first and hl % 3 == 0, last,
                                       [p_.tok, v_.tok], [pst[ab]])
                            pending = pv
                            bi_all += 1
                            c0 += kn
                            yield
                    pending()
                    for hl in range(4):
                        ab, ao = accb(hl)
                        r1 = sm["r1"]
                        fw.op("dve", lambda e, o=r1.ap[0:nq, :], i_=ps[ab][0:nq, ao + 128:ao + 129]: e.reciprocal(out=o, in_=i_),
                              [pst[ab]], [r1.tok])
                        ts("dve", ob16.ap[0:nq, :], ps[ab][0:nq, ao:ao + 128], r1.ap[0:nq, :], None, ALU.mult, None,
                           [pst[ab], r1.tok], [ob16.tok])
                        pb_ = ps[7][:].bitcast(BF16)
                        tr(pb_[:, 0:nq], ob16.ap[0:nq, :], idn[0:nq, 0:nq], [ob16.tok, ident.tok], [pst[7]])
                        cp("act", c.obT.ap[:, (4 * g + hl) * 128 + qoff:(4 * g + hl) * 128 + qoff + nq], pb_[:, 0:nq],
                           [pst[7]], [c.obT.tok])
                    yield

            def post(t, oaT_, obT_):
                dma("sp", gt.ap, gs[t][:, 0:2048], [dtoks["gs"]], [gt.tok])
                dma("sp", x1b.ap, x1s[t], [dtoks["x1s"]], [x1b.tok])
                oa3 = oaT_.ap.rearrange("p (h c) -> p h c", h=8)
                ob3 = obT_.ap.rearrange("p (h c) -> p h c", h=8)
                for half in range(2):
                    for h in range(8):
                        mm(ps[0][:, :], oa3[:, h, :], wa3[:, h, half * 512:(half + 1) * 512], h == 0, h == 7,
                           [oaT_.tok, wab[0].tok], [pst[0]])
                    for h in range(8):
                        mm(ps[1][:, :], ob3[:, h, :], wb3[:, h, half * 512:(half + 1) * 512], h == 0, h == 7,
                           [obT_.tok, wab[1].tok], [pst[1]])
                    tt("dve", tmpA.ap, ps[0][:, :], gt.ap[:, half * 512:(half + 1) * 512], ALU.mult, [pst[0], gt.tok], [tmpA.tok])
                    tt("dve", tmpB.ap, ps[1][:, :], gt.ap[:, 1024 + half * 512:1024 + (half + 1) * 512], ALU.mult,
                       [pst[1], gt.tok], [tmpB.tok])
                    tt("dve", mg.ap[:, half * 512:(half + 1) * 512], tmpA.ap, tmpB.ap, ALU.add, [tmpA.tok, tmpB.tok], [mg.tok])
                pb_ = ps[2][:].bitcast(BF16)
                for kc in range(8):
                    tr(pb_[:, kc * 128:(kc + 1) * 128], mg.ap[:, kc * 128:(kc + 1) * 128], idn, [mg.tok, ident.tok], [pst[2]])
                cp("act", mgT.ap, pb_, [pst[2]], [mgT.tok])
                mg3 = mgT.ap.rearrange("p (k c) -> p k c", k=8)
                for half in range(2):
                    for kc in range(8):
                        mm(ps[0][:, :], mg3[:, kc, :], wo3[:, kc, half * 512:(half + 1) * 512], kc == 0, kc == 7,
                           [mgT.tok, wab[2].tok], [pst[0]])
                    tt("dve", x2b.ap[:, half * 512:(half + 1) * 512], ps[0][:, :], x1b.ap[:, half * 512:(half + 1) * 512],
                       ALU.add, [pst[0], x1b.tok], [x2b.tok])
                dma("sp", x2s[t], x2b.ap, [x2b.tok], [dtoks["x2s"]])
                if debug and stop_after == 3:
                    dma("sp", dbg[t], x2b.ap, [x2b.tok], [dtoks["dbg"]])
                    dma("pool", dbgq[t][:, 0:1024], oaT_.ap, [oaT_.tok], [dtoks["dbg"]])
                    dma("pool", dbgq[t][:, 1024:2048], obT_.ap, [obT_.tok], [dtoks["dbg"]])
                rms_to_xT(x2b, xst.ap.rearrange("p (k c) -> p k c", k=8), xst.tok, gB, 1)
                dma("sp", xn3s[t], xst.ap, [xst.tok], [dtoks["xn3s"]])

            def run(g):
                for _ in g:
                    pass

            def interleave(ga, gb, na, nb_):
                da = db = True
                ia = ib = 0
                while da or db:
                    if da and (not db or ia * max(nb_, 1) <= ib * max(na, 1)):
                        try:
                            next(ga); ia += 1
                        except StopIteration:
                            da = False
                    elif db:
                        try:
                            next(gb); ib += 1
                        except StopIteration:
                            db = False

            mkv3 = mkv.rearrange("(b p) c -> p b c", b=1)
            ctxs = []
            k = 0
            for s in range(2):
                sk3 = skv[s].rearrange("b p c -> p b c")
                groups = [(asrc(sk3[:, 4 * q_:4 * q_ + 4, :]), dtoks["skv"], [128] * 4) for q_ in range(4)]
                groups.append((asrc(sk3[:, 16:18, :]), dtoks["skv"], [16, 32]))
                ctxs.append(make_ctx(k, 16, 32, 32 + 32 * s, groups, False, 0)); k += 1
            order = [0, 1, 2, 3] + list(range(15, 3, -1))
            for oi, t in enumerate(order):
                groups = [(asrc(mkv3), dtoks["mkv"], [16])]
                for mm_ in range(t + 1):
                    groups.append((gsrc(mm_), gtoks[mm_], [128] * 4))
                ctxs.append(make_ctx(k, t, 128, 0, groups, True, (oi + 1) % 2)); k += 1
            fw.op("dve", lambda e: e.memset(oaTs[0].ap, 0.0), [], [oaTs[0].tok])
            fw.op("dve", lambda e: e.memset(obTs[0].ap, 0.0), [], [obTs[0].tok])
            def gen_wab():
                mbf = MB.ap.bitcast(F32)
                slots = [(mbf[:, 0:4 * D], [MB.tok]), (slotB, [x2b.tok, gt.tok, x1b.tok])]
                n_ = 0
                for wb_, src in zip(wab, (wa, wb, wo)):
                    src3_ = src.rearrange("(k p) n -> p k n", p=128)
                    for hf_ in range(2):
                        ap_, tk_ = slots[n_ % 2]
                        n_ += 1
                        dma("sp", ap_.rearrange("p (k n) -> p k n", k=4), src3_[:, hf_ * 4:(hf_ + 1) * 4, :], [], tk_)
                        for q_ in range(2):
                            cp("dve", wb_.ap[:, hf_ * 4096 + q_ * 2048:hf_ * 4096 + (q_ + 1) * 2048],
                               ap_[:, q_ * 2048:(q_ + 1) * 2048], tk_, [wb_.tok])
                        yield

            run(gen_loadq(ctxs[0]))
            interleave(gen_diff(ctxs[0]), gen_wab(), 2 * ctxs[0].nblk + 8, 6)
            run(gen_idx(ctxs[0]))
            for k, c in enumerate(ctxs):
                nxt = ctxs[k + 1] if k + 1 < len(ctxs) else None
                if nxt is not None:
                    run(gen_loadq(nxt))
                    interleave(gen_topk(c), gen_diff(nxt), NBIS + 2, 2 * nxt.nblk + 8)
                else:
                    run(gen_topk(c))
                if nxt is not None:
                    interleave(gen_dsa(c), gen_idx(nxt), 2 * c.nblk + 2, 8 * (len(nxt.groups) + 1))
                else:
                    run(gen_dsa(c))
                if k >= 1:
                    post(c.t, c.oaT, c.obT)
            fw.barrier()

        if stop_after >= 3:
            phase_attn()
        if stop_after >= 4:
            for t in range(NT):
                dma("sp", xnT3[:, :, t * 128:(t + 1) * 128], xn3s[t].rearrange("p (k c) -> p k c", k=8),
                    [dtoks["xn3s"]], [xnT_tok[t]])
            ffn(w2g, w2u, w2d, x2s, dtoks["x2s"], 3, True)

        fw.barrier()
        fw.run()
    return nc


def _host_prep(inputs):
    xp = np.asarray(inputs["x_prompt"], np.float32)
    xs = np.asarray(inputs["x_sample"], np.float32)
    meta = np.asarray(inputs["meta"], np.float32)
    half_a = 32
    half_b = 64
    inv_a = (np.float32(10000.0) ** (-np.arange(half_a, dtype=np.float32) / np.float32(half_a))).astype(np.float32)
    inv_b = (np.float32(10000.0) ** (-np.arange(half_b, dtype=np.float32) / np.float32(half_b))).astype(np.float32)
    ident = np.tile(np.eye(128, dtype=np.float32), (1, 4))
    gvec = np.stack([np.asarray(inputs[k], np.float32).reshape(-1) for k in ["g_ffn1", "g_mix", "g_ffn2", "g_final"]])
    lamv = np.stack([np.asarray(inputs[k], np.float32).reshape(-1) for k in ["lam_q1", "lam_k1", "lam_q2", "lam_k2"]])
    common = {
        "gvec": gvec, "subln": np.asarray(inputs["a_subln"], np.float32).reshape(-1), "lamv": lamv,
        "w1g": np.asarray(inputs["w1_gate"], np.float32)[0], "w1u": np.asarray(inputs["w1_up"], np.float32)[0],
        "w1d": np.asarray(inputs["w1_down"], np.float32)[0],
        "w2g": np.asarray(inputs["w2_gate"], np.float32)[0], "w2u": np.asarray(inputs["w2_up"], np.float32)[0],
        "w2d": np.asarray(inputs["w2_down"], np.float32)[0],
        "win": np.asarray(inputs["w_in"], np.float32)[0],
        "wa": np.asarray(inputs["w_a"], np.float32)[0], "wb": np.asarray(inputs["w_b"], np.float32)[0],
        "wo": np.asarray(inputs["w_o"], np.float32)[0],
        "ident": ident,
    }
    maps = []
    for c in range(8):
        b, i = c // 4, c % 4
        xin = np.zeros((NT, 128, D), np.float32)
        pos = np.zeros((NT, 128), np.float32)
        for m in range(16):
            j = 4 * m + i
            xin[m] = xp[b, 128 * j:128 * (j + 1)]
            pos[m] = 16 + 128 * j + np.arange(128)
        xin[16, 0:16] = meta
        pos[16, 0:16] = np.arange(16)
        for s in range(2):
            xin[16, 32 + 32 * s:64 + 32 * s] = xs[2 * c + s]
            pos[16, 32 + 32 * s:64 + 32 * s] = 16 + 2048 + np.arange(32)
        ang_a = pos[:, :, None].astype(np.float32) * inv_a[None, None, :]
        ang_b = pos[:, :, None].astype(np.float32) * inv_b[None, None, :]
        rope = np.concatenate([np.cos(ang_a), np.sin(ang_a), np.cos(ang_b), np.sin(ang_b)], -1).astype(np.float32)
        cm = np.zeros((4, 128, 128), np.float32)
        for r in range(4):
            if r > i:
                cm[r] = NEG
            elif r == i:
                cm[r, 0:64, 64:128] = NEG
        d = dict(common)
        d.update({
            "xin": xin, "rope": rope, "cmb": cm,
            "cak": np.asarray(inputs["cache_a_k"], np.float32)[0, 2 * c:2 * c + 2].reshape(2, 2064, D),
            "cav": np.asarray(inputs["cache_a_v"], np.float32)[0, 2 * c:2 * c + 2].reshape(2, 2064, D),
            "cbk": np.asarray(inputs["cache_b_k"], np.float32)[0, 2 * c:2 * c + 2].reshape(2, 2064, D),
            "cbv": np.asarray(inputs["cache_b_v"], np.float32)[0, 2 * c:2 * c + 2].reshape(2, 2064, D),
            "cki": np.asarray(inputs["cache_b_kidx"], np.float32)[0, 2 * c:2 * c + 2],
        })
        maps.append(d)
    return maps


def _assemble(results):
    y_p = np.zeros((2, 8192, D), np.float32)
    y_s = np.zeros((16, 32, D), np.float32)
    kp = [np.zeros((1, 2, 8208, 1024), np.float32) for _ in range(4)] + [np.zeros((1, 2, 8208, 64), np.float32)]
    ks = [np.zeros((1, 16, 32, 1024), np.float32) for _ in range(4)] + [np.zeros((1, 16, 32, 64), np.float32)]
    offs = [(0, 1024), (1024, 2048), (2048, 3072), (3072, 4096), (4096, 4160)]
    for c in range(8):
        b, i = c // 4, c % 4
        r = results[c]
        yy, kk = r["y"], r["kout"]
        for m in range(16):
            j = 4 * m + i
            y_p[b, 128 * j:128 * (j + 1)] = yy[m]
            for q, (a0, a1) in enumerate(offs):
                kp[q][0, b, 16 + 128 * j:16 + 128 * (j + 1)] = kk[m][:, a0:a1]
        if i == 0:
            for q, (a0, a1) in enumerate(offs):
                kp[q][0, b, 0:16] = kk[16][0:16, a0:a1]
        for s in range(2):
            y_s[2 * c + s] = yy[16][32 + 32 * s:64 + 32 * s]
            for q, (a0, a1) in enumerate(offs):
                ks[q][0, 2 * c + s] = kk[16][32 + 32 * s:64 + 32 * s, a0:a1]
    out = [y_p, y_s]
    out += [kp[0].reshape(1, 2, 8208, 8, 128), kp[1].reshape(1, 2, 8208, 8, 128),
            kp[2].reshape(1, 2, 8208, 8, 128), kp[3].reshape(1, 2, 8208, 8, 128), kp[4]]
    out += [ks[0].reshape(1, 16, 32, 8, 128), ks[1].reshape(1, 16, 32, 8, 128),
            ks[2].reshape(1, 16, 32, 8, 128), ks[3].reshape(1, 16, 32, 8, 128), ks[4]]
    return tuple(out)


_NC_CACHE = {}


def kernel(**inputs):
    maps = _host_prep(inputs)
    if "nc" not in _NC_CACHE:
        _NC_CACHE["nc"] = build_program()
    res = run_bass_kernel_spmd(_NC_CACHE["nc"], maps, core_ids=list(range(8)))
    return _assemble(res.results)
```
